# Optimizing a Trainium2 kernel written in Bass

```python
import math
import jax, jax.numpy as jnp
from jax import lax
import numpy as np

D_MODEL = 1024
BATCH = 4
SEQ = 4096
DEPTH = 1
DEC_BATCH = 32
DEC_SEQ = 1
PAST_LEN = 16384
PAGE_SIZE = 128

HEAD_DIM = 64
NSA_WIDTH = D_MODEL // 2
NSA_HEADS = NSA_WIDTH // HEAD_DIM
NSA_GROUPS = 2
NSA_REP = NSA_HEADS // NSA_GROUPS
CMP_BLOCK = 32
CMP_STRIDE = 16
CMP_HIDDEN = 128
SEL_BLOCK = 64
SEL_TOPN = 16
WINDOW = 512
FORCE_BONUS = 1.0e4
DIFF_WIDTH = D_MODEL - NSA_WIDTH
DIFF_VDIM = 2 * HEAD_DIM
DIFF_HEADS = DIFF_WIDTH // DIFF_VDIM
MIX_WIDTH = NSA_WIDTH + DIFF_WIDTH
D_FF = ((8 * D_MODEL // 3 + 255) // 256) * 256
CONV_W = 3
ROPE_THETA = 10000.0
Q_BLOCK = 128
RMS_EPS = 1e-6
ATTN_SCALE = 1.0 / math.sqrt(HEAD_DIM)

KV_W = NSA_GROUPS * HEAD_DIM
SPLITS = (NSA_WIDTH, KV_W, KV_W, KV_W, KV_W, KV_W, KV_W, NSA_HEADS * 3,
          DIFF_HEADS * 2 * HEAD_DIM, DIFF_HEADS * 2 * HEAD_DIM, DIFF_WIDTH)
IN_WIDTH = NSA_WIDTH + 6 * KV_W + NSA_HEADS * 3 + 2 * DIFF_HEADS * 2 * HEAD_DIM + DIFF_WIDTH

kernel_name = "nsa_diffattn_parallel_convffn_step"


def _rmsnorm(x, g):
    xf = x.astype(jnp.float32)
    y = xf * lax.rsqrt(jnp.mean(xf * xf, axis=-1, keepdims=True) + RMS_EPS)
    return (y * g.astype(jnp.float32)).astype(x.dtype)


def _rope(x, pos):
    half = HEAD_DIM // 2
    inv = 1.0 / (ROPE_THETA ** (jnp.arange(half, dtype=jnp.float32) / half))
    ang = pos.astype(jnp.float32)[:, None] * inv[None, :]
    shape = (1, pos.shape[0]) + (1,) * (x.ndim - 3) + (half,)
    cos = jnp.cos(ang).reshape(shape)
    sin = jnp.sin(ang).reshape(shape)
    xf = x.astype(jnp.float32)
    x1, x2 = xf[..., :half], xf[..., half:]
    return jnp.concatenate([x1 * cos - x2 * sin, x2 * cos + x1 * sin], axis=-1).astype(x.dtype)


def _masked_softmax(s, mask):
    s = jnp.where(mask, s, -jnp.inf)
    m = jnp.max(s, axis=-1, keepdims=True)
    m = jnp.where(jnp.isfinite(m), m, 0.0)
    p = jnp.exp(s - m)
    return p / jnp.maximum(jnp.sum(p, axis=-1, keepdims=True), 1e-30)


def _project(x, pos, norm_g, w_in):
    B, T = x.shape[:2]
    h = _rmsnorm(x, norm_g)
    z = h @ w_in
    cuts = [int(c) for c in np.cumsum(SPLITS)[:-1]]
    q, ck, cv, sk, sv, wk, wv, gate, dq, dk, dv = jnp.split(z, cuts, axis=-1)
    kv = lambda a: a.reshape(B, T, NSA_GROUPS, HEAD_DIM)
    q = q.reshape(B, T, NSA_GROUPS, NSA_REP, HEAD_DIM)
    q_rot = _rope(q, pos)
    gate = jax.nn.sigmoid(gate.astype(jnp.float32)).reshape(B, T, NSA_GROUPS, NSA_REP, 3)
    dq = _rope(dq.reshape(B, T, DIFF_HEADS, 2, HEAD_DIM), pos)
    dk = _rope(dk.reshape(B, T, DIFF_HEADS, 2, HEAD_DIM), pos)
    dv = dv.reshape(B, T, DIFF_HEADS, DIFF_VDIM)
    return (q, q_rot, kv(ck), kv(cv), _rope(kv(sk), pos), kv(sv), _rope(kv(wk), pos), kv(wv),
            gate, dq, dk, dv)


def _compress(x_raw, pos_emb, w1, w2):
    B, T, G, d = x_raw.shape
    r = CMP_BLOCK // CMP_STRIDE
    n = T // CMP_STRIDE - r + 1
    c = x_raw.reshape(B, T // CMP_STRIDE, CMP_STRIDE, G, d)
    pos_r = pos_emb.reshape(r, CMP_STRIDE, d)
    w1_r = w1.reshape(r, CMP_STRIDE, d, CMP_HIDDEN)
    hid = 0.0
    for i in range(r):
        hid = hid + jnp.einsum('bnsgd,sdh->bngh', c[:, i:i + n] + pos_r[i][None, None, :, None, :], w1_r[i])
    return jax.nn.gelu(hid) @ w2


def _nsa_branches(q, q_rot, gate, q_pos, kc, vc, cmp_end, n_sel, sel_gather, kw, vw, w_pos):
    B, Q = q.shape[:2]
    s = jnp.einsum('bqgrd,bkgd->bgrqk', q, kc, preferred_element_type=jnp.float32) * ATTN_SCALE
    p = _masked_softmax(s, cmp_end[None, :] <= q_pos[:, None])
    o_c = jnp.einsum('bgrqk,bkgd->bqgrd', p, vc)
    imp = jnp.sum(p, axis=2)
    n_cmp = kc.shape[1]
    cs = jnp.arange(n_cmp, dtype=jnp.int32) * CMP_STRIDE
    ss = jnp.arange(n_sel, dtype=jnp.int32) * SEL_BLOCK
    ov = ((cs[:, None] < ss[None, :] + SEL_BLOCK) & (cs[:, None] + CMP_BLOCK > ss[None, :])).astype(jnp.float32)
    score = jnp.einsum('bgqi,ij->bgqj', imp, ov)
    blk = jnp.arange(n_sel, dtype=jnp.int32)[None, :]
    cur = (q_pos // SEL_BLOCK)[:, None]
    forced = (blk == 0) | (blk == cur) | (blk == cur - 1)
    ok = blk * SEL_BLOCK <= q_pos[:, None]
    score = jnp.where(ok, score + jnp.where(forced, FORCE_BONUS, 0.0), -jnp.inf)
    top_vals, top_idx = lax.top_k(score, min(SEL_TOPN, n_sel))
    tok = top_idx[..., None] * SEL_BLOCK + jnp.arange(SEL_BLOCK, dtype=jnp.int32)
    kmask = (jnp.isfinite(top_vals)[..., None] & (tok <= q_pos[None, None, :, None, None]))
    kmask = kmask.reshape(B, NSA_GROUPS, Q, -1)
    tok = tok.reshape(B, NSA_GROUPS, Q, -1)
    ks, vs = sel_gather(tok)
    s = jnp.einsum('bqgrd,bgqnd->bgrqn', q_rot, ks, preferred_element_type=jnp.float32) * ATTN_SCALE
    p = _masked_softmax(s, kmask[:, :, None])
    o_s = jnp.einsum('bgrqn,bgqnd->bqgrd', p, vs)
    s = jnp.einsum('bqgrd,bkgd->bgrqk', q_rot, kw, preferred_element_type=jnp.float32) * ATTN_SCALE
    wm = ((w_pos[None, :] <= q_pos[:, None]) & (w_pos[None, :] > q_pos[:, None] - WINDOW)
          & (w_pos[None, :] >= 0))
    p = _masked_softmax(s, wm)
    o_w = jnp.einsum('bgrqk,bkgd->bqgrd', p, vw)
    o = gate[..., 0:1] * o_c + gate[..., 1:2] * o_s + gate[..., 2:3] * o_w
    return o.reshape(B, Q, NSA_WIDTH)


def _nsa_prompt(q, q_rot, gate, ck, cv, sk, sv, wk, wv, pk, w1k, w2k, pv, w1v, w2v):
    B, T = q.shape[:2]
    kc = _compress(ck, pk, w1k, w2k)
    vc = _compress(cv, pv, w1v, w2v)
    cmp_end = jnp.arange(kc.shape[1], dtype=jnp.int32) * CMP_STRIDE + CMP_BLOCK - 1
    n_sel = T // SEL_BLOCK
    pad = ((0, 0), (WINDOW, 0), (0, 0), (0, 0))
    wk_pad, wv_pad = jnp.pad(wk, pad), jnp.pad(wv, pad)
    b_idx = jnp.arange(B)[:, None, None, None]
    g_idx = jnp.arange(NSA_GROUPS)[None, :, None, None]

    def sel_gather(tok):
        return sk[b_idx, tok, g_idx], sv[b_idx, tok, g_idx]

    def block(c):
        start = c * Q_BLOCK
        sl = lambda a: lax.dynamic_slice_in_dim(a, start, Q_BLOCK, axis=1)
        q_pos = start + jnp.arange(Q_BLOCK, dtype=jnp.int32)
        kw = lax.dynamic_slice_in_dim(wk_pad, start, Q_BLOCK + WINDOW, axis=1)
        vw = lax.dynamic_slice_in_dim(wv_pad, start, Q_BLOCK + WINDOW, axis=1)
        w_pos = start - WINDOW + jnp.arange(Q_BLOCK + WINDOW, dtype=jnp.int32)
        return _nsa_branches(sl(q), sl(q_rot), sl(gate), q_pos, kc, vc, cmp_end, n_sel,
                             sel_gather, kw, vw, w_pos)

    out = lax.map(block, jnp.arange(T // Q_BLOCK, dtype=jnp.int32))
    return out.transpose(1, 0, 2, 3).reshape(B, T, NSA_WIDTH)


def _nsa_sample(q, q_rot, gate, ck, cv, sk, sv, wk, wv, page_table, pool_ck, pool_cv,
                pool_sk, pool_sv, win_k, win_v, pk, w1k, w2k, pv, w1v, w2v):
    B, Q = q.shape[:2]
    P = PAST_LEN
    q_pos = P + jnp.arange(Q, dtype=jnp.int32)
    t_full = P + Q
    t_pad = -(-t_full // SEL_BLOCK) * SEL_BLOCK
    padw = ((0, 0), (0, t_pad - t_full), (0, 0), (0, 0))
    past_ck = pool_ck[page_table].reshape(B, P, NSA_GROUPS, HEAD_DIM)
    past_cv = pool_cv[page_table].reshape(B, P, NSA_GROUPS, HEAD_DIM)
    kc = _compress(jnp.pad(jnp.concatenate([past_ck, ck], axis=1), padw), pk, w1k, w2k)
    vc = _compress(jnp.pad(jnp.concatenate([past_cv, cv], axis=1), padw), pv, w1v, w2v)
    cmp_end = jnp.arange(kc.shape[1], dtype=jnp.int32) * CMP_STRIDE + CMP_BLOCK - 1
    n_sel = t_pad // SEL_BLOCK
    b_idx = jnp.arange(B)[:, None, None, None]
    g_idx = jnp.arange(NSA_GROUPS)[None, :, None, None]

    def sel_gather(tok):
        in_past = (tok < P)[..., None]
        tp = jnp.minimum(tok, P - 1)
        phys = page_table[b_idx, tp // PAGE_SIZE]
        off = tp % PAGE_SIZE
        tn = jnp.clip(tok - P, 0, Q - 1)
        kg = jnp.where(in_past, pool_sk[phys, off, g_idx], sk[b_idx, tn, g_idx])
        vg = jnp.where(in_past, pool_sv[phys, off, g_idx], sv[b_idx, tn, g_idx])
        return kg, vg

    wbuf = win_k.shape[1]
    kw = jnp.concatenate([win_k, wk], axis=1)
    vw = jnp.concatenate([win_v, wv], axis=1)
    w_pos = P - wbuf + jnp.arange(wbuf + Q, dtype=jnp.int32)
    o = _nsa_branches(q, q_rot, gate, q_pos, kc, vc, cmp_end, n_sel, sel_gather, kw, vw, w_pos)
    return o, kw[:, -wbuf:], vw[:, -wbuf:]


def _diff_prompt(dq, dk, dv):
    B, T = dq.shape[:2]
    k_pos = jnp.arange(T, dtype=jnp.int32)

    def block(c):
        start = c * Q_BLOCK
        qb = lax.dynamic_slice_in_dim(dq, start, Q_BLOCK, axis=1)
        q_pos = start + jnp.arange(Q_BLOCK, dtype=jnp.int32)
        s = jnp.einsum('bqhmd,bkhmd->bhmqk', qb, dk, preferred_element_type=jnp.float32) * ATTN_SCALE
        s = jnp.where(k_pos[None, :] <= q_pos[:, None], s, -jnp.inf)
        p = jax.nn.softmax(s, axis=-1)
        return jnp.einsum('bhmqk,bkhe->bqhme', p, dv)

    o = lax.map(block, jnp.arange(T // Q_BLOCK, dtype=jnp.int32))
    return o.transpose(1, 0, 2, 3, 4, 5).reshape(B, T, DIFF_HEADS, 2, DIFF_VDIM)


def _attn_partial(q, k, v, mask):
    s = jnp.einsum('bqhmd,bkhmd->bhmqk', q, k, preferred_element_type=jnp.float32) * ATTN_SCALE
    s = jnp.where(mask, s, -jnp.inf)
    m = jnp.max(s, axis=-1)
    p = jnp.exp(s - jnp.where(jnp.isfinite(m), m, 0.0)[..., None])
    return m, jnp.sum(p, axis=-1), jnp.einsum('bhmqk,bkhe->bhmqe', p, v)


def _diff_sample(dq, dk, dv, page_table, pool_k, pool_v):
    B, Q = dq.shape[:2]
    P = PAST_LEN
    q_pos = P + jnp.arange(Q, dtype=jnp.int32)
    full = jnp.ones((Q, PAGE_SIZE), dtype=bool)

    def page(n):
        phys = page_table[:, n]
        return _attn_partial(dq, pool_k[phys], pool_v[phys], full)

    m_p, l_p, a_p = lax.map(page, jnp.arange(P // PAGE_SIZE, dtype=jnp.int32))
    m_n, l_n, a_n = _attn_partial(dq, dk, dv, q_pos[None, :] <= q_pos[:, None])
    m_all = jnp.concatenate([m_p, m_n[None]], axis=0)
    l_all = jnp.concatenate([l_p, l_n[None]], axis=0)
    a_all = jnp.concatenate([a_p, a_n[None]], axis=0)
    w = jnp.exp(m_all - jnp.max(m_all, axis=0, keepdims=True))
    o = jnp.sum(w[..., None] * a_all, axis=0) / jnp.sum(w * l_all, axis=0)[..., None]
    return o.transpose(0, 3, 1, 2, 4)


def _diff_merge(o, lam, lam_init, subln_g):
    B, T = o.shape[:2]
    od = o[..., 0, :] - lam * o[..., 1, :]
    od = _rmsnorm(od.astype(jnp.float32), subln_g) * (1.0 - lam_init)
    return od.reshape(B, T, DIFF_WIDTH)


def _conv_ffn(x, prev, norm_g, w_up, conv_w, conv_b, w_down):
    T = x.shape[1]
    u = _rmsnorm(x, norm_g) @ w_up
    ext = jnp.concatenate([prev, u], axis=1)
    c = conv_b
    for i in range(CONV_W):
        c = c + ext[:, i:i + T] * conv_w[i]
    a, b = jnp.split(c, 2, axis=-1)
    return (jax.nn.silu(a) * b) @ w_down, ext[:, -(CONV_W - 1):]


def setup_inputs(seed: int = 0) -> dict:
    key = jax.random.key(seed)
    ks = iter(jax.random.split(key, 40))
    f32 = jnp.float32
    nrm = lambda shape, scale: jax.random.normal(next(ks), shape, f32) * scale
    n_pages = PAST_LEN // PAGE_SIZE
    n_pool = (5 * DEC_BATCH * n_pages) // 4
    wbuf = min(WINDOW, PAST_LEN)
    page_table = jax.random.permutation(next(ks), n_pool)[:DEC_BATCH * n_pages]
    page_table = page_table.reshape(DEC_BATCH, n_pages).astype(jnp.int32)
    kvp = (DEPTH, n_pool, PAGE_SIZE, NSA_GROUPS, HEAD_DIM)
    return {
        "x_prompt": nrm((BATCH, SEQ, D_MODEL), 1.0),
        "x_sample": nrm((DEC_BATCH, DEC_SEQ, D_MODEL), 1.0),
        "cache_cmp_k": nrm(kvp, 1.0),
        "cache_cmp_v": nrm(kvp, 1.0),
        "cache_sel_k": nrm(kvp, 1.0),
        "cache_sel_v": nrm(kvp, 1.0),
        "cache_diff_k": nrm((DEPTH, n_pool, PAGE_SIZE, DIFF_HEADS, 2, HEAD_DIM), 1.0),
        "cache_diff_v": nrm((DEPTH, n_pool, PAGE_SIZE, DIFF_HEADS, DIFF_VDIM), 1.0),
        "cache_win_k": nrm((DEPTH, DEC_BATCH, wbuf, NSA_GROUPS, HEAD_DIM), 1.0),
        "cache_win_v": nrm((DEPTH, DEC_BATCH, wbuf, NSA_GROUPS, HEAD_DIM), 1.0),
        "state_ffn_conv": nrm((DEPTH, DEC_BATCH, CONV_W - 1, 2 * D_FF), 1.0),
        "page_table": page_table,
        "attn_norm": 1.0 + nrm((DEPTH, D_MODEL), 0.01),
        "w_in": nrm((DEPTH, D_MODEL, IN_WIDTH), D_MODEL ** -0.5),
        "cmp_pos_k": nrm((DEPTH, CMP_BLOCK, HEAD_DIM), 0.1),
        "cmp_w1_k": nrm((DEPTH, CMP_BLOCK * HEAD_DIM, CMP_HIDDEN), (CMP_BLOCK * HEAD_DIM) ** -0.5),
        "cmp_w2_k": nrm((DEPTH, CMP_HIDDEN, HEAD_DIM), CMP_HIDDEN ** -0.5),
        "cmp_pos_v": nrm((DEPTH, CMP_BLOCK, HEAD_DIM), 0.1),
        "cmp_w1_v": nrm((DEPTH, CMP_BLOCK * HEAD_DIM, CMP_HIDDEN), (CMP_BLOCK * HEAD_DIM) ** -0.5),
        "cmp_w2_v": nrm((DEPTH, CMP_HIDDEN, HEAD_DIM), CMP_HIDDEN ** -0.5),
        "lambda_q1": nrm((DEPTH, HEAD_DIM), 0.1),
        "lambda_k1": nrm((DEPTH, HEAD_DIM), 0.1),
        "lambda_q2": nrm((DEPTH, HEAD_DIM), 0.1),
        "lambda_k2": nrm((DEPTH, HEAD_DIM), 0.1),
        "subln_g": 1.0 + nrm((DEPTH, DIFF_VDIM), 0.01),
        "w_out": nrm((DEPTH, MIX_WIDTH, D_MODEL), MIX_WIDTH ** -0.5),
        "ffn_norm": 1.0 + nrm((DEPTH, D_MODEL), 0.01),
        "w_up": nrm((DEPTH, D_MODEL, 2 * D_FF), D_MODEL ** -0.5),
        "conv_w": nrm((DEPTH, CONV_W, 2 * D_FF), CONV_W ** -0.5),
        "conv_b": nrm((DEPTH, 2 * D_FF), 0.01),
        "w_down": nrm((DEPTH, D_FF, D_MODEL), D_FF ** -0.5),
        "final_norm": 1.0 + nrm((D_MODEL,), 0.01),
    }


def reference(x_prompt, x_sample, cache_cmp_k, cache_cmp_v, cache_sel_k, cache_sel_v,
              cache_diff_k, cache_diff_v, cache_win_k, cache_win_v, state_ffn_conv, page_table,
              attn_norm, w_in, cmp_pos_k, cmp_w1_k, cmp_w2_k, cmp_pos_v, cmp_w1_v, cmp_w2_v,
              lambda_q1, lambda_k1, lambda_q2, lambda_k2, subln_g, w_out, ffn_norm, w_up,
              conv_w, conv_b, w_down, final_norm):
    xp, xs = x_prompt, x_sample
    Bp, Tp = xp.shape[:2]
    pos_p = jnp.arange(Tp, dtype=jnp.int32)
    pos_s = PAST_LEN + jnp.arange(xs.shape[1], dtype=jnp.int32)
    new_p = [[] for _ in range(9)]
    new_s = [[] for _ in range(9)]
    for l in range(DEPTH):
        lam_init = 0.8 - 0.6 * math.exp(-0.3 * l)
        f32 = jnp.float32
        lam = (jnp.exp(jnp.sum(lambda_q1[l].astype(f32) * lambda_k1[l].astype(f32)))
               - jnp.exp(jnp.sum(lambda_q2[l].astype(f32) * lambda_k2[l].astype(f32))) + lam_init)
        cw = (cmp_pos_k[l], cmp_w1_k[l], cmp_w2_k[l], cmp_pos_v[l], cmp_w1_v[l], cmp_w2_v[l])

        q, q_rot, ck, cv, sk, sv, wk, wv, gate, dq, dk, dv = _project(xp, pos_p, attn_norm[l], w_in[l])
        o_nsa = _nsa_prompt(q, q_rot, gate, ck, cv, sk, sv, wk, wv, *cw)
        o_diff = _diff_merge(_diff_prompt(dq, dk, dv), lam, lam_init, subln_g[l])
        xp = xp + jnp.concatenate([o_nsa, o_diff], axis=-1).astype(xp.dtype) @ w_out[l]
        f, conv_p = _conv_ffn(xp, jnp.zeros((Bp, CONV_W - 1, 2 * D_FF), xp.dtype), ffn_norm[l],
                              w_up[l], conv_w[l], conv_b[l], w_down[l])
        xp = xp + f
        wb = min(WINDOW, Tp)
        for lst, val in zip(new_p, (ck, cv, sk, sv, dk, dv, wk[:, -wb:], wv[:, -wb:], conv_p)):
            lst.append(val)

        q, q_rot, ck, cv, sk, sv, wk, wv, gate, dq, dk, dv = _project(xs, pos_s, attn_norm[l], w_in[l])
        o_nsa, nwk, nwv = _nsa_sample(q, q_rot, gate, ck, cv, sk, sv, wk, wv, page_table,
                                      cache_cmp_k[l], cache_cmp_v[l], cache_sel_k[l], cache_sel_v[l],
                                      cache_win_k[l], cache_win_v[l], *cw)
        o_diff = _diff_merge(_diff_sample(dq, dk, dv, page_table, cache_diff_k[l], cache_diff_v[l]),
                             lam, lam_init, subln_g[l])
        xs = xs + jnp.concatenate([o_nsa, o_diff], axis=-1).astype(xs.dtype) @ w_out[l]
        f, conv_s = _conv_ffn(xs, state_ffn_conv[l], ffn_norm[l], w_up[l], conv_w[l], conv_b[l], w_down[l])
        xs = xs + f
        for lst, val in zip(new_s, (ck, cv, sk, sv, dk, dv, nwk, nwv, conv_s)):
            lst.append(val)

    y_prompt = _rmsnorm(xp, final_norm)
    y_sample = _rmsnorm(xs, final_norm)
    p_cmp_k, p_cmp_v, p_sel_k, p_sel_v, p_diff_k, p_diff_v, p_win_k, p_win_v, p_conv = [jnp.stack(v) for v in new_p]
    s_cmp_k, s_cmp_v, s_sel_k, s_sel_v, s_diff_k, s_diff_v, s_win_k, s_win_v, s_conv = [jnp.stack(v) for v in new_s]
    return (y_prompt, y_sample, p_cmp_k, p_cmp_v, p_sel_k, p_sel_v, p_diff_k, p_diff_v, p_win_k, p_win_v,
            p_conv, s_cmp_k, s_cmp_v, s_sel_k, s_sel_v, s_diff_k, s_diff_v, s_win_k, s_win_v, s_conv)
```

```python
import contextlib
import math
import os
import numpy as np
import concourse.bass as bass
import concourse.mybir as mybir
from concourse.bass_utils import run_bass_kernel_spmd

F32 = mybir.dt.float32
BF16 = mybir.dt.bfloat16
I32 = mybir.dt.int32
AF = mybir.ActivationFunctionType
ALU = mybir.AluOpType
AX = mybir.AxisListType

D = 1024
T = 4096
NT = 32
QB = [3, 4, 5, 6, 7, 11, 12, 13, 14, 15, 19, 20, 21, 22, 23, 27, 28, 29, 30, 31]
NQB = len(QB)
DFF = 2816
SCALE = 0.125
EPS = 1e-6
LAM_INIT = 0.2
PAST = 16384
NPAGE = 128
KVW = 1792
QW = 1048
NCMP = 255


class Sched:
    def __init__(self, nc, n_dma_sems=8):
        self.nc = nc
        self.ops = []
        self.lw = {}
        self.rd = {}
        self.nds = n_dma_sems
        self.rr = {}
        self.last_on_sem = {}
        self.cnt_on_sem = {}
        self.dma_sem_of = {}
        self.last_eng = {}
        self.barrier_idx = None

    def add(self, eng, fn, reads=(), writes=(), dma=False):
        idx = len(self.ops)
        deps = set()
        if self.barrier_idx is not None:
            deps.add(self.barrier_idx)
        for r in reads:
            w = self.lw.get(r)
            if w is not None:
                deps.add(w)
        for r in writes:
            w = self.lw.get(r)
            if w is not None:
                deps.add(w)
            for x in self.rd.get(r, ()):
                deps.add(x)
        deps.discard(idx)
        for r in reads:
            self.rd.setdefault(r, []).append(idx)
        for r in writes:
            self.lw[r] = idx
            self.rd[r] = []
        if dma:
            k = self.rr.get(eng, 0)
            self.rr[eng] = k + 1
            sid = (eng, k % self.nds)
            if sid in self.last_on_sem:
                deps.add(self.last_on_sem[sid])
            self.last_on_sem[sid] = idx
            self.cnt_on_sem[sid] = self.cnt_on_sem.get(sid, 0) + 1
            self.dma_sem_of[idx] = (sid, 16 * self.cnt_on_sem[sid])
        else:
            self.last_eng[eng] = idx
        self.ops.append(dict(eng=eng, fn=fn, deps=deps, dma=dma))
        return idx

    def dma(self, eng, out, in_, reads=(), writes=(), **kw):
        return self.add(eng, lambda e: e.dma_start(out=out, in_=in_, **kw), reads, writes, dma=True)

    def barrier(self, scratch_ap):
        deps = set(self.last_eng.values()) | set(self.last_on_sem.values())
        idx = self.add('dve', lambda e: e.memset(scratch_ap, 0.0))
        self.ops[idx]['deps'] |= deps
        self.ops[idx]['deps'].discard(idx)
        self.barrier_idx = idx

    def emit(self):
        nc = self.nc
        ops = self.ops
        engs = ['pe', 'act', 'dve', 'pool', 'sp']
        dma_sem_of = self.dma_sem_of
        flagged = set()
        for i, o in enumerate(ops):
            for d in o['deps']:
                if not ops[d]['dma']:
                    if ops[d]['eng'] == 'pe' and o['eng'] == 'pe' and not o['dma']:
                        continue
                    flagged.add(d)
        rank = {}
        cnt = {e: 0 for e in engs}
        for i, o in enumerate(ops):
            if i in flagged:
                cnt[o['eng']] += 1
                rank[i] = cnt[o['eng']]
        self.stats = dict(n_ops=len(ops), flagged=dict(cnt))
        with contextlib.ExitStack() as st:
            psem = {e: st.enter_context(nc.semaphore('p_' + e)) for e in engs}
            dsem = {}
            for sid in self.cnt_on_sem:
                dsem[sid] = st.enter_context(nc.semaphore('d_%s_%d' % sid))
            block = st.enter_context(nc.Block())
            by_eng = {e: [i for i, o in enumerate(ops) if o['eng'] == e] for e in engs}

            def run(ename, eobj):
                seen = {}
                nwait = 0
                for i in by_eng[ename]:
                    o = ops[i]
                    need = {}
                    for d in o['deps']:
                        od = ops[d]
                        if od['dma']:
                            sid, val = dma_sem_of[d]
                            key = ('d', sid)
                            sem = dsem[sid]
                        else:
                            if od['eng'] == 'pe' and ename == 'pe' and not o['dma']:
                                continue
                            key = ('p', od['eng'])
                            sem = psem[od['eng']]
                            val = rank[d]
                        if val > need.get(key, (None, 0))[1]:
                            need[key] = (sem, val)
                    for key, (sem, val) in need.items():
                        if seen.get(key, 0) >= val:
                            continue
                        seen[key] = val
                        eobj.wait_ge(sem, val)
                        nwait += 1
                    ins = o['fn'](eobj)
                    if o['dma']:
                        sid, val = dma_sem_of[i]
                        ins.then_inc(dsem[sid], 16)
                    elif i in flagged:
                        ins.then_inc(psem[ename], 1)
                for sid, c in self.cnt_on_sem.items():
                    if sid[0] == ename:
                        eobj.wait_ge(dsem[sid], 16 * c)
                self.stats['waits_' + ename] = nwait

            @block.tensor
            def _(e):
                run('pe', e)

            @block.scalar
            def _(e):
                run('act', e)

            @block.vector
            def _(e):
                run('dve', e)

            @block.gpsimd
            def _(e):
                run('pool', e)

            @block.sync
            def _(e):
                run('sp', e)


class Ring:
    def __init__(self, name, n):
        self.name, self.n, self.i = name, n, 0

    def next(self):
        k = self.i % self.n
        self.i += 1
        return k


def build_nc(phases=('kv', 'cmp', 'attn', 'ffn', 'sample')):
    nc = bass.Bass("TRN2", target_bir_lowering=False)

    def din(name, shape, dt=F32):
        return nc.dram_tensor(name, list(shape), dt, kind="ExternalInput").ap()

    def dout(name, shape, dt=F32):
        return nc.dram_tensor(name, list(shape), dt, kind="ExternalOutput").ap()

    xkv = din("xkv", [T, D])
    ropekv = din("ropekv", [T, 64])
    validc = din("validc", [128, NT])
    cmaskd = din("cmaskd", [NQB, 128, 2, 128])
    selmd = din("selmd", [NQB, 128, 128])
    ovd = din("ovd", [128, 2, 64])
    trid = din("trid", [128, 256])
    e2d = din("e2d", [128, T])
    identd = din("identd", [128, 128])
    wq_d = din("wq", [D, QW])
    wkv_d = din("wkv", [D, KVW])
    attn_norm_d = din("attn_norm", [1, D])
    ffn_norm_d = din("ffn_norm", [1, D])
    final_norm_d = din("final_norm", [1, D])
    w1k_d = din("cmp_w1_k", [2048, 128])
    w1v_d = din("cmp_w1_v", [2048, 128])
    w2k_d = din("cmp_w2_k", [128, 64])
    w2v_d = din("cmp_w2_v", [128, 64])
    posk_d = din("cmp_pos_kT", [64, 32])
    posv_d = din("cmp_pos_vT", [64, 32])
    lam_d = din("lam4", [1, 256])
    subln_d = din("subln_g", [1, 128])
    wout_d = din("w_out", [D, D])
    wup_d = din("w_up", [D, 2 * DFF])
    wdown_d = din("w_down", [DFF, D])
    convpT_d = din("convpT", [2 * DFF, 4])
    cvalidd = din("cvalidd", [128, 2])

    xs_d = din("xs", [4, D])
    ropes_d = din("ropes", [4, 64])
    ptc_d = din("ptc", [128, 4], I32)
    ptr_d = din("ptr", [4, 128], I32)
    cck_d = din("cache_cmp_k", [5120, 16384])
    ccv_d = din("cache_cmp_v", [5120, 16384])
    csk_d = din("cache_sel_k", [10240, 8192])
    csv_d = din("cache_sel_v", [10240, 8192])
    cdk_d = din("cache_diff_k", [5120, 65536])
    cdv_d = din("cache_diff_v", [5120, 65536])
    wink_d = din("win_k", [4, 512, 128])
    winv_d = din("win_v", [4, 512, 128])
    stT_d = din("stT", [2 * DFF, 8])
    stp_d = din("stp", [4, 2 * DFF])
    bm8_d = din("bm8", [8, 512])
    c01_d = din("c01", [8, 8])
    cmsk_d = din("cmsk", [128, 8])
    ovs_d = din("ovs", [128, 8, 257])
    winm_d = din("winm", [128, 32])
    identf_d = din("identf", [128, 128])
    iota16_d = din("iota16", [128, 16])

    o_kv = dout("o_kv", [T, KVW])
    o_skv = dout("o_skv", [4, KVW])
    o_swk = dout("o_swk", [4, 512, 128])
    o_swv = dout("o_swv", [4, 512, 128])
    o_suT = dout("o_suT", [2 * DFF, 4])
    o_sprev = dout("o_sprev", [4, 2 * DFF])
    o_ys = dout("o_ys", [4, D])
    zs_scr = nc.dram_tensor("zs_scr", [4, 3400], F32).ap()
    o_y = dout("o_y", [2048, D])
    o_conv = dout("o_conv", [2, 2 * DFF])
    xp_scr = nc.dram_tensor("xp_scr", [NQB * 128, D], F32).ap()

    S = Sched(nc)
    est = contextlib.ExitStack()
    with est:
        _cnt = [0]

        def sb(name, shape, dt, stack=est):
            _cnt[0] += 1
            return stack.enter_context(nc.sbuf_tensor("s%d_%s" % (_cnt[0], name), list(shape), dt))

        PF = [est.enter_context(nc.psum_tensor("pf%d" % i, [128, 512], F32)) for i in range(6)]
        PB = [est.enter_context(nc.psum_tensor("pb%d" % i, [128, 1024], BF16)) for i in range(2)]
        rS = Ring('S', 3)
        rA = Ring('A', 3)
        rB = Ring('B', 2)

        def bankS():
            k = rS.next()
            return PF[k], ('PF', k)

        def bankA():
            k = 3 + rA.next()
            return PF[k], ('PF', k)

        def bankB():
            k = rB.next()
            return PB[k], ('PB', k)

        ident = sb("ident", [128, 128], BF16)
        trim = sb("trim", [128, 256], BF16)
        gb = sb("gb", [128, D], F32)
        junk = sb("junk", [128, D], BF16)
        stat = sb("stat", [128, 16], F32)
        bar = sb("bar", [128, 1], F32)
        identf = sb("identf_sb", [128, 128], F32)
        xsa = sb("xsa", [4, D], F32)
        lamv = sb("lamv", [128, 4], F32)
        sgb = sb("sgb", [128, 128], F32)
        S.dma('sp', identf[:], identf_d, writes=['identf'])
        S.dma('pool', ident[:], identd, writes=['ident'])
        S.dma('pool', trim[:], trid, writes=['trim'])
        S.dma('sp', gb[:], attn_norm_d.partition_broadcast(128), writes=['gb'])

        statr = Ring('stat', 8)

        def norm_T(x_ap, x_res, hT_ap, hT_res, hn_ap, hn_res):
            k = statr.next()
            ss = stat[:, 2 * k:2 * k + 1]
            rs = stat[:, 2 * k + 1:2 * k + 2]
            sres = ('stat', k)
            S.add('act', lambda e: e.activation(out=junk[:], in_=x_ap, func=AF.Square, scale=1.0 / 32.0, accum_out=ss),
                  reads=[x_res], writes=[sres])
            S.add('act', lambda e: e.activation(out=rs, in_=ss, func=AF.Ln, bias=EPS, scale=1.0), reads=[sres], writes=[sres])
            S.add('act', lambda e: e.activation(out=rs, in_=rs, func=AF.Exp, scale=-0.5), reads=[sres], writes=[sres])
            S.add('dve', lambda e: e.scalar_tensor_tensor(out=hn_ap, in0=x_ap, scalar=rs, in1=gb[:], op0=ALU.mult, op1=ALU.mult),
                  reads=[x_res, sres, 'gb'], writes=[hn_res])
            pb, pbr = bankB()
            for c in range(8):
                S.add('pe', lambda e, c=c: e.transpose(out=pb[:, c * 128:(c + 1) * 128], in_=hn_ap[:, c * 128:(c + 1) * 128], identity=ident[:]),
                      reads=[hn_res, 'ident'], writes=[pbr])
            S.add('act', lambda e: e.copy(out=hT_ap, in_=pb[:].rearrange("p (c t) -> p c t", c=8)), reads=[pbr], writes=[hT_res])

        ropetmp = sb("ropetmp", [128, 1, 4, 256], F32)
        rtr = Ring('rt', 1)

        def rope(src, src_res, dst, dst_res, cs, cs_res, nh, P=128):
            k = rtr.next()
            tr = ('ropetmp', k)
            t = [ropetmp[0:P, k, i, 0:nh * 32].rearrange("p (h d) -> p h d", h=nh) for i in range(4)]
            cosb = cs[:, 0:32].unsqueeze(1).to_broadcast([P, nh, 32])
            sinb = cs[:, 32:64].unsqueeze(1).to_broadcast([P, nh, 32])
            x1 = src[:, :, 0:32]
            x2 = src[:, :, 32:64]
            S.add('dve', lambda e: e.tensor_tensor(out=t[0], in0=x1, in1=cosb, op=ALU.mult), reads=[src_res, cs_res], writes=[(tr, 0)])
            S.add('pool', lambda e: e.tensor_tensor(out=t[1], in0=x2, in1=sinb, op=ALU.mult), reads=[src_res, cs_res], writes=[(tr, 1)])
            S.add('dve', lambda e: e.tensor_tensor(out=t[2], in0=x2, in1=cosb, op=ALU.mult), reads=[src_res, cs_res], writes=[(tr, 2)])
            S.add('pool', lambda e: e.tensor_tensor(out=t[3], in0=x1, in1=sinb, op=ALU.mult), reads=[src_res, cs_res], writes=[(tr, 3)])
            S.add('dve', lambda e: e.tensor_tensor(out=dst[:, :, 0:32], in0=t[0], in1=t[1], op=ALU.subtract),
                  reads=[(tr, 0), (tr, 1), src_res], writes=[dst_res])
            S.add('pool', lambda e: e.tensor_tensor(out=dst[:, :, 32:64], in0=t[2], in1=t[3], op=ALU.add),
                  reads=[(tr, 2), (tr, 3), src_res], writes=[dst_res])

        with contextlib.ExitStack() as lst:
            lam = sb("lam", [128, 256], F32, lst)
            lamp = sb("lamp", [128, 128], F32, lst)
            S.dma('sp', lam[:], lam_d.partition_broadcast(128), writes=['lam'])
            S.dma('sp', sgb[:], subln_d.partition_broadcast(128), writes=['sgb'])
            lam4v = lam[:].rearrange("p (a b d) -> p a b d", a=2, b=2)
            S.add('dve', lambda e: e.tensor_tensor(out=lamp[:].rearrange("p (a d) -> p a d", a=2), in0=lam4v[:, :, 0, :], in1=lam4v[:, :, 1, :], op=ALU.mult), reads=['lam'], writes=['lamp'])
            S.add('dve', lambda e: e.reduce_sum(out=lamv[:, 0:2], in_=lamp[:].rearrange("p (a d) -> p a d", a=2), axis=AX.X), reads=['lamp'], writes=['lamv'])
            S.add('act', lambda e: e.activation(out=lamv[:, 0:2], in_=lamv[:, 0:2], func=AF.Exp), reads=['lamv'], writes=['lamv'])
            S.add('dve', lambda e: e.tensor_tensor(out=lamv[:, 2:3], in0=lamv[:, 1:2], in1=lamv[:, 0:1], op=ALU.subtract), reads=['lamv'], writes=['lamv'])
            S.add('dve', lambda e: e.tensor_scalar_add(out=lamv[:, 2:3], in0=lamv[:, 2:3], scalar1=-LAM_INIT), reads=['lamv'], writes=['lamv'])
            S.add('act', lambda e: e.mul(out=sgb[:], in_=sgb[:], mul=1.0 - LAM_INIT), reads=['sgb'], writes=['sgb'])
            S.barrier(bar[:])

        kvst = contextlib.ExitStack()
        kT4 = sb("kT4", [128, 4, T], BF16, kvst)
        dkT = sb("dkT", [128, 4, T], BF16, kvst)
        svx = sb("svx", [128, NT, 2, 65], BF16, kvst)
        wvx = sb("wvx", [128, NT, 2, 65], BF16, kvst)
        dvx = sb("dvx", [128, NT, 4, 129], BF16, kvst)
        kcT = sb("kcT", [128, 256], BF16, kvst)
        vcx = sb("vcx", [128, 2, 2, 129], BF16, kvst)
        wq = sb("wq_sb", [128, 8, QW], BF16, kvst)
        vld = sb("vld", [128, NT], F32, kvst)
        S.dma('sp', vld[:], validc, writes=['vld'])
        for k in range(8):
            S.dma('pool', wq[:, k, :], wq_d[k * 128:(k + 1) * 128, :], writes=['wq'])
        S.add('pool', lambda e: e.tensor_copy(out=svx[:, :, :, 64], in_=vld[:].unsqueeze(2).to_broadcast([128, NT, 2])), reads=['vld'], writes=['svx_v'])
        S.add('pool', lambda e: e.tensor_copy(out=wvx[:, :, :, 64], in_=vld[:].unsqueeze(2).to_broadcast([128, NT, 2])), reads=['vld'], writes=['wvx_v'])
        S.add('pool', lambda e: e.tensor_copy(out=dvx[:, :, :, 128], in_=vld[:].unsqueeze(2).to_broadcast([128, NT, 4])), reads=['vld'], writes=['dvx_v'])

        if 'kv' in phases:
            with contextlib.ExitStack() as pst:
                wkv = sb("wkv_sb", [128, 8, KVW], BF16, pst)
                for k in range(8):
                    S.dma('pool', wkv[:, k, :], wkv_d[k * 128:(k + 1) * 128, :], writes=['wkv'])
                if 'sample' in phases:
                  with contextlib.ExitStack() as s0st:
                      xs4 = sb("xs4", [4, D], F32, s0st)
                      hn4 = sb("hn4", [4, D], BF16, s0st)
                      hsT = sb("hsT", [128, 8, 4], BF16, s0st)
                      zc = sb("zc", [4, 2, 512], F32, s0st)
                      rps = sb("rps", [4, 64], F32, s0st)
                      S.dma('sp', xs4[:], xs_d, writes=['xs4'])
                      S.dma('sp', rps[:], ropes_d, writes=['rps'])
                      k_ = statr.next()
                      ss4 = stat[0:4, 2 * k_:2 * k_ + 1]
                      rs4 = stat[0:4, 2 * k_ + 1:2 * k_ + 2]
                      sres4 = ('stat', k_)
                      S.add('act', lambda e, ss4=ss4: e.activation(out=junk[0:4, :], in_=xs4[:], func=AF.Square, scale=1.0 / 32.0, accum_out=ss4), reads=['xs4'], writes=[sres4])
                      S.add('act', lambda e, ss4=ss4, rs4=rs4: e.activation(out=rs4, in_=ss4, func=AF.Ln, bias=EPS, scale=1.0), reads=[sres4], writes=[sres4])
                      S.add('act', lambda e, rs4=rs4: e.activation(out=rs4, in_=rs4, func=AF.Exp, scale=-0.5), reads=[sres4], writes=[sres4])
                      S.add('dve', lambda e, rs4=rs4: e.scalar_tensor_tensor(out=hn4[:], in0=xs4[:], scalar=rs4, in1=gb[0:4, :], op0=ALU.mult, op1=ALU.mult), reads=['xs4', sres4, 'gb'], writes=['hn4'])
                      S.add('pool', lambda e: e.tensor_copy(out=xsa[:], in_=xs4[:]), reads=['xs4'], writes=['xsa'])
                      pb, pbr = bankB()
                      for c in range(8):
                          S.add('pe', lambda e, c=c, pb=pb: e.transpose(out=pb[:, c * 4:(c + 1) * 4], in_=hn4[:, c * 128:(c + 1) * 128], identity=ident[0:4, 0:4]),
                                reads=['hn4', 'ident'], writes=[pbr])
                      S.add('act', lambda e, pb=pb: e.copy(out=hsT[:], in_=pb[:, 0:32].rearrange("p (c t) -> p c t", c=8)), reads=[pbr], writes=['hsT'])
                      zi = 0
                      for (W, wres, c0, cw, kind) in ([(wq, 'wq', 0, 512, 'q'), (wq, 'wq', 512, 512, 'dq'), (wq, 'wq', 1024, 24, 'gate')]
                                                      + [(wkv, 'wkv', c * 512, min(512, KVW - c * 512), 'kv%d' % c) for c in range(4)]):
                          pf, pfr = bankS()
                          for k in range(8):
                              S.add('pe', lambda e, k=k, pf=pf, W=W, c0=c0, cw=cw: e.matmul(pf[0:4, 0:cw], lhsT=hsT[:, k, :], rhs=W[:, k, c0:c0 + cw], start=(k == 0), stop=(k == 7)),
                                    reads=['hsT', wres], writes=[pfr])
                          zr_ = zi % 2
                          zi += 1
                          zres = ('zc', zr_)
                          zz = zc[:, zr_, :]
                          if kind == 'gate':
                              S.add('act', lambda e, pf=pf, zz=zz: e.activation(out=zz[:, 0:24], in_=pf[0:4, 0:24], func=AF.Exp, scale=-1.0), reads=[pfr], writes=[zres])
                              S.add('dve', lambda e, zz=zz: e.tensor_scalar_add(out=zz[:, 0:24], in0=zz[:, 0:24], scalar1=1.0), reads=[zres], writes=[zres])
                              S.add('dve', lambda e, zz=zz: e.reciprocal(out=zz[:, 0:24], in_=zz[:, 0:24]), reads=[zres], writes=[zres])
                              S.dma('sp', zs_scr[:, 1536:1560], zz[:, 0:24], reads=[zres], writes=['zs'])
                              continue
                          S.add('act', lambda e, pf=pf, zz=zz, cw=cw: e.copy(out=zz[:, 0:cw], in_=pf[0:4, 0:cw]), reads=[pfr], writes=[zres])

                          def rps_(a, b, nh, zz=zz, zres=zres):
                              v = zz[:, a:b].rearrange("p (h d) -> p h d", h=nh)
                              rope(v, zres, v, zres, rps[:], 'rps', nh, P=4)
                          if kind == 'q':
                              S.dma('sp', zs_scr[:, 0:512], zz[:, 0:512], reads=[zres], writes=['zs'])
                              rps_(0, 512, 8)
                              S.dma('sp', zs_scr[:, 512:1024], zz[:, 0:512], reads=[zres], writes=['zs'])
                          elif kind == 'dq':
                              rps_(0, 512, 8)
                              S.dma('sp', zs_scr[:, 1024:1536], zz[:, 0:512], reads=[zres], writes=['zs'])
                          else:
                              c = int(kind[2])
                              if c == 0:
                                  rps_(256, 384, 2)
                              elif c == 1:
                                  rps_(0, 128, 2)
                                  rps_(256, 512, 4)
                              elif c == 2:
                                  rps_(0, 256, 4)
                              S.dma('sp', zs_scr[:, 1600 + c0:1600 + c0 + cw], zz[:, 0:cw], reads=[zres], writes=['zs'])
                              S.dma('sp', o_skv[:, c0:c0 + cw], zz[:, 0:cw], reads=[zres])
                  S.barrier(bar[:])
                xk = sb("xk", [128, 2, D], F32, pst)
                hnk = sb("hnk", [128, 2, D], BF16, pst)
                hTk = sb("hTk", [128, 2, 8, 128], BF16, pst)
                zst = sb("zst", [128, 2, KVW], F32, pst)
                zb = sb("zb", [128, 1, 1024], BF16, pst)
                csk = sb("csk", [128, 2, 64], F32, pst)
                for t in range(int(os.environ.get('KV_TILES', NT))):
                    r = t % 2
                    S.dma('sp', xk[:, r, :], xkv[t * 128:(t + 1) * 128, :], writes=[('xk', r)])
                    S.dma('sp', csk[:, r, :], ropekv[t * 128:(t + 1) * 128, :], writes=[('csk', r)])
                    norm_T(xk[:, r, :], ('xk', r), hTk[:, r], ('hTk', r), hnk[:, r, :], ('hnk', r))
                    zr = ('zst', r)
                    z = zst[:, r, :]
                    for c in range(4):
                        c0 = c * 512
                        cw = min(512, KVW - c0)
                        pf, pfr = bankS()
                        for k in range(8):
                            S.add('pe', lambda e, k=k, pf=pf, c0=c0, cw=cw, r=r: e.matmul(pf[:, 0:cw], lhsT=hTk[:, r, k, :], rhs=wkv[:, k, c0:c0 + cw], start=(k == 0), stop=(k == 7)),
                                  reads=[('hTk', r), 'wkv'], writes=[pfr])
                        zc = (zr, c)
                        S.add('act', lambda e, pf=pf, z=z, c0=c0, cw=cw: e.copy(out=z[:, c0:c0 + cw], in_=pf[:, 0:cw]), reads=[pfr], writes=[zc])

                        def rp(a, b, nh, z=z, zc=zc, r=r):
                            v = z[:, a:b].rearrange("p (h d) -> p h d", h=nh)
                            rope(v, zc, v, zc, csk[:, r, :], ('csk', r), nh)
                        if c == 0:
                            rp(256, 384, 2)
                        elif c == 1:
                            rp(512, 640, 2)
                            rp(768, 1024, 4)
                        elif c == 2:
                            rp(1024, 1280, 4)
                    zall = [(zr, c) for c in range(4)]
                    S.dma('sp', o_kv[t * 128:(t + 1) * 128, :], z, reads=zall)
                    zbr = ('zb', 0)
                    S.add('pool', lambda e, z=z: e.tensor_copy(out=zb[:, 0, 0:384], in_=z[:, 0:384]), reads=zall, writes=[(zbr, 0)])
                    S.add('pool', lambda e, z=z: e.tensor_copy(out=zb[:, 0, 384:512], in_=z[:, 512:640]), reads=zall, writes=[(zbr, 1)])
                    S.add('pool', lambda e, z=z: e.tensor_copy(out=zb[:, 0, 512:1024], in_=z[:, 768:1280]), reads=zall, writes=[(zbr, 2)])
                    S.add('pool', lambda e, t=t, z=z: e.tensor_copy(out=svx[:, t, :, 0:64], in_=z[:, 384:512].rearrange("p (g d) -> p g d", g=2)), reads=zall, writes=[('svx', t)])
                    S.add('pool', lambda e, t=t, z=z: e.tensor_copy(out=wvx[:, t, :, 0:64], in_=z[:, 640:768].rearrange("p (g d) -> p g d", g=2)), reads=zall, writes=[('wvx', t)])
                    S.add('pool', lambda e, t=t, z=z: e.tensor_copy(out=dvx[:, t, :, 0:128], in_=z[:, 1280:1792].rearrange("p (h d) -> p h d", h=4)), reads=zall, writes=[('dvx', t)])
                    pb, pbr = bankB()
                    for c in range(8):
                        S.add('pe', lambda e, c=c, pb=pb: e.transpose(out=pb[:, c * 128:(c + 1) * 128], in_=zb[:, 0, c * 128:(c + 1) * 128], identity=ident[:]),
                              reads=[(zbr, 0), (zbr, 1), (zbr, 2), 'ident'], writes=[pbr])
                    S.add('act', lambda e, t=t, pb=pb: e.copy(out=kT4[:, :, t * 128:(t + 1) * 128], in_=pb[:, 0:512].rearrange("p (c t) -> p c t", c=4)),
                          reads=[pbr], writes=[('kT4', t)])
                    S.add('act', lambda e, t=t, pb=pb: e.copy(out=dkT[:, :, t * 128:(t + 1) * 128], in_=pb[:, 512:1024].rearrange("p (c t) -> p c t", c=4)),
                          reads=[pbr], writes=[('dkT', t)])
            S.barrier(bar[:])

        if 'cmp' in phases:
            with contextlib.ExitStack() as pst:
                w1s = sb("w1s", [128, 2, 32, 128], BF16, pst)
                posT = sb("posT", [128, 2, 32], BF16, pst)
                w2p = sb("w2p", [128, 2, 128], BF16, pst)
                w2v = sb("w2v", [128, 64], BF16, pst)
                posb = sb("posb", [128, 2], F32, pst)
                xh = sb("xh", [128, 2, 256], F32, pst)
                gtmp = sb("gtmp", [128, 2, 256], F32, pst)
                gl = sb("gl", [128, 2, 256], BF16, pst)
                cvl = sb("cvl", [128, 2], F32, pst)
                ovs = sb("ovs", [128, 2, 64], F32, pst)
                for X, wd_ in enumerate((w1k_d, w1v_d)):
                    for half in range(2):
                        S.dma('pool', w1s[half * 64:(half + 1) * 64, X, :, :], wd_.rearrange("(s d) h -> d s h", d=64), writes=['w1s'])
                S.dma('pool', posT[0:64, 0, :], posk_d, writes=['posT'])
                S.dma('pool', posT[0:64, 1, :], posv_d, writes=['posT'])
                S.add('pool', lambda e: e.memset(w2p[:], 0.0), writes=['w2p0'])
                S.dma('pool', w2p[:, 0, 0:64], w2k_d, reads=['w2p0'], writes=['w2p'])
                S.dma('pool', w2p[:, 1, 64:128], w2k_d, reads=['w2p0'], writes=['w2p'])
                S.dma('pool', w2v[:], w2v_d, writes=['w2v'])
                S.dma('sp', cvl[:], cvalidd, writes=['cvl'])
                S.dma('sp', ovs[:], ovd, writes=['ovs'])
                S.add('pool', lambda e: e.memset(vcx[:], 0.0), writes=['vcx'])
                S.add('pool', lambda e: e.memset(kcT[:], 0.0), writes=['kcT'])
                S.add('pool', lambda e: e.tensor_copy(out=vcx[:, :, :, 64], in_=cvl[:].unsqueeze(2).to_broadcast([128, 2, 2])), reads=['cvl', 'vcx'], writes=['vcx'])
                for g in range(2):
                    S.add('pool', lambda e, g=g: e.tensor_copy(out=vcx[:, :, g, 65:129], in_=ovs[:]), reads=['ovs', 'vcx'], writes=['vcx'])
                for X in range(2):
                    pfb, pfbr = bankS()
                    for s_ in range(32):
                        S.add('pe', lambda e, s_=s_, X=X, pfb=pfb: e.matmul(pfb[:, 0:1], lhsT=w1s[0:64, X, s_, :], rhs=posT[0:64, X, s_:s_ + 1], start=(s_ == 0), stop=(s_ == 31)),
                              reads=['w1s', 'posT'], writes=[pfbr])
                    S.add('act', lambda e, X=X, pfb=pfb: e.copy(out=posb[:, X:X + 1], in_=pfb[:, 0:1]), reads=[pfbr], writes=[('posb', X)])
                    kvv = kT4[:, X, :].rearrange("p (n s) -> p n s", s=16)
                    for g in range(2):
                        pf, pfr = bankS()
                        for s_ in range(32):
                            S.add('pe', lambda e, s_=s_, X=X, g=g, pf=pf, kvv=kvv: e.matmul(pf[:, 0:NCMP], lhsT=w1s[g * 64:(g + 1) * 64, X, s_, :],
                                                                                   rhs=kvv[g * 64:(g + 1) * 64, (s_ // 16):(s_ // 16) + NCMP, s_ % 16],
                                                                                   start=(s_ == 0), stop=(s_ == 31)),
                                  reads=['w1s'] + [('kT4', t) for t in range(NT)], writes=[pfr])
                        xg = xh[:, g, 0:NCMP]
                        tg = gtmp[:, g, 0:NCMP]
                        S.add('act', lambda e, pf=pf, xg=xg, X=X: e.activation(out=xg, in_=pf[:, 0:NCMP], func=AF.Identity, bias=posb[:, X:X + 1], scale=1.0),
                              reads=[pfr, ('posb', X)], writes=[('xh', g)])
                        S.add('dve', lambda e, xg=xg, tg=tg: e.tensor_tensor(out=tg, in0=xg, in1=xg, op=ALU.mult), reads=[('xh', g)], writes=[('gtmp', g)])
                        S.add('dve', lambda e, tg=tg: e.tensor_scalar(out=tg, in0=tg, scalar1=0.044715, scalar2=1.0, op0=ALU.mult, op1=ALU.add), reads=[('gtmp', g)], writes=[('gtmp', g)])
                        S.add('dve', lambda e, xg=xg, tg=tg: e.tensor_tensor(out=tg, in0=tg, in1=xg, op=ALU.mult), reads=[('gtmp', g), ('xh', g)], writes=[('gtmp', g)])
                        S.add('act', lambda e, tg=tg: e.activation(out=tg, in_=tg, func=AF.Tanh, scale=0.7978845608028654), reads=[('gtmp', g)], writes=[('gtmp', g)])
                        S.add('dve', lambda e, xg=xg, tg=tg: e.scalar_tensor_tensor(out=tg, in0=tg, scalar=1.0, in1=xg, op0=ALU.add, op1=ALU.mult), reads=[('gtmp', g), ('xh', g)], writes=[('gtmp', g)])
                        S.add('dve', lambda e, tg=tg, g=g: e.tensor_scalar_mul(out=gl[:, g, 0:NCMP], in0=tg, scalar1=0.5), reads=[('gtmp', g)], writes=[('gl', g)])
                    if X == 0:
                        pk, pkr = bankS()
                        for g in range(2):
                            S.add('pe', lambda e, g=g, pk=pk: e.matmul(pk[:, 0:NCMP], lhsT=w2p[:, g, :], rhs=gl[:, g, 0:NCMP], start=(g == 0), stop=(g == 1)),
                                  reads=['w2p', ('gl', g)], writes=[pkr])
                        S.add('act', lambda e, pk=pk: e.copy(out=kcT[:, 0:NCMP], in_=pk[:, 0:NCMP]), reads=[pkr, 'kcT'], writes=['kcT'])
                    else:
                        for tt in range(2):
                            rows = 128 if tt == 0 else NCMP - 128
                            pv, pvr = bankS()
                            for g in range(2):
                                S.add('pe', lambda e, g=g, tt=tt, rows=rows, pv=pv: e.matmul(pv[0:rows, g * 64:(g + 1) * 64], lhsT=gl[:, g, tt * 128:tt * 128 + rows], rhs=w2v[:], start=True, stop=True),
                                      reads=['w2v', ('gl', g)], writes=[pvr])
                            S.add('act', lambda e, tt=tt, rows=rows, pv=pv: e.copy(out=vcx[0:rows, tt, :, 0:64], in_=pv[0:rows, 0:128].rearrange("p (g d) -> p g d", g=2)),
                                  reads=[pvr, 'vcx'], writes=['vcx'])
            S.barrier(bar[:])

        if 'attn' in phases:
            with contextlib.ExitStack() as pst:
                wo = kT4[:, 0:2, :].rearrange("p a (k n) -> p (a k) n", k=4)
                e2 = sb("e2", [128, T], BF16, pst)
                xq = sb("xq", [128, 2, D], F32, pst)
                csq = sb("csq", [128, 2, 64], F32, pst)
                cm = sb("cm", [128, 2, 2, 128], BF16, pst)
                slm = sb("slm", [128, 2, 128], F32, pst)
                hnq = sb("hnq", [128, D], BF16, pst)
                hTq = sb("hTq", [128, 8, 128], BF16, pst)
                qf = sb("qf", [128, 2, 512], F32, pst)
                qb = sb("qb", [128, 1536], BF16, pst)
                qT = sb("qT", [128, 12, 128], BF16, pst)
                gsg = sb("gsg", [128, 24], F32, pst)
                pP = sb("pP", [128, 3, 512], BF16, pst)
                pM = sb("pM", [128, 2, 512], BF16, pst)
                osb = sb("osb", [128, 2, 520], F32, pst)
                usb = sb("usb", [128, 512], F32, pst)
                rl = sb("rl", [128, 32], F32, pst)
                cf = sb("cf", [128, 8], F32, pst)
                sc = sb("sc", [128, 2, 64], F32, pst)
                screp = sb("screp", [128, 2, 64], F32, pst)
                t8 = sb("t8", [128, 4, 8], F32, pst)
                sel01 = sb("sel01", [128, 128], F32, pst)
                negb = sb("negb", [128, 128], BF16, pst)
                negT = sb("negT", [128, 128], BF16, pst)
                negTr = sb("negTr", [128, 4, 128], BF16, pst)
                onsa = sb("onsa", [128, 512], F32, pst)
                otmp = sb("otmp", [128, 256], F32, pst)
                od = sb("od", [128, 128], F32, pst)
                ob = sb("ob", [128, D], BF16, pst)
                oT = sb("oT", [128, 8, 128], BF16, pst)
                for k in range(8):
                    S.dma('pool', wo[:, k, :], wout_d[k * 128:(k + 1) * 128, :], writes=['wo'])
                for k in range(4):
                    S.dma('pool', e2[:, k * 1024:(k + 1) * 1024], e2d[:, k * 1024:(k + 1) * 1024], writes=['e2'])
                rP = Ring('pP', 3)
                rM = Ring('pM', 2)
                rO = Ring('osb', 2)

                def exp_tile(pf, pfr, mask_ap=None, mask_res=None, nbc=4, eng='dve'):
                    kp = rP.next()
                    p = pP[:, kp, :]
                    S.add('act', lambda e: e.activation(out=p, in_=pf[:, 0:512], func=AF.Exp, scale=SCALE), reads=[pfr], writes=[('pP', kp)])
                    if mask_ap is None:
                        return p, ('pP', kp)
                    km = rM.next()
                    pm = pM[:, km, :]
                    S.add(eng, lambda e: e.tensor_tensor(out=pm.rearrange("p (r q) -> p r q", r=nbc), in0=p.rearrange("p (r q) -> p r q", r=nbc),
                                                         in1=mask_ap.unsqueeze(1).to_broadcast([128, nbc, 128]), op=ALU.mult),
                          reads=[('pP', kp), mask_res], writes=[('pM', km)])
                    return pm, ('pM', km)

                def evac(pa, par, ncol):
                    ko = rO.next()
                    o = osb[:, ko, 0:ncol]
                    S.add('act', lambda e: e.copy(out=o, in_=pa[:, 0:ncol]), reads=[par], writes=[('osb', ko)])
                    return o, ('osb', ko)

                rlr = Ring('rl', 8)

                def recip_l(lview, lres, n):
                    k = rlr.next()
                    o = rl[:, k * 4:k * 4 + n]
                    S.add('dve', lambda e: e.tensor_scalar_max(out=o, in0=lview, scalar1=1e-30), reads=[lres], writes=[('rl', k)])
                    S.add('dve', lambda e: e.reciprocal(out=o, in_=o), reads=[('rl', k)], writes=[('rl', k)])
                    return o, ('rl', k)

                nqb = int(os.environ.get('N_QB', NQB))
                for j in range(nqb):
                    fb = QB[j]
                    r = j % 2
                    xr_ = ('xq', r)
                    ACUT = int(os.environ.get('ACUT', 99))
                    S.dma('sp', xq[:, r, :], xkv[fb * 128:(fb + 1) * 128, :], writes=[xr_])
                    S.dma('sp', csq[:, r, :], ropekv[fb * 128:(fb + 1) * 128, :], writes=[('csq', r)])
                    S.dma('pool', cm[:, r], cmaskd[j], writes=[('cm', r)])
                    S.dma('sp', slm[:, r, :], selmd[j], writes=[('slm', r)])
                    norm_T(xq[:, r, :], xr_, hTq[:], 'hTq', hnq[:], 'hnq')
                    for c in range(3):
                        c0 = c * 512
                        cw = min(512, QW - c0)
                        pf, pfr = bankS()
                        for k in range(8):
                            S.add('pe', lambda e, k=k, pf=pf, c0=c0, cw=cw: e.matmul(pf[:, 0:cw], lhsT=hTq[:, k, :], rhs=wq[:, k, c0:c0 + cw], start=(k == 0), stop=(k == 7)),
                                  reads=['hTq', 'wq'], writes=[pfr])
                        if c < 2:
                            S.add('act', lambda e, pf=pf, c=c: e.copy(out=qf[:, c, :], in_=pf[:, 0:512]), reads=[pfr], writes=[('qf', c)])
                            if c == 0:
                                S.add('pool', lambda e: e.tensor_copy(out=qb[:, 0:512], in_=qf[:, 0, :]), reads=[('qf', 0)], writes=[('qb', 0)])
                            rope(qf[:, c, :].rearrange("p (h d) -> p h d", h=8), ('qf', c),
                                 qb[:, 512 * (c + 1):512 * (c + 2)].rearrange("p (h d) -> p h d", h=8), ('qb', c + 1), csq[:, r, :], ('csq', r), 8)
                        else:
                            S.add('act', lambda e, pf=pf: e.activation(out=gsg[:], in_=pf[:, 0:24], func=AF.Exp, scale=-1.0), reads=[pfr], writes=['gsg'])
                            S.add('dve', lambda e: e.tensor_scalar_add(out=gsg[:], in0=gsg[:], scalar1=1.0), reads=['gsg'], writes=['gsg'])
                            S.add('dve', lambda e: e.reciprocal(out=gsg[:], in_=gsg[:]), reads=['gsg'], writes=['gsg'])
                    if ACUT < 2:
                        continue
                    for half in range(2):
                        pb, pbr = bankB()
                        nb = 8 if half == 0 else 4
                        for c in range(nb):
                            cc = half * 8 + c
                            S.add('pe', lambda e, c=c, cc=cc, pb=pb: e.transpose(out=pb[:, c * 128:(c + 1) * 128], in_=qb[:, cc * 128:(cc + 1) * 128], identity=ident[:]),
                                  reads=[('qb', 0), ('qb', 1), ('qb', 2), 'ident'], writes=[pbr])
                        S.add('act', lambda e, half=half, nb=nb, pb=pb: e.copy(out=qT[:, half * 8:half * 8 + nb, :], in_=pb[:, 0:nb * 128].rearrange("p (c t) -> p c t", c=nb)),
                              reads=[pbr], writes=[('qT', half)])
                    qTr = [('qT', 0), ('qT', 1)]
                    gv = gsg[:].rearrange("p (g r b) -> p g r b", g=2, r=4)

                    def combine(o, ores, g, br, first):
                        ov_ = o.rearrange("p (r c) -> p r c", c=65)
                        rcp, rres = recip_l(ov_[:, :, 64], ores, 4)
                        cfa = cf[:, g * 4:(g + 1) * 4]
                        S.add('dve', lambda e: e.tensor_tensor(out=cfa, in0=rcp, in1=gv[:, g, :, br], op=ALU.mult), reads=[rres, 'gsg'], writes=[('cf', g)])
                        dst = onsa[:, g * 256:(g + 1) * 256].rearrange("p (r d) -> p r d", r=4)
                        cfb = cfa.unsqueeze(2).to_broadcast([128, 4, 64])
                        if first:
                            S.add('pool', lambda e: e.tensor_tensor(out=dst, in0=ov_[:, :, 0:64], in1=cfb, op=ALU.mult), reads=[ores, ('cf', g)], writes=[('onsa', g)])
                        else:
                            tv = otmp[:].rearrange("p (r d) -> p r d", r=4)
                            S.add('pool', lambda e: e.tensor_tensor(out=tv, in0=ov_[:, :, 0:64], in1=cfb, op=ALU.mult), reads=[ores, ('cf', g)], writes=['otmp'])
                            S.add('pool', lambda e: e.tensor_tensor(out=dst, in0=dst, in1=tv, op=ALU.add), reads=['otmp', ('onsa', g)], writes=[('onsa', g)])

                    if ACUT < 3:
                        continue
                    pu, pur = bankA()
                    for g in range(2):
                        pa, par = bankA()
                        for tt in range(2):
                            pf, pfr = bankS()
                            S.add('pe', lambda e, g=g, tt=tt, pf=pf: e.matmul(pf[:, 0:512], lhsT=kcT[g * 64:(g + 1) * 64, tt * 128:(tt + 1) * 128], rhs=qT[g * 64:(g + 1) * 64, 0:4, :], start=True, stop=True),
                                  reads=['kcT'] + qTr, writes=[pfr])
                            p, pres = exp_tile(pf, pfr, cm[:, r, tt, :], ('cm', r))
                            for rr in range(4):
                                S.add('pe', lambda e, g=g, tt=tt, rr=rr, pa=pa, p=p: e.matmul(pa[:, rr * 65:(rr + 1) * 65], lhsT=p[:, rr * 128:(rr + 1) * 128], rhs=vcx[:, tt, g, 0:65], start=(tt == 0 and rr == 0), stop=(tt == 1), skip_group_check=True),
                                      reads=[pres, 'vcx'], writes=[par])
                                S.add('pe', lambda e, g=g, tt=tt, rr=rr, pu=pu, p=p: e.matmul(pu[:, g * 256 + rr * 64:g * 256 + (rr + 1) * 64], lhsT=p[:, rr * 128:(rr + 1) * 128], rhs=vcx[:, tt, g, 65:129], start=(g == 0 and tt == 0 and rr == 0), stop=(tt == 1), skip_group_check=True),
                                      reads=[pres, 'vcx'], writes=[pur])
                        o, ores = evac(pa, par, 260)
                        combine(o, ores, g, 0, True)
                        ov_ = o.rearrange("p (r c) -> p r c", c=65)
                        rcp, rres = recip_l(ov_[:, :, 64], ores, 4)
                        if g == 0:
                            rc0, rr0 = rcp, rres
                        else:
                            rc1, rr1 = rcp, rres
                    if ACUT < 4:
                        continue
                    S.add('act', lambda e, pu=pu: e.copy(out=usb[:], in_=pu[:, 0:512]), reads=[pur], writes=['usb'])
                    for g, (rcp, rres) in enumerate(((rc0, rr0), (rc1, rr1))):
                        for rr in range(4):
                            u_ = usb[:, g * 256 + rr * 64:g * 256 + (rr + 1) * 64]
                            if rr == 0:
                                S.add('dve', lambda e, g=g, u_=u_, rcp=rcp: e.tensor_scalar_mul(out=sc[:, g, :], in0=u_, scalar1=rcp[:, 0:1]), reads=['usb', rres], writes=[('sc', g)])
                            else:
                                S.add('dve', lambda e, g=g, u_=u_, rcp=rcp, rr=rr: e.scalar_tensor_tensor(out=sc[:, g, :], in0=u_, scalar=rcp[:, rr:rr + 1], in1=sc[:, g, :], op0=ALU.mult, op1=ALU.add),
                                      reads=['usb', rres, ('sc', g)], writes=[('sc', g)])
                    okm = slm[:, r, 0:64]
                    adm = slm[:, r, 64:128]
                    S.add('dve', lambda e, okm=okm: e.tensor_tensor(out=sc[:], in0=sc[:], in1=okm.unsqueeze(1).to_broadcast([128, 2, 64]), op=ALU.mult), reads=[('sc', 0), ('sc', 1), ('slm', r)], writes=[('sc', 0), ('sc', 1)])
                    S.add('dve', lambda e, adm=adm: e.tensor_tensor(out=sc[:], in0=sc[:], in1=adm.unsqueeze(1).to_broadcast([128, 2, 64]), op=ALU.add), reads=[('sc', 0), ('sc', 1), ('slm', r)], writes=[('sc', 0), ('sc', 1)])
                    for g in range(2):
                        S.add('dve', lambda e, g=g: e.max(out=t8[:, g, :], in_=sc[:, g, :]), reads=[('sc', g)], writes=[('t8', g)])
                        S.add('dve', lambda e, g=g: e.match_replace(out=screp[:, g, :], in_to_replace=t8[:, g, :], in_values=sc[:, g, :], imm_value=-3.0e38), reads=[('sc', g), ('t8', g)], writes=[('screp', g)])
                        S.add('dve', lambda e, g=g: e.max(out=t8[:, 2 + g, :], in_=screp[:, g, :]), reads=[('screp', g)], writes=[('t8b', g)])
                        S.add('dve', lambda e, g=g, okm=okm: e.scalar_tensor_tensor(out=sel01[:, g * 64:(g + 1) * 64], in0=sc[:, g, :], scalar=t8[:, 2 + g, 7:8], in1=okm, op0=ALU.is_ge, op1=ALU.mult),
                              reads=[('sc', g), ('t8b', g), ('slm', r)], writes=[('sel01', g)])
                    S.add('dve', lambda e: e.tensor_scalar(out=negb[:], in0=sel01[:], scalar1=-1.0, scalar2=30000.0, op0=ALU.add, op1=ALU.mult), reads=[('sel01', 0), ('sel01', 1)], writes=['negb'])
                    if ACUT < 5:
                        continue
                    pb, pbr = bankB()
                    S.add('pe', lambda e, pb=pb: e.transpose(out=pb[:, 0:128], in_=negb[:], identity=ident[:]), reads=['negb', 'ident'], writes=[pbr])
                    S.add('act', lambda e, pb=pb: e.copy(out=negT[:], in_=pb[:, 0:128]), reads=[pbr], writes=['negT'])
                    S.add('pool', lambda e: e.tensor_copy(out=negTr[:], in_=negT[:].unsqueeze(1).to_broadcast([128, 4, 128])), reads=['negT'], writes=['negTr'])

                    if ACUT < 6:
                        continue
                    for g in range(2):
                        pa, par = bankA()
                        t_lo = max(0, fb - 4)
                        for t in range(t_lo, fb + 1):
                            pf, pfr = bankS()
                            S.add('pe', lambda e, g=g, t=t, pf=pf: e.matmul(pf[:, 0:512], lhsT=kT4[g * 64:(g + 1) * 64, 3, t * 128:(t + 1) * 128], rhs=qT[g * 64:(g + 1) * 64, 4:8, :], start=True, stop=True),
                                  reads=[('kT4', t)] + qTr, writes=[pfr])
                            if t == fb:
                                p, pres = exp_tile(pf, pfr, trim[:, 0:128], 'trim', eng='pool')
                            elif t == fb - 4:
                                p, pres = exp_tile(pf, pfr, trim[:, 128:256], 'trim', eng='pool')
                            else:
                                p, pres = exp_tile(pf, pfr)
                            for rr in range(4):
                                S.add('pe', lambda e, g=g, t=t, rr=rr, pa=pa, p=p, t_lo=t_lo: e.matmul(pa[:, rr * 65:(rr + 1) * 65], lhsT=p[:, rr * 128:(rr + 1) * 128], rhs=wvx[:, t, g, :], start=(t == t_lo and rr == 0), stop=(t == fb), skip_group_check=True),
                                      reads=[pres, ('wvx', t), 'wvx_v'], writes=[par])
                        o, ores = evac(pa, par, 260)
                        combine(o, ores, g, 2, False)

                    if ACUT < 7:
                        continue
                    DCUT = int(os.environ.get('DCUT', 99))
                    for hp in range(2):
                        pas = [bankA(), bankA()]
                        for t in range(fb + 1):
                            kp = rP.next()
                            p = pP[:, kp, :]
                            pres = ('pP', kp)
                            for m in range(2):
                                pf, pfr = bankS()
                                for hh in range(2):
                                    h_ = hp * 2 + hh
                                    S.add('pe', lambda e, h_=h_, m=m, hh=hh, t=t, pf=pf: e.matmul(pf[:, hh * 128:(hh + 1) * 128], lhsT=dkT[m * 64:(m + 1) * 64, h_, t * 128:(t + 1) * 128],
                                                                                            rhs=qT[m * 64:(m + 1) * 64, 8 + h_, :], start=True, stop=True),
                                          reads=[('dkT', t)] + qTr, writes=[pfr])
                                S.add('act', lambda e, m=m, pf=pf, p=p: e.activation(out=p[:, m * 256:(m + 1) * 256], in_=pf[:, 0:256], func=AF.Exp, scale=SCALE), reads=[pfr], writes=[pres])
                            pres2 = [pres]
                            if t == fb:
                                km = rM.next()
                                pm = pM[:, km, :]
                                S.add('dve', lambda e, p=p, pm=pm: e.tensor_tensor(out=pm.rearrange("p (r q) -> p r q", r=4), in0=p.rearrange("p (r q) -> p r q", r=4),
                                                                             in1=trim[:, 0:128].unsqueeze(1).to_broadcast([128, 4, 128]), op=ALU.mult),
                                      reads=pres2 + ['trim'], writes=[('pM', km)])
                                p = pm
                                pres2 = [('pM', km)]
                            DCUT = int(os.environ.get('DCUT', 99))
                            if DCUT < 2:
                                continue
                            for hh in range(2):
                                h_ = hp * 2 + hh
                                pa, par = pas[hh]
                                for m in range(2):
                                    cidx = m * 2 + hh
                                    S.add('pe', lambda e, h_=h_, m=m, cidx=cidx, t=t, pa=pa, p=p: e.matmul(pa[:, m * 129:(m + 1) * 129], lhsT=p[:, cidx * 128:(cidx + 1) * 128], rhs=dvx[:, t, h_, :], start=(t == 0 and m == 0), stop=(t == fb), skip_group_check=True),
                                          reads=pres2 + [('dvx', t), 'dvx_v'], writes=[par])
                        if DCUT < 3:
                            continue
                        for hh in range(2):
                            h_ = hp * 2 + hh
                            pa, par = pas[hh]
                            o, ores = evac(pa, par, 258)
                            if DCUT < 4:
                                continue
                            ov_ = o.rearrange("p (m c) -> p m c", c=129)
                            rcp, rres = recip_l(ov_[:, :, 128], ores, 2)
                            S.add('dve', lambda e, rcp=rcp: e.tensor_tensor(out=rcp[:, 1:2], in0=rcp[:, 1:2], in1=lamv[:, 2:3], op=ALU.mult), reads=[rres, 'lamv'], writes=[rres])
                            S.add('dve', lambda e, o=o, rcp=rcp: e.tensor_scalar_mul(out=od[:], in0=o[:, 0:128], scalar1=rcp[:, 0:1]), reads=[ores, rres], writes=['od'])
                            S.add('dve', lambda e, o=o, rcp=rcp: e.scalar_tensor_tensor(out=od[:], in0=o[:, 129:257], scalar=rcp[:, 1:2], in1=od[:], op0=ALU.mult, op1=ALU.add), reads=[ores, rres, 'od'], writes=['od'])
                            k = statr.next()
                            ss = stat[:, 2 * k:2 * k + 1]
                            rs = stat[:, 2 * k + 1:2 * k + 2]
                            sres = ('stat', k)
                            S.add('act', lambda e, ss=ss: e.activation(out=junk[:, 0:128], in_=od[:], func=AF.Square, scale=1.0 / math.sqrt(128.0), accum_out=ss), reads=['od'], writes=[sres])
                            S.add('act', lambda e, ss=ss, rs=rs: e.activation(out=rs, in_=ss, func=AF.Ln, bias=EPS, scale=1.0), reads=[sres], writes=[sres])
                            S.add('act', lambda e, rs=rs: e.activation(out=rs, in_=rs, func=AF.Exp, scale=-0.5), reads=[sres], writes=[sres])
                            S.add('dve', lambda e, rs=rs, h_=h_: e.scalar_tensor_tensor(out=ob[:, 512 + h_ * 128:512 + (h_ + 1) * 128], in0=od[:], scalar=rs, in1=sgb[:], op0=ALU.mult, op1=ALU.mult),
                                  reads=['od', sres, 'sgb'], writes=[('ob', 4 + h_)])

                    if ACUT < 8:
                        continue
                    for g in range(2):
                        pa, par = bankA()
                        for t in range(fb + 1):
                            pf, pfr = bankS()
                            S.add('pe', lambda e, g=g, t=t, pf=pf: e.matmul(pf[:, 0:512], lhsT=kT4[g * 64:(g + 1) * 64, 2, t * 128:(t + 1) * 128], rhs=qT[g * 64:(g + 1) * 64, 4:8, :], start=True, stop=False),
                                  reads=[('kT4', t)] + qTr, writes=[pfr])
                            S.add('pe', lambda e, g=g, t=t, pf=pf: e.matmul(pf[:, 0:512], lhsT=e2[g * 64:(g + 1) * 64, t * 128:(t + 1) * 128], rhs=negTr[g * 64:(g + 1) * 64, :, :], start=False, stop=True),
                                  reads=['e2', 'negTr'], writes=[pfr])
                            if t == fb:
                                p, pres = exp_tile(pf, pfr, trim[:, 0:128], 'trim')
                            else:
                                p, pres = exp_tile(pf, pfr)
                            for rr in range(4):
                                S.add('pe', lambda e, g=g, t=t, rr=rr, pa=pa, p=p: e.matmul(pa[:, rr * 65:(rr + 1) * 65], lhsT=p[:, rr * 128:(rr + 1) * 128], rhs=svx[:, t, g, :], start=(t == 0 and rr == 0), stop=(t == fb), skip_group_check=True),
                                      reads=[pres, ('svx', t), 'svx_v'], writes=[par])
                        o, ores = evac(pa, par, 260)
                        combine(o, ores, g, 1, False)
                    S.add('pool', lambda e: e.tensor_copy(out=ob[:, 0:512], in_=onsa[:]), reads=[('onsa', 0), ('onsa', 1)], writes=[('ob', 0)])

                    if ACUT < 9:
                        continue
                    pb, pbr = bankB()
                    for c in range(8):
                        S.add('pe', lambda e, c=c, pb=pb: e.transpose(out=pb[:, c * 128:(c + 1) * 128], in_=ob[:, c * 128:(c + 1) * 128], identity=ident[:]),
                              reads=[('ob', 0)] + [('ob', 4 + h_) for h_ in range(4)] + ['ident'], writes=[pbr])
                    S.add('act', lambda e, pb=pb: e.copy(out=oT[:], in_=pb[:].rearrange("p (c t) -> p c t", c=8)), reads=[pbr], writes=['oT'])
                    for n in range(2):
                        pf, pfr = bankS()
                        for c in range(8):
                            S.add('pe', lambda e, c=c, n=n, pf=pf: e.matmul(pf[:, 0:512], lhsT=oT[:, c, :], rhs=wo[:, c, n * 512:(n + 1) * 512], start=(c == 0), stop=(c == 7)),
                                  reads=['oT', 'wo'], writes=[pfr])
                        S.add('dve', lambda e, n=n, pf=pf, r=r: e.tensor_tensor(out=xq[:, r, n * 512:(n + 1) * 512], in0=pf[:, 0:512], in1=xq[:, r, n * 512:(n + 1) * 512], op=ALU.add),
                              reads=[pfr, xr_], writes=[xr_])
                    S.dma('sp', xp_scr[j * 128:(j + 1) * 128, :], xq[:, r, :], reads=[xr_], writes=[('xps', j)])
            S.barrier(bar[:])
        kvst.close()

        if 'sample' in phases:
            IOA = bass.IndirectOffsetOnAxis
            with contextlib.ExitStack() as pst:
                wo2 = sb("wo2", [128, 8, D], BF16, pst)
                ptc = sb("ptc", [128, 4], I32, pst)
                onesb = sb("onesb", [128, 1], BF16, pst)
                onesf = sb("onesf", [128, 128], F32, pst)
                bm8 = sb("bm8", [8, 512], F32, pst)
                c01 = sb("c01", [8, 8], F32, pst)
                cmat = sb("cmat", [8, 4], F32, pst)
                T2 = sb("T2", [128, 8, 4], BF16, pst)
                oTd = sb("oTd", [128, 4, 4], BF16, pst)
                qdb = sb("qdb", [128, 512], F32, pst)
                qrb = sb("qrb", [128, 512], F32, pst)
                small = sb("small", [128, 512], F32, pst)
                smallb = sb("smallb", [128, 512], BF16, pst)
                for k in range(8):
                    S.dma('pool', wo2[:, k, :], wout_d[k * 128:(k + 1) * 128, :], writes=['wo2'])
                S.dma('sp', ptc[:], ptc_d, writes=['ptc'])
                io16 = sb("io16", [128, 16], F32, pst)
                ptf = sb("ptf", [128, 8], F32, pst)
                idxf = sb("idxf", [128, 4, 16], F32, pst)
                idxd = sb("idxd", [128, 4, 16], I32, pst)
                idxc = sb("idxc", [128, 4, 4], I32, pst)
                S.dma('sp', io16[:], iota16_d, writes=['io16'])
                S.add('dve', lambda e: e.tensor_copy(out=ptf[:, 0:4], in_=ptc[:]), reads=['ptc'], writes=['ptf'])
                S.add('dve', lambda e: e.tensor_scalar_mul(out=ptf[:, 4:8], in0=ptf[:, 0:4], scalar1=16.0), reads=['ptf'], writes=['ptf'])
                for si_ in range(4):
                    S.add('dve', lambda e, si_=si_: e.tensor_scalar_add(out=idxf[:, si_, :], in0=io16[:], scalar1=ptf[:, 4 + si_:5 + si_]), reads=['ptf', 'io16'], writes=['idxf'])
                S.add('dve', lambda e: e.tensor_copy(out=idxd[:], in_=idxf[:]), reads=['idxf'], writes=['idxd'])
                S.add('dve', lambda e: e.tensor_scalar_mul(out=ptf[:, 4:8], in0=ptf[:, 0:4], scalar1=4.0), reads=['ptf', 'idxf'], writes=['ptf'])
                for si_ in range(4):
                    S.add('dve', lambda e, si_=si_: e.tensor_scalar_add(out=idxf[:, si_, 0:4], in0=io16[:, 0:4], scalar1=ptf[:, 4 + si_:5 + si_]), reads=['ptf', 'io16', 'idxd'], writes=['idxf'])
                S.add('dve', lambda e: e.tensor_copy(out=idxc[:], in_=idxf[:, :, 0:4]), reads=['idxf'], writes=['idxc'])
                cdk_r = cdk_d.rearrange("n (t c) -> (n t) c", c=4096)
                cdv_r = cdv_d.rearrange("n (t c) -> (n t) c", c=4096)
                cck_r = cck_d.rearrange("n (t c) -> (n t) c", c=4096)
                ccv_r = ccv_d.rearrange("n (t c) -> (n t) c", c=4096)
                S.dma('sp', bm8[:], bm8_d, writes=['bm8'])
                S.dma('sp', c01[:], c01_d, writes=['c01'])
                S.add('pool', lambda e: e.memset(onesb[:], 1.0), writes=['onesb'])
                S.add('pool', lambda e: e.memset(T2[:], 0.0), writes=['T2'])
                S.add('pool', lambda e: e.memset(oTd[:], 0.0), writes=['oTd'])
                S.add('pool', lambda e: e.memset(onesf[:], 1.0), writes=['onesf'])
                S.add('dve', lambda e: e.scalar_tensor_tensor(out=cmat[:], in0=c01[:, 4:8], scalar=lamv[0:8, 2:3], in1=c01[:, 0:4], op0=ALU.mult, op1=ALU.add), reads=['c01', 'lamv'], writes=['cmat'])
                nseq = int(os.environ.get('N_SEQ', 4))

                def evac_s(pa, par, P, ncol, dst, dres):
                    S.add('act', lambda e: e.copy(out=dst, in_=pa[0:P, 0:ncol]), reads=[par], writes=[dres])

                for si in range(nseq):
                    with contextlib.ExitStack() as dst_:
                        kch = sb("kch", [128, 2, 4096], F32, dst_)
                        vch = sb("vch", [128, 2, 4096], F32, dst_)
                        vbf = sb("vbf", [128, 2, 4096], BF16, dst_)
                        prod = sb("prod", [128, 4096], F32, dst_)
                        sS = sb("sS", [128, 2, 64], F32, dst_)
                        pS = sb("pS", [128, 2, 64], BF16, dst_)
                        knv = sb("knv", [1, 1024], F32, dst_)
                        od8 = sb("od8", [8, 512], F32, dst_)
                        on8 = sb("on8", [8, 132], F32, dst_)
                        S.dma('sp', qdb[:], zs_scr[si:si + 1, 1024:1536].partition_broadcast(128), reads=['zs'], writes=['qdb'])
                        S.dma('sp', knv[:], zs_scr[si:si + 1, 1600 + 768:1600 + 1792], reads=['zs'], writes=['knv'])
                        pod, podr = bankA()
                        pld, pldr = bankA()
                        ntk = int(os.environ.get('N_TK', 16))
                        first = True
                        for tk in range(ntk):
                            r = tk % 2
                            S.add('pool', lambda e, r=r, tk=tk, si=si: e.indirect_dma_start(out=kch[:, r, :], out_offset=None, in_=cdk_r[:, :],
                                                                                           in_offset=IOA(ap=idxd[:, si, tk:tk + 1], axis=0)),
                                  reads=['idxd'], writes=[('kch', r)], dma=True)
                            S.add('pool', lambda e, r=r, tk=tk, si=si: e.indirect_dma_start(out=vch[:, r, :], out_offset=None, in_=cdv_r[:, :],
                                                                                           in_offset=IOA(ap=idxd[:, si, tk:tk + 1], axis=0)),
                                  reads=['idxd'], writes=[('vch', r)], dma=True)
                            S.add('dve', lambda e, r=r: e.tensor_tensor(out=prod[:].rearrange("p (t c) -> p t c", t=8), in0=kch[:, r, :].rearrange("p (t c) -> p t c", t=8),
                                                                        in1=qdb[:].unsqueeze(1).to_broadcast([128, 8, 512]), op=ALU.mult),
                                  reads=[('kch', r), 'qdb'], writes=['prod'])
                            S.add('dve', lambda e, r=r: e.reduce_sum(out=sS[:, r, :], in_=prod[:].rearrange("p (a d) -> p a d", d=64), axis=AX.X), reads=['prod'], writes=[('sS', r)])
                            S.add('act', lambda e, r=r: e.activation(out=pS[:, r, :], in_=sS[:, r, :], func=AF.Exp, scale=SCALE), reads=[('sS', r)], writes=[('pS', r)])
                            S.add('act', lambda e, r=r: e.copy(out=vbf[:, r, :], in_=vch[:, r, :]), reads=[('vch', r)], writes=[('vbf', r)])
                            for tok in range(8):
                                S.add('pe', lambda e, r=r, tok=tok, first=first, pod=pod: e.matmul(pod[0:8, 0:512], lhsT=pS[:, r, tok * 8:(tok + 1) * 8], rhs=vbf[:, r, tok * 512:(tok + 1) * 512], start=first, stop=False),
                                      reads=[('pS', r), ('vbf', r)], writes=[podr])
                                S.add('pe', lambda e, r=r, tok=tok, first=first, pld=pld: e.matmul(pld[0:8, 0:1], lhsT=pS[:, r, tok * 8:(tok + 1) * 8], rhs=onesb[:, 0:1], start=first, stop=False),
                                      reads=[('pS', r), 'onesb'], writes=[pldr])
                                first = False
                        S.add('dve', lambda e: e.tensor_tensor(out=prod[0:1, 0:512], in0=knv[:, 0:512], in1=qdb[0:1, :], op=ALU.mult), reads=['knv', 'qdb', 'prod'], writes=['prod'])
                        S.add('dve', lambda e: e.reduce_sum(out=sS[0:1, 0, 0:8], in_=prod[0:1, 0:512].rearrange("p (a d) -> p a d", d=64), axis=AX.X), reads=['prod', ('sS', 0)], writes=[('sS', 0)])
                        S.add('act', lambda e: e.activation(out=pS[0:1, 0, 0:8], in_=sS[0:1, 0, 0:8], func=AF.Exp, scale=SCALE), reads=[('sS', 0), ('pS', 0)], writes=[('pS', 0)])
                        S.add('act', lambda e: e.copy(out=vbf[0:1, 0, 0:512], in_=knv[:, 512:1024]), reads=['knv', ('vbf', 0)], writes=[('vbf', 0)])
                        S.add('pe', lambda e, pod=pod: e.matmul(pod[0:8, 0:512], lhsT=pS[0:1, 0, 0:8], rhs=vbf[0:1, 0, 0:512], start=False, stop=True), reads=[('pS', 0), ('vbf', 0)], writes=[podr])
                        S.add('pe', lambda e, pld=pld: e.matmul(pld[0:8, 0:1], lhsT=pS[0:1, 0, 0:8], rhs=onesb[0:1, 0:1], start=False, stop=True), reads=[('pS', 0), 'onesb'], writes=[pldr])
                        evac_s(pod, podr, 8, 512, od8[:], 'od8')
                        evac_s(pld, pldr, 8, 1, on8[:, 128:129], 'ld8')
                        S.add('dve', lambda e: e.tensor_tensor(out=od8[:], in0=od8[:], in1=bm8[:], op=ALU.mult), reads=['od8', 'bm8'], writes=['od8'])
                        S.add('dve', lambda e: e.reduce_sum(out=on8[:, 0:128], in_=od8[:].rearrange("p (h e) -> p e h", h=4), axis=AX.X), reads=['od8'], writes=['on8'])
                        S.add('dve', lambda e: e.tensor_scalar_max(out=on8[:, 128:129], in0=on8[:, 128:129], scalar1=1e-30), reads=['ld8'], writes=['ld8'])
                        S.add('dve', lambda e: e.reciprocal(out=on8[:, 128:129], in_=on8[:, 128:129]), reads=['ld8'], writes=['ld8'])
                        S.add('dve', lambda e: e.tensor_scalar_mul(out=on8[:, 0:128], in0=on8[:, 0:128], scalar1=on8[:, 128:129]), reads=['on8', 'ld8'], writes=['on8'])
                        pf, pfr = bankS()
                        S.add('pe', lambda e, pf=pf: e.matmul(pf[0:4, 0:128], lhsT=cmat[:], rhs=on8[:, 0:128], start=True, stop=True), reads=['cmat', 'on8'], writes=[pfr])
                        od4 = small[0:4, 0:128]
                        evac_s(pf, pfr, 4, 128, od4, 'od4')
                        k_ = statr.next()
                        ss4 = stat[0:4, 2 * k_:2 * k_ + 1]
                        rs4 = stat[0:4, 2 * k_ + 1:2 * k_ + 2]
                        sres4 = ('stat', k_)
                        S.add('act', lambda e, ss4=ss4: e.activation(out=junk[0:4, 0:128], in_=od4, func=AF.Square, scale=1.0 / math.sqrt(128.0), accum_out=ss4), reads=['od4'], writes=[sres4])
                        S.add('act', lambda e, ss4=ss4, rs4=rs4: e.activation(out=rs4, in_=ss4, func=AF.Ln, bias=EPS, scale=1.0), reads=[sres4], writes=[sres4])
                        S.add('act', lambda e, rs4=rs4: e.activation(out=rs4, in_=rs4, func=AF.Exp, scale=-0.5), reads=[sres4], writes=[sres4])
                        odn = smallb[0:4, 0:128]
                        S.add('dve', lambda e, rs4=rs4: e.scalar_tensor_tensor(out=odn, in0=od4, scalar=rs4, in1=sgb[0:4, :], op0=ALU.mult, op1=ALU.mult), reads=['od4', sres4, 'sgb'], writes=['odn'])
                        pb, pbr = bankB()
                        S.add('pe', lambda e, pb=pb: e.transpose(out=pb[:, 0:4], in_=odn, identity=ident[0:4, 0:4]), reads=['odn', 'ident'], writes=[pbr])
                        S.add('act', lambda e, pb=pb, si=si: e.copy(out=oTd[:, :, si], in_=pb[:, 0:4]), reads=[pbr, 'oTd'], writes=['oTd'])
                    S.barrier(bar[:])
                    with contextlib.ExitStack() as wst:
                        wkt = sb("wkt", [128, 4, 128], F32, wst)
                        wvt = sb("wvt", [128, 4, 128], F32, wst)
                        wvx_s = sb("wvx_s", [128, 4, 2, 65], BF16, wst)
                        prw = sb("prw", [128, 4, 512], F32, wst)
                        sW = sb("sW", [128, 32], F32, wst)
                        pW8 = sb("pW8", [128, 4, 2, 8], BF16, wst)
                        winm = sb("winm", [128, 32], F32, wst)
                        nkv = sb("nkv", [1, 1792], F32, wst)
                        nvx = sb("nvx", [1, 2, 2, 65], BF16, wst)
                        pn8 = sb("pn8", [1, 2, 2, 8], BF16, wst)
                        gate8 = sb("gate8", [8, 3], F32, wst)
                        o3 = sb("o3", [8, 3, 65], F32, wst)
                        onsa8 = sb("onsa8", [8, 128], F32, wst)
                        onsab = sb("onsab", [8, 128], BF16, wst)
                        S.dma('sp', wkt[:], wink_d[si].rearrange("(t p) c -> p t c", p=128), writes=['wkt'])
                        S.dma('sp', wvt[:], winv_d[si].rearrange("(t p) c -> p t c", p=128), writes=['wvt'])
                        S.dma('sp', winm[:], winm_d, writes=['winm'])
                        S.dma('sp', qrb[:], zs_scr[si:si + 1, 512:1024].partition_broadcast(128), reads=['zs'], writes=['qrb'])
                        S.dma('sp', nkv[:], zs_scr[si:si + 1, 1600:1600 + 1792], reads=['zs'], writes=['nkv'])
                        S.dma('sp', gate8[:], zs_scr[si, 1536:1560].rearrange("(h b) -> h b", b=3), reads=['zs'], writes=['gate8'])
                        S.dma('sp', o_swk[si, 0:511, :], wink_d[si, 1:512, :])
                        S.dma('sp', o_swv[si, 0:511, :], winv_d[si, 1:512, :])
                        S.dma('sp', o_swk[si, 511:512, :], zs_scr[si:si + 1, 1600 + 512:1600 + 640], reads=['zs'])
                        S.dma('sp', o_swv[si, 511:512, :], zs_scr[si:si + 1, 1600 + 640:1600 + 768], reads=['zs'])
                        qv = qrb[:].rearrange("p (r g d) -> p g r d", r=4, g=2)
                        S.add('pool', lambda e: e.memset(wvx_s[:], 1.0), writes=['wvx_s'])
                        S.add('pool', lambda e: e.tensor_copy(out=wvx_s[:, :, :, 0:64], in_=wvt[:].rearrange("p t (g d) -> p t g d", g=2)), reads=['wvt', 'wvx_s'], writes=['wvx_s'])
                        S.add('pool', lambda e: e.memset(pW8[:], 0.0), writes=['pW8'])
                        S.add('pool', lambda e: e.memset(pn8[:], 0.0), writes=['pn8'])
                        S.add('pool', lambda e: e.memset(nvx[:], 1.0), writes=['nvx'])
                        for t in range(4):
                            S.add('dve', lambda e, t=t: e.tensor_tensor(out=prw[:, t, :].rearrange("p (g r d) -> p g r d", g=2, r=4),
                                                                        in0=wkt[:, t, :].rearrange("p (g d) -> p g d", g=2).unsqueeze(2).to_broadcast([128, 2, 4, 64]),
                                                                        in1=qv, op=ALU.mult), reads=['wkt', 'qrb'], writes=['prw'])
                        S.add('dve', lambda e: e.reduce_sum(out=sW[:], in_=prw[:].rearrange("p t (a d) -> p (t a) d", d=64), axis=AX.X), reads=['prw'], writes=['sW'])
                        S.add('act', lambda e: e.activation(out=sW[:], in_=sW[:], func=AF.Exp, scale=SCALE), reads=['sW'], writes=['sW'])
                        S.add('dve', lambda e: e.tensor_tensor(out=sW[:], in0=sW[:], in1=winm[:], op=ALU.mult), reads=['sW', 'winm'], writes=['sW'])
                        sWv = sW[:].rearrange("p (t g r) -> p t g r", t=4, g=2)
                        for g in range(2):
                            S.add('pool', lambda e, g=g: e.tensor_copy(out=pW8[:, :, g, g * 4:(g + 1) * 4], in_=sWv[:, :, g, :]), reads=['sW', 'pW8'], writes=['pW8'])
                        for bi, (kc0, vc0) in enumerate(((512, 640), (256, 384))):
                            S.add('dve', lambda e, kc0=kc0: e.tensor_tensor(out=prw[0:1, 0, :].rearrange("p (g r d) -> p g r d", g=2, r=4),
                                                                          in0=nkv[:, kc0:kc0 + 128].rearrange("p (g d) -> p g d", g=2).unsqueeze(2).to_broadcast([1, 2, 4, 64]),
                                                                          in1=qv[0:1], op=ALU.mult), reads=['nkv', 'qrb', 'prw', 'sW'], writes=['prw'])
                            sn = small[0:1, 256 + bi * 8:256 + bi * 8 + 8]
                            S.add('dve', lambda e, sn=sn: e.reduce_sum(out=sn, in_=prw[0:1, 0, :].rearrange("p (a d) -> p a d", d=64), axis=AX.X), reads=['prw'], writes=[('sn', bi)])
                            S.add('act', lambda e, sn=sn: e.activation(out=sn, in_=sn, func=AF.Exp, scale=SCALE), reads=[('sn', bi)], writes=[('sn', bi)])
                            for g in range(2):
                                S.add('pool', lambda e, g=g, bi=bi, sn=sn: e.tensor_copy(out=pn8[:, bi, g, g * 4:(g + 1) * 4], in_=sn[:, g * 4:(g + 1) * 4]), reads=[('sn', bi), 'pn8'], writes=['pn8'])
                            S.add('pool', lambda e, bi=bi, vc0=vc0: e.tensor_copy(out=nvx[:, bi, :, 0:64], in_=nkv[:, vc0:vc0 + 128].rearrange("p (g d) -> p g d", g=2)), reads=['nkv', 'nvx'], writes=['nvx'])
                        pw_, pwr = bankA()
                        first = True
                        for t in range(4):
                            for g in range(2):
                                S.add('pe', lambda e, t=t, g=g, first=first, pw_=pw_: e.matmul(pw_[0:8, 0:65], lhsT=pW8[:, t, g, :], rhs=wvx_s[:, t, g, :], start=first, stop=False), reads=['pW8', 'wvx_s'], writes=[pwr])
                                first = False
                        for g in range(2):
                            S.add('pe', lambda e, g=g, pw_=pw_: e.matmul(pw_[0:8, 0:65], lhsT=pn8[:, 0, g, :], rhs=nvx[:, 0, g, :], start=False, stop=(g == 1)), reads=['pn8', 'nvx'], writes=[pwr])
                        evac_s(pw_, pwr, 8, 65, o3[:, 2, :], ('o3', 2))

                        with contextlib.ExitStack() as cst:
                            cTs = sb("cTs", [128, 16384], BF16, cst)
                            pg = sb("pg", [128, 2, 4096], F32, cst)
                            pgb = sb("pgb", [128, 2, 4096], BF16, cst)
                            w1s2 = sb("w1s2", [128, 2, 32, 128], BF16, cst)
                            posT2 = sb("posT2", [128, 2, 32], BF16, cst)
                            w2p2 = sb("w2p2", [128, 2, 128], BF16, cst)
                            w2v2 = sb("w2v2", [128, 64], BF16, cst)
                            posb2 = sb("posb2", [128, 2], F32, cst)
                            xh2 = sb("xh2", [128, 512], F32, cst)
                            gt2 = sb("gt2", [128, 512], F32, cst)
                            gl2 = sb("gl2", [128, 2, 1024], BF16, cst)
                            kcTs = sb("kcTs", [128, 1024], BF16, cst)
                            vcs = sb("vcs", [128, 8, 128], BF16, cst)
                            qsT = sb("qsT", [128, 4], F32, cst)
                            qsTb = sb("qsTb", [128, 4], BF16, cst)
                            cmsk = sb("cmsk", [128, 8], F32, cst)
                            pC = sb("pC", [128, 2, 8, 4], F32, cst)
                            lsum = sb("lsum", [128, 64], F32, cst)
                            l8 = sb("l8", [128, 8], F32, cst)
                            imp = sb("imp", [128, 2, 8], F32, cst)
                            pC8 = sb("pC8", [128, 2, 8, 8], BF16, cst)
                            ovsS = sb("ovs_s", [128, 8, 257], F32, cst)
                            for X, wd_ in enumerate((w1k_d, w1v_d)):
                                for half in range(2):
                                    S.dma('pool', w1s2[half * 64:(half + 1) * 64, X, :, :], wd_.rearrange("(s d) h -> d s h", d=64), writes=['w1s2'])
                            S.dma('pool', posT2[0:64, 0, :], posk_d, writes=['posT2'])
                            S.dma('pool', posT2[0:64, 1, :], posv_d, writes=['posT2'])
                            S.add('pool', lambda e: e.memset(w2p2[:], 0.0), writes=['w2p20'])
                            S.dma('pool', w2p2[:, 0, 0:64], w2k_d, reads=['w2p20'], writes=['w2p2'])
                            S.dma('pool', w2p2[:, 1, 64:128], w2k_d, reads=['w2p20'], writes=['w2p2'])
                            S.dma('pool', w2v2[:], w2v_d, writes=['w2v2'])
                            S.dma('sp', cmsk[:], cmsk_d, writes=['cmsk'])
                            S.dma('sp', ovsS[:], ovs_d, writes=['ovs_s'])
                            S.dma('sp', qsT[:], zs_scr[si, 0:512].rearrange("(r g d) -> (g d) r", r=4, g=2), reads=['zs'], writes=['qsT'], allow_slow_non_contiguous=True)
                            S.add('pool', lambda e: e.tensor_copy(out=qsTb[:], in_=qsT[:]), reads=['qsT'], writes=['qsTb'])
                            S.add('pool', lambda e: e.memset(pC8[:], 0.0), writes=['pC8'])
                            S.add('pool', lambda e: e.memset(gl2[:], 0.0), writes=[('gl2', 0), ('gl2', 1)])
                            cTv = cTs[:].rearrange("p (pg tk) -> p tk pg", tk=128)
                            for X, cache_d in enumerate((cck_r, ccv_r)):
                                for c in range(4):
                                    r = c % 2
                                    S.add('pool', lambda e, r=r, c=c, si=si, cache_d=cache_d: e.indirect_dma_start(out=pg[:, r, :], out_offset=None, in_=cache_d[:, :],
                                                                                                              in_offset=IOA(ap=idxc[:, si, c:c + 1], axis=0)),
                                          reads=['idxc'], writes=[('pg', r)], dma=True)
                                    S.add('act', lambda e, r=r: e.copy(out=pgb[:, r, :], in_=pg[:, r, :]), reads=[('pg', r)], writes=[('pgb', r)])
                                    for b8 in range(4):
                                        pb, pbr = bankB()
                                        for jj in range(8):
                                            tok = b8 * 8 + jj
                                            S.add('pe', lambda e, r=r, tok=tok, jj=jj, pb=pb: e.transpose(out=pb[:, jj * 128:(jj + 1) * 128], in_=pgb[:, r, tok * 128:(tok + 1) * 128], identity=ident[:]),
                                                  reads=[('pgb', r), 'ident'], writes=[pbr])
                                        t0_ = c * 32 + b8 * 8
                                        S.add('act', lambda e, pb=pb, t0_=t0_: e.copy(out=cTv[:, t0_:t0_ + 8, :], in_=pb[:].rearrange("p (j q) -> p j q", j=8)), reads=[pbr], writes=['cTs'])
                                pfb, pfbr = bankS()
                                for s_ in range(32):
                                    S.add('pe', lambda e, s_=s_, X=X, pfb=pfb: e.matmul(pfb[:, 0:1], lhsT=w1s2[0:64, X, s_, :], rhs=posT2[0:64, X, s_:s_ + 1], start=(s_ == 0), stop=(s_ == 31)),
                                          reads=['w1s2', 'posT2'], writes=[pfbr])
                                S.add('act', lambda e, X=X, pfb=pfb: e.copy(out=posb2[:, X:X + 1], in_=pfb[:, 0:1]), reads=[pfbr], writes=[('posb2', X)])
                                kvv = cTs[:].rearrange("p (n s) -> p n s", s=16)
                                for g in range(2):
                                    for (n0, ncl) in ((0, 512), (512, 511)):
                                        pf, pfr = bankS()
                                        for s_ in range(32):
                                            S.add('pe', lambda e, s_=s_, X=X, g=g, pf=pf, n0=n0, ncl=ncl: e.matmul(pf[:, 0:ncl], lhsT=w1s2[g * 64:(g + 1) * 64, X, s_, :],
                                                                                                      rhs=kvv[g * 64:(g + 1) * 64, n0 + (s_ // 16):n0 + (s_ // 16) + ncl, s_ % 16],
                                                                                                      start=(s_ == 0), stop=(s_ == 31)),
                                                  reads=['w1s2', 'cTs'], writes=[pfr])
                                        xg = xh2[:, 0:ncl]
                                        tg = gt2[:, 0:ncl]
                                        S.add('act', lambda e, pf=pf, xg=xg, X=X, ncl=ncl: e.activation(out=xg, in_=pf[:, 0:ncl], func=AF.Identity, bias=posb2[:, X:X + 1], scale=1.0), reads=[pfr, ('posb2', X)], writes=['xh2'])
                                        S.add('dve', lambda e, xg=xg, tg=tg: e.tensor_tensor(out=tg, in0=xg, in1=xg, op=ALU.mult), reads=['xh2'], writes=['gt2'])
                                        S.add('dve', lambda e, tg=tg: e.tensor_scalar(out=tg, in0=tg, scalar1=0.044715, scalar2=1.0, op0=ALU.mult, op1=ALU.add), reads=['gt2'], writes=['gt2'])
                                        S.add('dve', lambda e, xg=xg, tg=tg: e.tensor_tensor(out=tg, in0=tg, in1=xg, op=ALU.mult), reads=['gt2', 'xh2'], writes=['gt2'])
                                        S.add('act', lambda e, tg=tg: e.activation(out=tg, in_=tg, func=AF.Tanh, scale=0.7978845608028654), reads=['gt2'], writes=['gt2'])
                                        S.add('dve', lambda e, xg=xg, tg=tg: e.scalar_tensor_tensor(out=tg, in0=tg, scalar=1.0, in1=xg, op0=ALU.add, op1=ALU.mult), reads=['gt2', 'xh2'], writes=['gt2'])
                                        S.add('dve', lambda e, tg=tg, g=g, n0=n0, ncl=ncl: e.tensor_scalar_mul(out=gl2[:, g, n0:n0 + ncl], in0=tg, scalar1=0.5), reads=['gt2'], writes=[('gl2', g)])
                                if X == 0:
                                    for (n0, ncl) in ((0, 512), (512, 512)):
                                        pk, pkr = bankS()
                                        for g in range(2):
                                            S.add('pe', lambda e, g=g, pk=pk, n0=n0, ncl=ncl: e.matmul(pk[:, 0:ncl], lhsT=w2p2[:, g, :], rhs=gl2[:, g, n0:n0 + ncl], start=(g == 0), stop=(g == 1)),
                                                  reads=['w2p2', ('gl2', g)], writes=[pkr])
                                        S.add('act', lambda e, pk=pk, n0=n0, ncl=ncl: e.copy(out=kcTs[:, n0:n0 + ncl], in_=pk[:, 0:ncl]), reads=[pkr], writes=['kcTs'])
                                else:
                                    for tt in range(8):
                                        pv, pvr = bankS()
                                        for g in range(2):
                                            S.add('pe', lambda e, g=g, tt=tt, pv=pv: e.matmul(pv[:, g * 64:(g + 1) * 64], lhsT=gl2[:, g, tt * 128:(tt + 1) * 128], rhs=w2v2[:], start=True, stop=True),
                                                  reads=['w2v2', ('gl2', g)], writes=[pvr])
                                        S.add('act', lambda e, tt=tt, pv=pv: e.copy(out=vcs[:, tt, :], in_=pv[:, 0:128]), reads=[pvr], writes=['vcs'])
                            psg = []
                            for g in range(2):
                                pf, pfr = bankS()
                                psg.append((pf, pfr))
                                for tt in range(8):
                                    S.add('pe', lambda e, g=g, tt=tt, pf=pf: e.matmul(pf[:, tt * 4:(tt + 1) * 4], lhsT=kcTs[g * 64:(g + 1) * 64, tt * 128:(tt + 1) * 128], rhs=qsTb[g * 64:(g + 1) * 64, :], start=True, stop=True),
                                          reads=['kcTs', 'qsTb'], writes=[pfr])
                            for g in range(2):
                                pf, pfr = psg[g]
                                S.add('act', lambda e, g=g, pf=pf: e.activation(out=pC[:, g].rearrange("p t r -> p (t r)"), in_=pf[:, 0:32], func=AF.Exp, scale=SCALE), reads=[pfr], writes=[('pC', g)])
                            S.add('dve', lambda e: e.tensor_tensor(out=pC[:], in0=pC[:], in1=cmsk[:].unsqueeze(1).unsqueeze(3).to_broadcast([128, 2, 8, 4]), op=ALU.mult), reads=[('pC', 0), ('pC', 1), 'cmsk'], writes=[('pC', 0), ('pC', 1)])
                            pl, plr = bankS()
                            S.add('pe', lambda e, pl=pl: e.matmul(pl[:, 0:64], lhsT=onesf[:], rhs=pC[:].rearrange("p g t r -> p (g t r)"), start=True, stop=True), reads=['onesf', ('pC', 0), ('pC', 1)], writes=[plr])
                            evac_s(pl, plr, 128, 64, lsum[:], 'lsum')
                            S.add('dve', lambda e: e.reduce_sum(out=l8[:].rearrange("p (g r) -> p g r", g=2), in_=lsum[:].rearrange("p (g t r) -> p g r t", g=2, t=8), axis=AX.X), reads=['lsum'], writes=['l8'])
                            S.add('dve', lambda e: e.tensor_scalar_max(out=l8[:], in0=l8[:], scalar1=1e-30), reads=['l8'], writes=['l8'])
                            S.add('dve', lambda e: e.reciprocal(out=l8[:], in_=l8[:]), reads=['l8'], writes=['l8'])
                            S.add('dve', lambda e: e.tensor_tensor(out=pC[:], in0=pC[:], in1=l8[:].rearrange("p (g r) -> p g r", g=2).unsqueeze(2).to_broadcast([128, 2, 8, 4]), op=ALU.mult), reads=[('pC', 0), ('pC', 1), 'l8'], writes=[('pC', 0), ('pC', 1)])
                            S.add('dve', lambda e: e.reduce_sum(out=imp[:], in_=pC[:], axis=AX.X), reads=[('pC', 0), ('pC', 1)], writes=['imp'])
                            for g in range(2):
                                S.add('pool', lambda e, g=g: e.tensor_copy(out=pC8[:, g, :, g * 4:(g + 1) * 4], in_=pC[:, g]), reads=[('pC', g), 'pC8'], writes=['pC8'])
                            psc, pscr = bankS()
                            for tt in range(8):
                                S.add('pe', lambda e, tt=tt, psc=psc: e.matmul(psc[0:2, 0:257], lhsT=imp[:, :, tt], rhs=ovsS[:, tt, :], start=(tt == 0), stop=(tt == 7)), reads=['imp', 'ovs_s'], writes=[pscr])
                            scs = small[0:2, 0:257]
                            evac_s(psc, pscr, 2, 257, scs, 'scs')
                            poc, pocr = bankA()
                            first = True
                            for g in range(2):
                                for tt in range(8):
                                    S.add('pe', lambda e, g=g, tt=tt, first=first, poc=poc: e.matmul(poc[0:8, 0:64], lhsT=pC8[:, g, tt, :], rhs=vcs[:, tt, g * 64:(g + 1) * 64], start=first, stop=(g == 1 and tt == 7)), reads=['pC8', 'vcs'], writes=[pocr])
                                    first = False
                            evac_s(poc, pocr, 8, 64, o3[:, 0, 0:64], ('o3', 0))
                        S.barrier(bar[:])

                        with contextlib.ExitStack() as sst:
                            ptr2 = sb("ptr2", [2, 128], I32, sst)
                            hp1 = sb("hp1", [2, 256], F32, sst)
                            sq = sb("sq", [2, 256], F32, sst)
                            t8s = sb("t8s", [2, 32], F32, sst)
                            ids = sb("ids", [2, 16], F32, sst)
                            offs = sb("offs", [16, 2], I32, sst)
                            offf = sb("offf", [16, 2], F32, sst)
                            ksel = sb("ksel", [16, 8192], F32, sst)
                            vsel = sb("vsel", [16, 8192], F32, sst)
                            vselb = sb("vselb", [16, 64, 65], BF16, sst)
                            qrs = sb("qrs", [16, 4, 64], F32, sst)
                            prs = sb("prs", [16, 8, 4, 64], F32, sst)
                            sSel = sb("sSel", [16, 256], F32, sst)
                            pSel8 = sb("pSel8", [16, 64, 8], BF16, sst)
                            S.dma('sp', ptr2[:], ptr_d[si:si + 1, :].partition_broadcast(2), writes=['ptr2'])
                            hv = hp1[:].rearrange("p (k b) -> p k b", b=2)
                            S.add('dve', lambda e: e.tensor_copy(out=hv[:, :, 0], in_=ptr2[:]), reads=['ptr2'], writes=['hp1'])
                            S.add('dve', lambda e: e.tensor_scalar(out=hv[:, :, 0], in0=hv[:, :, 0], scalar1=2.0, scalar2=1.0, op0=ALU.mult, op1=ALU.add), reads=['hp1'], writes=['hp1'])
                            S.add('dve', lambda e: e.tensor_scalar_add(out=hv[:, :, 1], in0=hv[:, :, 0], scalar1=1.0), reads=['hp1'], writes=['hp1'])
                            sp_ = scs[:, 1:255]
                            S.add('dve', lambda e: e.max(out=t8s[:, 0:8], in_=sp_), reads=['scs'], writes=['t8s'])
                            S.add('dve', lambda e: e.match_replace(out=sq[:, 0:254], in_to_replace=t8s[:, 0:8], in_values=sp_, imm_value=-3.0e38), reads=['scs', 't8s'], writes=['sq'])
                            S.add('dve', lambda e: e.max(out=t8s[:, 8:16], in_=sq[:, 0:254]), reads=['sq'], writes=['t8s'])
                            S.add('dve', lambda e: e.scalar_tensor_tensor(out=sq[:, 0:254], in0=sp_, scalar=t8s[:, 12:13], in1=hp1[:, 1:255], op0=ALU.is_ge, op1=ALU.mult), reads=['scs', 't8s', 'hp1', 'sq'], writes=['sq'])
                            S.add('dve', lambda e: e.memset(ids[:], 0.0), writes=['ids'])
                            S.add('dve', lambda e: e.max(out=ids[:, 0:8], in_=sq[:, 0:254]), reads=['sq', 'ids'], writes=['ids'])
                            S.add('dve', lambda e: e.match_replace(out=sq[:, 0:254], in_to_replace=ids[:, 0:8], in_values=sq[:, 0:254], imm_value=0.0), reads=['sq', 'ids'], writes=['sq'])
                            S.add('dve', lambda e: e.max(out=t8s[:, 16:24], in_=sq[:, 0:254]), reads=['sq'], writes=['t8s'])
                            S.add('dve', lambda e: e.tensor_copy(out=ids[:, 8:13], in_=t8s[:, 16:21]), reads=['t8s', 'ids'], writes=['ids'])
                            S.add('dve', lambda e: e.tensor_copy(out=ids[:, 13:14], in_=hp1[:, 0:1]), reads=['hp1', 'ids'], writes=['ids'])
                            S.add('dve', lambda e: e.tensor_copy(out=ids[:, 14:15], in_=hp1[:, 255:256]), reads=['hp1', 'ids'], writes=['ids'])
                            pt_, ptr_ = bankS()
                            S.add('pe', lambda e, pt_=pt_: e.transpose(out=pt_[0:16, 0:2], in_=ids[:], identity=identf[0:2, 0:2]), reads=['ids', 'identf'], writes=[ptr_])
                            evac_s(pt_, ptr_, 16, 2, offf[:], 'offf')
                            S.add('dve', lambda e: e.tensor_scalar_add(out=offf[:], in0=offf[:], scalar1=-1.0), reads=['offf'], writes=['offf'])
                            S.add('dve', lambda e: e.tensor_copy(out=offs[:], in_=offf[:]), reads=['offf'], writes=['offs'])
                            S.add('pool', lambda e: e.memset(pSel8[:], 0.0), writes=['pSel8'])
                            S.add('pool', lambda e: e.memset(vselb[:], 1.0), writes=['vselb'])
                            pos_, posr_ = bankA()
                            first = True
                            for g in range(2):
                                S.add('pool', lambda e, g=g: e.indirect_dma_start(out=ksel[0:15, :], out_offset=None, in_=csk_d[:, :], in_offset=IOA(ap=offs[0:15, g:g + 1], axis=0)),
                                      reads=['offs'], writes=['ksel'], dma=True)
                                S.add('pool', lambda e, g=g: e.indirect_dma_start(out=vsel[0:15, :], out_offset=None, in_=csv_d[:, :], in_offset=IOA(ap=offs[0:15, g:g + 1], axis=0)),
                                      reads=['offs'], writes=['vsel'], dma=True)
                                S.dma('sp', qrs[0:15], zs_scr[si, 512:1024].rearrange("(r g d) -> g r d", r=4, g=2)[g].partition_broadcast(15), reads=['zs'], writes=['qrs'])
                                kv3 = ksel[0:15, :].rearrange("p (t c) -> p t c", c=128)
                                for c8 in range(8):
                                    S.add('dve', lambda e, g=g, c8=c8, kv3=kv3: e.tensor_tensor(out=prs[0:15], in0=kv3[:, c8 * 8:(c8 + 1) * 8, g * 64:(g + 1) * 64].unsqueeze(2).to_broadcast([15, 8, 4, 64]),
                                                                                         in1=qrs[0:15].unsqueeze(1).to_broadcast([15, 8, 4, 64]), op=ALU.mult), reads=['ksel', 'qrs'], writes=['prs'])
                                    S.add('dve', lambda e, c8=c8: e.reduce_sum(out=sSel[0:15, c8 * 32:(c8 + 1) * 32], in_=prs[0:15].rearrange("p t r d -> p (t r) d"), axis=AX.X), reads=['prs'], writes=['sSel'])
                                S.add('act', lambda e: e.activation(out=sSel[0:15, :], in_=sSel[0:15, :], func=AF.Exp, scale=SCALE), reads=['sSel'], writes=['sSel'])
                                S.add('pool', lambda e, g=g: e.tensor_copy(out=pSel8[0:15, :, g * 4:(g + 1) * 4], in_=sSel[0:15, :].rearrange("p (t r) -> p t r", r=4)), reads=['sSel', 'pSel8'], writes=['pSel8'])
                                if g == 1:
                                    S.add('pool', lambda e: e.memset(pSel8[0:15, :, 0:4], 0.0), reads=['pSel8'], writes=['pSel8'])
                                S.add('act', lambda e, g=g: e.copy(out=vselb[0:15, :, 0:64], in_=vsel[0:15, :].rearrange("p (t c) -> p t c", c=128)[:, :, g * 64:(g + 1) * 64]), reads=['vsel', 'vselb'], writes=['vselb'])
                                for tok in range(64):
                                    S.add('pe', lambda e, tok=tok, first=first, pos_=pos_: e.matmul(pos_[0:8, 0:65], lhsT=pSel8[0:15, tok, :], rhs=vselb[0:15, tok, :], start=first, stop=False), reads=['pSel8', 'vselb'], writes=[posr_])
                                    first = False
                                S.add('pe', lambda e, g=g, pos_=pos_: e.matmul(pos_[0:8, 0:65], lhsT=pn8[:, 1, g, :], rhs=nvx[:, 1, g, :], start=False, stop=(g == 1)), reads=['pn8', 'nvx'], writes=[posr_])
                            evac_s(pos_, posr_, 8, 65, o3[:, 1, :], ('o3', 1))
                        S.add('dve', lambda e: e.memset(o3[:, 0, 64:65], 1.0), reads=[('o3', 0)], writes=[('o3', 0)])
                        rl3 = small[0:8, 300:303]
                        S.add('dve', lambda e: e.tensor_scalar_max(out=rl3, in0=o3[:, :, 64], scalar1=1e-30), reads=[('o3', 0), ('o3', 1), ('o3', 2)], writes=['rl3'])
                        S.add('dve', lambda e: e.reciprocal(out=rl3, in_=rl3), reads=['rl3'], writes=['rl3'])
                        S.add('dve', lambda e: e.tensor_tensor(out=rl3, in0=rl3, in1=gate8[:], op=ALU.mult), reads=['rl3', 'gate8'], writes=['rl3'])
                        S.add('dve', lambda e: e.tensor_scalar_mul(out=onsa8[:, 0:64], in0=o3[:, 0, 0:64], scalar1=rl3[:, 0:1]), reads=['rl3', ('o3', 0)], writes=['onsa8'])
                        for br in (1, 2):
                            S.add('dve', lambda e, br=br: e.scalar_tensor_tensor(out=onsa8[:, 0:64], in0=o3[:, br, 0:64], scalar=rl3[:, br:br + 1], in1=onsa8[:, 0:64], op0=ALU.mult, op1=ALU.add), reads=['rl3', ('o3', br), 'onsa8'], writes=['onsa8'])
                        S.add('dve', lambda e: e.tensor_copy(out=onsa8[:, 64:128], in_=onsa8[:, 0:64]), reads=['onsa8'], writes=['onsa8'])
                        S.add('dve', lambda e: e.tensor_copy(out=onsab[:], in_=onsa8[:]), reads=['onsa8'], writes=['onsab'])
                        pb, pbr = bankB()
                        S.add('pe', lambda e, pb=pb: e.transpose(out=pb[:, 0:8], in_=onsab[:], identity=ident[0:8, 0:8]), reads=['onsab', 'ident'], writes=[pbr])
                        S.add('act', lambda e, pb=pb, si=si: e.copy(out=T2[:, :, si], in_=pb[:, 0:8]), reads=[pbr, 'T2'], writes=['T2'])
                    S.barrier(bar[:])
                for n in range(2):
                    pf, pfr = bankS()
                    seq_ = [('e', hd) for hd in (0, 2, 4, 6)] + [('d', h_) for h_ in range(4)] + [('o', hd) for hd in (1, 3, 5, 7)]
                    for ii, (kd, v_) in enumerate(seq_):
                        if kd == 'd':
                            S.add('pe', lambda e, v_=v_, n=n, pf=pf, ii=ii: e.matmul(pf[0:4, 0:512], lhsT=oTd[:, v_, :], rhs=wo2[:, 4 + v_, n * 512:(n + 1) * 512], start=(ii == 0), stop=(ii == 11)), reads=['oTd', 'wo2'], writes=[pfr])
                        else:
                            b0 = (v_ % 2) * 64
                            S.add('pe', lambda e, v_=v_, n=n, pf=pf, ii=ii, b0=b0: e.matmul(pf[0:4, 0:512], lhsT=T2[b0:b0 + 64, v_, :], rhs=wo2[b0:b0 + 64, v_ // 2, n * 512:(n + 1) * 512], start=(ii == 0), stop=(ii == 11)), reads=['T2', 'wo2'], writes=[pfr])
                    S.add('dve', lambda e, n=n, pf=pf: e.tensor_tensor(out=xsa[:, n * 512:(n + 1) * 512], in0=pf[0:4, 0:512], in1=xsa[:, n * 512:(n + 1) * 512], op=ALU.add), reads=[pfr, 'xsa'], writes=['xsa'])
            S.barrier(bar[:])

        if 'ffn' in phases:
            with contextlib.ExitStack() as pst:
                hw = sb("hw", [128, 22 * 1024 + 512], BF16, pst)
                h2T = hw[:, 0:8 * 2560].rearrange("p (k t) -> p k t", k=8)
                wdn = hw[:, 0:22 * 1024].rearrange("p (i n) -> p i n", i=22)
                gT = sb("gT", [128, 22, 2048], BF16, pst)
                wu = sb("wu", [128, 3, 2, 8, 128], BF16, pst)
                cvp = sb("cvp", [128, 44, 4], F32, pst)
                xr = sb("xr", [128, 2, D], F32, pst)
                hn2 = sb("hn2", [128, D], BF16, pst)
                cab = sb("cab", [128, 2, 2, 260], F32, pst)
                sab = sb("sab", [128, 2, 260], F32, pst)
                ucv = sb("ucv", [128, 44, 2], F32, pst)
                yt = sb("yt", [128, 2, D], F32, pst)
                do_s = 'sample' in phases
                if do_s:
                    hn4f = sb("hn4f", [4, D], BF16, pst)
                    h2sT = sb("h2sT", [128, 8, 4], BF16, pst)
                    stT = sb("stT", [128, 44, 8], F32, pst)
                    usT = sb("usT", [128, 44, 4], F32, pst)
                    csb = sb("csb", [128, 2, 4], F32, pst)
                    gsT = sb("gsT", [128, 22, 4], BF16, pst)
                    ys4 = sb("ys4", [4, D], F32, pst)
                    S.dma('sp', stT[:], stT_d.rearrange("(t p) c -> p t c", p=128), writes=['stT'])
                    S.dma('sp', o_sprev[:, :], stp_d[:, :])
                S.dma('sp', gb[:], ffn_norm_d.partition_broadcast(128), writes=['gb'])
                S.dma('sp', cvp[:], convpT_d.rearrange("(t p) c -> p t c", p=128), writes=['cvp'])
                nqb = int(os.environ.get('N_QB', NQB))
                for j in range(nqb):
                    r = j % 2
                    S.dma('sp', xr[:, r, :], xp_scr[j * 128:(j + 1) * 128, :], reads=[('xps', j)], writes=[('xr', r)])
                    norm_T(xr[:, r, :], ('xr', r), h2T[:, :, j * 128:(j + 1) * 128], ('h2T', j), hn2[:], 'hn2')
                if do_s:
                    k_ = statr.next()
                    ss4 = stat[0:4, 2 * k_:2 * k_ + 1]
                    rs4 = stat[0:4, 2 * k_ + 1:2 * k_ + 2]
                    sres4 = ('stat', k_)
                    S.add('act', lambda e, ss4=ss4: e.activation(out=junk[0:4, :], in_=xsa[:], func=AF.Square, scale=1.0 / 32.0, accum_out=ss4), reads=['xsa'], writes=[sres4])
                    S.add('act', lambda e, ss4=ss4, rs4=rs4: e.activation(out=rs4, in_=ss4, func=AF.Ln, bias=EPS, scale=1.0), reads=[sres4], writes=[sres4])
                    S.add('act', lambda e, rs4=rs4: e.activation(out=rs4, in_=rs4, func=AF.Exp, scale=-0.5), reads=[sres4], writes=[sres4])
                    S.add('dve', lambda e, rs4=rs4: e.scalar_tensor_tensor(out=hn4f[:], in0=xsa[:], scalar=rs4, in1=gb[0:4, :], op0=ALU.mult, op1=ALU.mult), reads=['xsa', sres4, 'gb'], writes=['hn4f'])
                    pb, pbr = bankB()
                    for c in range(8):
                        S.add('pe', lambda e, c=c, pb=pb: e.transpose(out=pb[:, c * 4:(c + 1) * 4], in_=hn4f[:, c * 128:(c + 1) * 128], identity=ident[0:4, 0:4]), reads=['hn4f', 'ident'], writes=[pbr])
                    S.add('act', lambda e, pb=pb: e.copy(out=h2sT[:], in_=pb[:, 0:32].rearrange("p (c t) -> p c t", c=8)), reads=[pbr], writes=['h2sT'])
                rW = Ring('wu', 3)
                rC = Ring('cab', 2)
                nseg = nqb // 5
                n_ch = int(os.environ.get('N_CH', 22))
                for i in range(n_ch):
                    kw = rW.next()
                    for ab in range(2):
                        col0 = ab * DFF + i * 128
                        S.dma('pool', wu[:, kw, ab, :, :], wup_d[:, col0:col0 + 128].rearrange("(k p) n -> p k n", p=128), writes=[('wu', kw)])
                    if do_s:
                        pfs, pfsr = bankS()
                        for ab in range(2):
                            for k in range(8):
                                S.add('pe', lambda e, k=k, ab=ab, pfs=pfs, kw=kw: e.matmul(pfs[:, ab * 4:(ab + 1) * 4], lhsT=wu[:, kw, ab, k, :], rhs=h2sT[:, k, :], start=(ab == 0 and k == 0), stop=(k == 7), skip_group_check=True),
                                      reads=[('wu', kw), 'h2sT'], writes=[pfsr])
                        for ab in range(2):
                            ct = i + ab * 22
                            S.add('act', lambda e, pfs=pfs, ab=ab, ct=ct: e.copy(out=usT[:, ct, :], in_=pfs[:, ab * 4:(ab + 1) * 4]), reads=[pfsr], writes=[('usT', ct)])
                            stv = stT[:, ct, :].rearrange("p (s j) -> p s j", j=2)
                            S.add('act', lambda e, pfs=pfs, ab=ab, ct=ct: e.activation(out=csb[:, ab, :], in_=pfs[:, ab * 4:(ab + 1) * 4], func=AF.Identity, bias=cvp[:, ct, 3:4], scale=cvp[:, ct, 2:3]), reads=[pfsr, 'cvp'], writes=[('csb', ab)])
                            S.add('dve', lambda e, ab=ab, ct=ct, stv=stv: e.scalar_tensor_tensor(out=csb[:, ab, :], in0=stv[:, :, 1], scalar=cvp[:, ct, 1:2], in1=csb[:, ab, :], op0=ALU.mult, op1=ALU.add), reads=['stT', 'cvp', ('csb', ab)], writes=[('csb', ab)])
                            S.add('dve', lambda e, ab=ab, ct=ct, stv=stv: e.scalar_tensor_tensor(out=csb[:, ab, :], in0=stv[:, :, 0], scalar=cvp[:, ct, 0:1], in1=csb[:, ab, :], op0=ALU.mult, op1=ALU.add), reads=['stT', 'cvp', ('csb', ab)], writes=[('csb', ab)])
                        S.add('act', lambda e: e.activation(out=csb[:, 0, :], in_=csb[:, 0, :], func=AF.Silu), reads=[('csb', 0)], writes=[('csb', 0)])
                        S.add('dve', lambda e, i=i: e.tensor_tensor(out=gsT[:, i, :], in0=csb[:, 0, :], in1=csb[:, 1, :], op=ALU.mult), reads=[('csb', 0), ('csb', 1)], writes=[('gsT', i)])
                    for seg in range(nseg):
                        for (c0, N) in ((126, 257), (381, 259)):
                            cols = seg * 640 + c0
                            kc = rC.next()
                            pu2 = []
                            for ab in range(2):
                                pf, pfr = bankS()
                                pu2.append((pf, pfr))
                                for k in range(8):
                                    S.add('pe', lambda e, k=k, ab=ab, pf=pf, kw=kw, cols=cols, N=N: e.matmul(pf[:, 0:N], lhsT=wu[:, kw, ab, k, :], rhs=h2T[:, k, cols:cols + N], start=(k == 0), stop=(k == 7)),
                                          reads=[('wu', kw)] + [('h2T', jj) for jj in range(seg * 5, seg * 5 + 5)], writes=[pfr])
                                ct = i + ab * 22
                                cc = cab[:, kc, ab, 0:N - 2]
                                cres = ('cab', kc, ab)
                                S.add('act', lambda e, pf=pf, cc=cc, ct=ct, N=N: e.activation(out=cc, in_=pf[:, 2:N], func=AF.Identity, bias=cvp[:, ct, 3:4], scale=cvp[:, ct, 2:3]), reads=[pfr, 'cvp'], writes=[cres])
                                S.add('dve', lambda e, pf=pf, cc=cc, ct=ct, N=N: e.scalar_tensor_tensor(out=cc, in0=pf[:, 1:N - 1], scalar=cvp[:, ct, 1:2], in1=cc, op0=ALU.mult, op1=ALU.add), reads=[pfr, 'cvp', cres], writes=[cres])
                                S.add('dve', lambda e, pf=pf, cc=cc, ct=ct, N=N: e.scalar_tensor_tensor(out=cc, in0=pf[:, 0:N - 2], scalar=cvp[:, ct, 0:1], in1=cc, op0=ALU.mult, op1=ALU.add), reads=[pfr, 'cvp', cres], writes=[cres])
                                if seg == nseg - 1 and c0 == 381:
                                    S.add('act', lambda e, pf=pf, ct=ct, N=N: e.copy(out=ucv[:, ct, :], in_=pf[:, N - 2:N]), reads=[pfr], writes=[('ucv', ct)])
                            sa = sab[:, kc, 0:N - 2]
                            S.add('act', lambda e, sa=sa, kc=kc, N=N: e.activation(out=sa, in_=cab[:, kc, 0, 0:N - 2], func=AF.Silu), reads=[('cab', kc, 0)], writes=[('sab', kc)])
                            o0 = seg * 512 + (c0 + 2 - 128)
                            S.add('pool', lambda e, sa=sa, kc=kc, N=N, i=i, o0=o0: e.tensor_tensor(out=gT[:, i, o0:o0 + N - 2], in0=sa, in1=cab[:, kc, 1, 0:N - 2], op=ALU.mult),
                                  reads=[('sab', kc), ('cab', kc, 1)], writes=[('gT', i)])
                for jj in range(2):
                    ocv = o_conv[jj:jj + 1, :].rearrange("o (t p) -> p (o t)", p=128)
                    for q4 in range(4):
                        S.dma('sp', ocv[:, q4 * 11:(q4 + 1) * 11], ucv[:, q4 * 11:(q4 + 1) * 11, jj],
                              reads=[('ucv', ct) for ct in range(44)], allow_slow_non_contiguous=True)
                if do_s:
                    S.dma('sp', o_suT.rearrange("(t p) c -> p t c", p=128), usT[:], reads=[('usT', ct) for ct in range(44)])
                S.barrier(bar[:])
                for i in range(22):
                    S.dma('pool', wdn[:, i, :], wdown_d[i * 128:(i + 1) * 128, :], writes=['wdn'])
                S.dma('sp', gb[:], final_norm_d.partition_broadcast(128), writes=['gb'])
                if do_s:
                    for n in range(2):
                        pf, pfr = bankS()
                        for i in range(22):
                            S.add('pe', lambda e, i=i, n=n, pf=pf: e.matmul(pf[0:4, 0:512], lhsT=gsT[:, i, :], rhs=wdn[:, i, n * 512:(n + 1) * 512], start=(i == 0), stop=(i == 21)),
                                  reads=['wdn'] + [('gsT', ii) for ii in range(22)], writes=[pfr])
                        S.add('dve', lambda e, n=n, pf=pf: e.tensor_tensor(out=xsa[:, n * 512:(n + 1) * 512], in0=pf[0:4, 0:512], in1=xsa[:, n * 512:(n + 1) * 512], op=ALU.add), reads=[pfr, 'xsa'], writes=['xsa'])
                    k_ = statr.next()
                    ss4 = stat[0:4, 2 * k_:2 * k_ + 1]
                    rs4 = stat[0:4, 2 * k_ + 1:2 * k_ + 2]
                    sres4 = ('stat', k_)
                    S.add('act', lambda e, ss4=ss4: e.activation(out=junk[0:4, :], in_=xsa[:], func=AF.Square, scale=1.0 / 32.0, accum_out=ss4), reads=['xsa'], writes=[sres4])
                    S.add('act', lambda e, ss4=ss4, rs4=rs4: e.activation(out=rs4, in_=ss4, func=AF.Ln, bias=EPS, scale=1.0), reads=[sres4], writes=[sres4])
                    S.add('act', lambda e, rs4=rs4: e.activation(out=rs4, in_=rs4, func=AF.Exp, scale=-0.5), reads=[sres4], writes=[sres4])
                    S.add('dve', lambda e, rs4=rs4: e.scalar_tensor_tensor(out=ys4[:], in0=xsa[:], scalar=rs4, in1=gb[0:4, :], op0=ALU.mult, op1=ALU.add if False else ALU.mult), reads=['xsa', sres4, 'gb'], writes=['ys4'])
                    S.dma('sp', o_ys[:, :], ys4[:], reads=['ys4'])
                ob_i = 0
                for j in range(nqb):
                    if j % 5 == 0:
                        continue
                    r = ob_i % 2
                    S.dma('sp', xr[:, r, :], xp_scr[j * 128:(j + 1) * 128, :], writes=[('xr', r)])
                    for n in range(2):
                        pf, pfr = bankS()
                        for i in range(22):
                            S.add('pe', lambda e, i=i, n=n, pf=pf, ob_i=ob_i: e.matmul(pf[:, 0:512], lhsT=gT[:, i, ob_i * 128:(ob_i + 1) * 128], rhs=wdn[:, i, n * 512:(n + 1) * 512], start=(i == 0), stop=(i == 21)),
                                  reads=['wdn'] + [('gT', ii) for ii in range(22)], writes=[pfr])
                        S.add('dve', lambda e, n=n, pf=pf, r=r: e.tensor_tensor(out=xr[:, r, n * 512:(n + 1) * 512], in0=pf[:, 0:512], in1=xr[:, r, n * 512:(n + 1) * 512], op=ALU.add),
                              reads=[pfr, ('xr', r)], writes=[('xr', r)])
                    k = statr.next()
                    ss = stat[:, 2 * k:2 * k + 1]
                    rs = stat[:, 2 * k + 1:2 * k + 2]
                    sres = ('stat', k)
                    S.add('act', lambda e, ss=ss, r=r: e.activation(out=junk[:], in_=xr[:, r, :], func=AF.Square, scale=1.0 / 32.0, accum_out=ss), reads=[('xr', r)], writes=[sres])
                    S.add('act', lambda e, ss=ss, rs=rs: e.activation(out=rs, in_=ss, func=AF.Ln, bias=EPS, scale=1.0), reads=[sres], writes=[sres])
                    S.add('act', lambda e, rs=rs: e.activation(out=rs, in_=rs, func=AF.Exp, scale=-0.5), reads=[sres], writes=[sres])
                    S.add('dve', lambda e, rs=rs, r=r: e.scalar_tensor_tensor(out=yt[:, r, :], in0=xr[:, r, :], scalar=rs, in1=gb[:], op0=ALU.mult, op1=ALU.mult), reads=[('xr', r), sres, 'gb'], writes=[('yt', r)])
                    S.dma('sp', o_y[ob_i * 128:(ob_i + 1) * 128, :], yt[:, r, :], reads=[('yt', r)])
                    ob_i += 1

        S.emit()
    return nc, S


_CACHE = {}


def _rope_tab(pos):
    half = 32
    inv = (1.0 / (10000.0 ** (np.arange(half, dtype=np.float32) / half))).astype(np.float32)
    ang = pos.astype(np.float32)[:, None] * inv[None, :]
    return np.concatenate([np.cos(ang), np.sin(ang)], axis=1).astype(np.float32)


def _host_consts(h):
    off = 512 * (1 - h)
    pos = np.arange(T) - off
    c = {}
    c["ropekv"] = _rope_tab(np.maximum(pos, 0))
    valid = (pos >= 0).astype(np.float32)
    c["validc"] = np.ascontiguousarray(valid.reshape(NT, 128).T)
    i = np.arange(256)
    cval = (16 * i - off >= 0) & (i < NCMP)
    cm = np.zeros((NQB, 128, 2, 128), np.float32)
    sm = np.zeros((NQB, 128, 128), np.float32)
    blk = np.arange(64)
    bvalid = blk >= (off // 64)
    for j, fb in enumerate(QB):
        q = 128 * fb + np.arange(128)
        for tt in range(2):
            ii = tt * 128 + np.arange(128)
            m = cval[ii][:, None] & ((16 * ii + 31)[:, None] <= q[None, :])
            cm[j, :, tt, :] = m
        ok = (64 * blk[None, :] <= q[:, None]) & bvalid[None, :]
        cur = q // 64
        forced = (blk[None, :] == off // 64) | (blk[None, :] == cur[:, None]) | (blk[None, :] == cur[:, None] - 1)
        okf = ok.astype(np.float32)
        sm[j, :, 0:64] = okf
        sm[j, :, 64:128] = okf * forced * 1.0e4 + (okf - 1.0) * 1.0e30
    c["cmaskd"] = cm
    c["cvalidd"] = np.ascontiguousarray(cval.astype(np.float32).reshape(2, 128).T)
    c["selmd"] = sm
    ii = np.arange(256)
    cs = ii * 16
    ss = blk * 64
    ov = ((cs[:, None] < ss[None, :] + 64) & (cs[:, None] + 32 > ss[None, :])).astype(np.float32)
    ov[NCMP:] = 0
    c["ovd"] = np.ascontiguousarray(ov.reshape(2, 128, 64).transpose(1, 0, 2))
    p = np.arange(128)
    tri = (p[:, None] <= p[None, :]).astype(np.float32)
    triu = (p[:, None] > p[None, :]).astype(np.float32)
    c["trid"] = np.concatenate([tri, triu], axis=1)
    cc = np.arange(T)
    e2 = ((np.arange(128) % 64)[:, None] == (cc // 64)[None, :]).astype(np.float32)
    c["e2d"] = e2
    c["identd"] = np.eye(128, dtype=np.float32)
    return c


def _sample_consts():
    c = {}
    c["ropes"] = np.repeat(_rope_tab(np.array([PAST])), 4, axis=0)
    bm = np.zeros((8, 4, 128), np.float32)
    c0 = np.zeros((8, 4), np.float32)
    c1 = np.zeros((8, 4), np.float32)
    for h in range(4):
        for m in range(2):
            bm[h * 2 + m, h, :] = 1.0
        c0[2 * h, h] = 1.0
        c1[2 * h + 1, h] = 1.0
    c["bm8"] = bm.reshape(8, 512)
    c["c01"] = np.concatenate([c0, c1], axis=1)
    cm = np.ones((128, 8), np.float32)
    cm[127, 7] = 0.0
    c["cmsk"] = cm
    n = np.arange(1024)
    j = np.arange(257)
    ov = ((16 * n[:, None] < 64 * j[None, :] + 64) & (16 * n[:, None] + 32 > 64 * j[None, :])).astype(np.float32)
    ov[1023:] = 0
    c["ovs"] = np.ascontiguousarray(ov.reshape(8, 128, 257).transpose(1, 0, 2))
    wm = np.ones((128, 32), np.float32)
    wm[0, 0:8] = 0.0
    c["winm"] = wm
    c["identf"] = np.eye(128, dtype=np.float32)
    c["iota16"] = np.tile(np.arange(16, dtype=np.float32)[None, :], (128, 1))
    return c


def kernel(**inp):
    f = lambda a: np.ascontiguousarray(np.asarray(a, dtype=np.float32))
    x = f(inp["x_prompt"])
    w_in = f(inp["w_in"])[0]
    permq = np.arange(512).reshape(2, 4, 64).transpose(1, 0, 2).reshape(-1)
    wq = np.ascontiguousarray(np.concatenate([w_in[:, permq], w_in[:, 1304:1816], w_in[:, 1280:1304]], axis=1))
    wkv = np.ascontiguousarray(np.concatenate([w_in[:, 512:1280], w_in[:, 1816:2840]], axis=1))
    lam4 = np.concatenate([f(inp["lambda_q1"])[0], f(inp["lambda_k1"])[0], f(inp["lambda_q2"])[0], f(inp["lambda_k2"])[0]])[None, :]
    convp = np.ascontiguousarray(np.concatenate([f(inp["conv_w"])[0], f(inp["conv_b"])], axis=0))
    shared = {
        "wq": wq, "wkv": wkv,
        "attn_norm": f(inp["attn_norm"]), "ffn_norm": f(inp["ffn_norm"]), "final_norm": f(inp["final_norm"])[None, :],
        "cmp_w1_k": f(inp["cmp_w1_k"])[0], "cmp_w1_v": f(inp["cmp_w1_v"])[0],
        "cmp_w2_k": f(inp["cmp_w2_k"])[0], "cmp_w2_v": f(inp["cmp_w2_v"])[0],
        "cmp_pos_kT": np.ascontiguousarray(f(inp["cmp_pos_k"])[0].T), "cmp_pos_vT": np.ascontiguousarray(f(inp["cmp_pos_v"])[0].T),
        "lam4": np.ascontiguousarray(lam4), "subln_g": f(inp["subln_g"]),
        "w_out": f(inp["w_out"])[0], "w_up": f(inp["w_up"])[0], "w_down": f(inp["w_down"])[0], "convpT": np.ascontiguousarray(convp.T),
    }
    shared.update(_sample_consts())
    shared["cache_cmp_k"] = f(inp["cache_cmp_k"]).reshape(5120, 16384)
    shared["cache_cmp_v"] = f(inp["cache_cmp_v"]).reshape(5120, 16384)
    shared["cache_sel_k"] = f(inp["cache_sel_k"]).reshape(10240, 8192)
    shared["cache_sel_v"] = f(inp["cache_sel_v"]).reshape(10240, 8192)
    shared["cache_diff_k"] = f(inp["cache_diff_k"]).reshape(5120, 65536)
    shared["cache_diff_v"] = f(inp["cache_diff_v"]).reshape(5120, 65536)
    xs = f(inp["x_sample"])[:, 0, :]
    pt = np.ascontiguousarray(np.asarray(inp["page_table"], dtype=np.int32))
    wink = f(inp["cache_win_k"])[0].reshape(32, 512, 128)
    winv = f(inp["cache_win_v"])[0].reshape(32, 512, 128)
    st = f(inp["state_ffn_conv"])[0]
    hc = [_host_consts(0), _host_consts(1)]
    in_maps = []
    for c in range(8):
        b, h = c % 4, c // 4
        m = dict(shared)
        m.update(hc[h])
        if h == 1:
            m["xkv"] = x[b]
        else:
            m["xkv"] = np.ascontiguousarray(np.concatenate([np.zeros((512, D), np.float32), x[b, :3584]], axis=0))
        sl = slice(4 * c, 4 * c + 4)
        m["xs"] = np.ascontiguousarray(xs[sl])
        m["ptc"] = np.ascontiguousarray(pt[sl].T)
        m["ptr"] = np.ascontiguousarray(pt[sl])
        m["win_k"] = np.ascontiguousarray(wink[sl])
        m["win_v"] = np.ascontiguousarray(winv[sl])
        m["stT"] = np.ascontiguousarray(st[sl].transpose(2, 0, 1).reshape(2 * DFF, 8))
        m["stp"] = np.ascontiguousarray(st[sl, 1, :])
        in_maps.append(m)
    if "nc" not in _CACHE:
        _CACHE["nc"] = build_nc()
    nc, _ = _CACHE["nc"]
    res = run_bass_kernel_spmd(nc, in_maps, core_ids=list(range(8)))
    R = res.results
    y_prompt = np.zeros((4, T, D), np.float32)
    for c in range(8):
        b, h = c % 4, c // 4
        oy = np.asarray(R[c]["o_y"])
        for s_ in range(4):
            a0 = 512 * (2 * s_ + h)
            y_prompt[b, a0:a0 + 512] = oy[512 * s_:512 * (s_ + 1)]
    y_sample = np.concatenate([np.asarray(R[c]["o_ys"]) for c in range(8)], axis=0).reshape(32, 1, D)
    okv = np.stack([np.asarray(R[4 + b]["o_kv"]) for b in range(4)], axis=0)
    p_cmp_k = okv[:, :, 0:128].reshape(1, 4, T, 2, 64)
    p_cmp_v = okv[:, :, 128:256].reshape(1, 4, T, 2, 64)
    p_sel_k = okv[:, :, 256:384].reshape(1, 4, T, 2, 64)
    p_sel_v = okv[:, :, 384:512].reshape(1, 4, T, 2, 64)
    p_win_k = okv[:, T - 512:, 512:640].reshape(1, 4, 512, 2, 64)
    p_win_v = okv[:, T - 512:, 640:768].reshape(1, 4, 512, 2, 64)
    p_diff_k = okv[:, :, 768:1280].reshape(1, 4, T, 4, 2, 64)
    p_diff_v = okv[:, :, 1280:1792].reshape(1, 4, T, 4, 128)
    p_conv = np.stack([np.asarray(R[4 + b]["o_conv"]) for b in range(4)], axis=0).reshape(1, 4, 2, 2 * DFF)
    skv = np.concatenate([np.asarray(R[c]["o_skv"]) for c in range(8)], axis=0)
    s_cmp_k = skv[:, 0:128].reshape(1, 32, 1, 2, 64)
    s_cmp_v = skv[:, 128:256].reshape(1, 32, 1, 2, 64)
    s_sel_k = skv[:, 256:384].reshape(1, 32, 1, 2, 64)
    s_sel_v = skv[:, 384:512].reshape(1, 32, 1, 2, 64)
    s_diff_k = skv[:, 768:1280].reshape(1, 32, 1, 4, 2, 64)
    s_diff_v = skv[:, 1280:1792].reshape(1, 32, 1, 4, 128)
    s_win_k = np.concatenate([np.asarray(R[c]["o_swk"]) for c in range(8)], axis=0).reshape(1, 32, 512, 2, 64)
    s_win_v = np.concatenate([np.asarray(R[c]["o_swv"]) for c in range(8)], axis=0).reshape(1, 32, 512, 2, 64)
    sprev = np.concatenate([np.asarray(R[c]["o_sprev"]) for c in range(8)], axis=0)
    su = np.concatenate([np.asarray(R[c]["o_suT"]).T for c in range(8)], axis=0)
    s_conv = np.stack([sprev, su], axis=1).reshape(1, 32, 2, 2 * DFF)
    outs = (y_prompt, y_sample, p_cmp_k, p_cmp_v, p_sel_k, p_sel_v, p_diff_k, p_diff_v, p_win_k, p_win_v, p_conv,
            s_cmp_k, s_cmp_v, s_sel_k, s_sel_v, s_diff_k, s_diff_v, s_win_k, s_win_v, s_conv)
    return tuple(np.ascontiguousarray(o, dtype=np.float32) for o in outs)
```

```python
import contextlib
import math
import os
import numpy as np
import concourse.bass as bass
import concourse.mybir as mybir
from concourse.bass_utils import run_bass_kernel_spmd

F32 = mybir.dt.float32
BF16 = mybir.dt.bfloat16
I32 = mybir.dt.int32
AF = mybir.ActivationFunctionType
ALU = mybir.AluOpType
AX = mybir.AxisListType

D = 1024
T = 4096
NT = 32
QB = [3, 4, 5, 6, 7, 11, 12, 13, 14, 15, 19, 20, 21, 22, 23, 27, 28, 29, 30, 31]
NQB = len(QB)
DFF = 2816
SCALE = 0.125
EPS = 1e-6
LAM_INIT = 0.2
PAST = 16384
NPAGE = 128
KVW = 1792
QW = 1048
NCMP = 255


class Sched:
    def __init__(self, nc, n_dma_sems=8):
        self.nc = nc
        self.ops = []
        self.lw = {}
        self.rd = {}
        self.nds = n_dma_sems
        self.rr = {}
        self.last_on_sem = {}
        self.cnt_on_sem = {}
        self.dma_sem_of = {}
        self.last_eng = {}
        self.barrier_idx = None

    def add(self, eng, fn, reads=(), writes=(), dma=False):
        idx = len(self.ops)
        deps = set()
        if self.barrier_idx is not None:
            deps.add(self.barrier_idx)
        for r in reads:
            w = self.lw.get(r)
            if w is not None:
                deps.add(w)
        for r in writes:
            w = self.lw.get(r)
            if w is not None:
                deps.add(w)
            for x in self.rd.get(r, ()):
                deps.add(x)
        deps.discard(idx)
        for r in reads:
            self.rd.setdefault(r, []).append(idx)
        for r in writes:
            self.lw[r] = idx
            self.rd[r] = []
        if dma:
            k = self.rr.get(eng, 0)
            self.rr[eng] = k + 1
            sid = (eng, k % self.nds)
            if sid in self.last_on_sem:
                deps.add(self.last_on_sem[sid])
            self.last_on_sem[sid] = idx
            self.cnt_on_sem[sid] = self.cnt_on_sem.get(sid, 0) + 1
            self.dma_sem_of[idx] = (sid, 16 * self.cnt_on_sem[sid])
        else:
            self.last_eng[eng] = idx
        self.ops.append(dict(eng=eng, fn=fn, deps=deps, dma=dma))
        return idx

    def dma(self, eng, out, in_, reads=(), writes=(), **kw):
        return self.add(eng, lambda e: e.dma_start(out=out, in_=in_, **kw), reads, writes, dma=True)

    def barrier(self, scratch_ap):
        deps = set(self.last_eng.values()) | set(self.last_on_sem.values())
        idx = self.add('dve', lambda e: e.memset(scratch_ap, 0.0))
        self.ops[idx]['deps'] |= deps
        self.ops[idx]['deps'].discard(idx)
        self.barrier_idx = idx

    def emit(self):
        nc = self.nc
        ops = self.ops
        engs = ['pe', 'act', 'dve', 'pool', 'sp']
        dma_sem_of = self.dma_sem_of
        flagged = set()
        for i, o in enumerate(ops):
            for d in o['deps']:
                if not ops[d]['dma']:
                    if ops[d]['eng'] == 'pe' and o['eng'] == 'pe' and not o['dma']:
                        continue
                    flagged.add(d)
        rank = {}
        cnt = {e: 0 for e in engs}
        for i, o in enumerate(ops):
            if i in flagged:
                cnt[o['eng']] += 1
                rank[i] = cnt[o['eng']]
        self.stats = dict(n_ops=len(ops), flagged=dict(cnt))
        with contextlib.ExitStack() as st:
            psem = {e: st.enter_context(nc.semaphore('p_' + e)) for e in engs}
            dsem = {}
            for sid in self.cnt_on_sem:
                dsem[sid] = st.enter_context(nc.semaphore('d_%s_%d' % sid))
            block = st.enter_context(nc.Block())
            by_eng = {e: [i for i, o in enumerate(ops) if o['eng'] == e] for e in engs}

            def run(ename, eobj):
                seen = {}
                nwait = 0
                for i in by_eng[ename]:
                    o = ops[i]
                    need = {}
                    for d in o['deps']:
                        od = ops[d]
                        if od['dma']:
                            sid, val = dma_sem_of[d]
                            key = ('d', sid)
                            sem = dsem[sid]
                        else:
                            if od['eng'] == 'pe' and ename == 'pe' and not o['dma']:
                                continue
                            key = ('p', od['eng'])
                            sem = psem[od['eng']]
                            val = rank[d]
                        if val > need.get(key, (None, 0))[1]:
                            need[key] = (sem, val)
                    for key, (sem, val) in need.items():
                        if seen.get(key, 0) >= val:
                            continue
                        seen[key] = val
                        eobj.wait_ge(sem, val)
                        nwait += 1
                    ins = o['fn'](eobj)
                    if o['dma']:
                        sid, val = dma_sem_of[i]
                        ins.then_inc(dsem[sid], 16)
                    elif i in flagged:
                        ins.then_inc(psem[ename], 1)
                for sid, c in self.cnt_on_sem.items():
                    if sid[0] == ename:
                        eobj.wait_ge(dsem[sid], 16 * c)
                self.stats['waits_' + ename] = nwait

            @block.tensor
            def _(e):
                run('pe', e)

            @block.scalar
            def _(e):
                run('act', e)

            @block.vector
            def _(e):
                run('dve', e)

            @block.gpsimd
            def _(e):
                run('pool', e)

            @block.sync
            def _(e):
                run('sp', e)


class Ring:
    def __init__(self, name, n):
        self.name, self.n, self.i = name, n, 0

    def next(self):
        k = self.i % self.n
        self.i += 1
        return k


def build_nc(phases=('kv', 'cmp', 'attn', 'ffn', 'sample')):
    nc = bass.Bass("TRN2", target_bir_lowering=False)

    def din(name, shape, dt=F32):
        return nc.dram_tensor(name, list(shape), dt, kind="ExternalInput").ap()

    def dout(name, shape, dt=F32):
        return nc.dram_tensor(name, list(shape), dt, kind="ExternalOutput").ap()

    xkv = din("xkv", [T, D])
    ropekv = din("ropekv", [T, 64])
    validc = din("validc", [128, NT])
    cmaskd = din("cmaskd", [NQB, 128, 2, 128])
    selmd = din("selmd", [NQB, 128, 128])
    ovd = din("ovd", [128, 2, 64])
    trid = din("trid", [128, 256])
    e2d = din("e2d", [128, T])
    identd = din("identd", [128, 128])
    wq_d = din("wq", [D, QW])
    wkv_d = din("wkv", [D, KVW])
    attn_norm_d = din("attn_norm", [1, D])
    ffn_norm_d = din("ffn_norm", [1, D])
    final_norm_d = din("final_norm", [1, D])
    w1k_d = din("cmp_w1_k", [2048, 128])
    w1v_d = din("cmp_w1_v", [2048, 128])
    w2k_d = din("cmp_w2_k", [128, 64])
    w2v_d = din("cmp_w2_v", [128, 64])
    posk_d = din("cmp_pos_kT", [64, 32])
    posv_d = din("cmp_pos_vT", [64, 32])
    lam_d = din("lam4", [1, 256])
    subln_d = din("subln_g", [1, 128])
    wout_d = din("w_out", [D, D])
    wup_d = din("w_up", [D, 2 * DFF])
    wdown_d = din("w_down", [DFF, D])
    convpT_d = din("convpT", [2 * DFF, 4])
    cvalidd = din("cvalidd", [128, 2])

    xs_d = din("xs", [4, D])
    ropes_d = din("ropes", [4, 64])
    ptc_d = din("ptc", [128, 4], I32)
    ptr_d = din("ptr", [4, 128], I32)
    cck_d = din("cache_cmp_k", [5120, 16384])
    ccv_d = din("cache_cmp_v", [5120, 16384])
    csk_d = din("cache_sel_k", [10240, 8192])
    csv_d = din("cache_sel_v", [10240, 8192])
    cdk_d = din("cache_diff_k", [5120, 65536])
    cdv_d = din("cache_diff_v", [5120, 65536])
    wink_d = din("win_k", [4, 512, 128])
    winv_d = din("win_v", [4, 512, 128])
    stT_d = din("stT", [2 * DFF, 8])
    stp_d = din("stp", [4, 2 * DFF])
    bm8_d = din("bm8", [8, 512])
    c01_d = din("c01", [8, 8])
    cmsk_d = din("cmsk", [128, 8])
    ovs_d = din("ovs", [128, 8, 257])
    winm_d = din("winm", [128, 32])
    identf_d = din("identf", [128, 128])
    iota16_d = din("iota16", [128, 16])

    o_kv = dout("o_kv", [T, KVW])
    o_skv = dout("o_skv", [4, KVW])
    o_swk = dout("o_swk", [4, 512, 128])
    o_swv = dout("o_swv", [4, 512, 128])
    o_suT = dout("o_suT", [2 * DFF, 4])
    o_sprev = dout("o_sprev", [4, 2 * DFF])
    o_ys = dout("o_ys", [4, D])
    zs_scr = nc.dram_tensor("zs_scr", [4, 3400], F32).ap()
    o_y = dout("o_y", [2048, D])
    o_conv = dout("o_conv", [2, 2 * DFF])
    xp_scr = nc.dram_tensor("xp_scr", [NQB * 128, D], F32).ap()

    S = Sched(nc)
    est = contextlib.ExitStack()
    with est:
        _cnt = [0]

        def sb(name, shape, dt, stack=est):
            _cnt[0] += 1
            return stack.enter_context(nc.sbuf_tensor("s%d_%s" % (_cnt[0], name), list(shape), dt))

        PF = [est.enter_context(nc.psum_tensor("pf%d" % i, [128, 512], F32)) for i in range(6)]
        PB = [est.enter_context(nc.psum_tensor("pb%d" % i, [128, 1024], BF16)) for i in range(2)]
        rS = Ring('S', 3)
        rA = Ring('A', 3)
        rB = Ring('B', 2)

        def bankS():
            k = rS.next()
            return PF[k], ('PF', k)

        def bankA():
            k = 3 + rA.next()
            return PF[k], ('PF', k)

        def bankB():
            k = rB.next()
            return PB[k], ('PB', k)

        ident = sb("ident", [128, 128], BF16)
        trim = sb("trim", [128, 256], BF16)
        gb = sb("gb", [128, D], F32)
        junk = sb("junk", [128, D], BF16)
        stat = sb("stat", [128, 16], F32)
        bar = sb("bar", [128, 1], F32)
        identf = sb("identf_sb", [128, 128], F32)
        xsa = sb("xsa", [4, D], F32)
        lamv = sb("lamv", [128, 4], F32)
        sgb = sb("sgb", [128, 128], F32)
        S.dma('sp', identf[:], identf_d, writes=['identf'])
        S.dma('pool', ident[:], identd, writes=['ident'])
        S.dma('pool', trim[:], trid, writes=['trim'])
        S.dma('sp', gb[:], attn_norm_d.partition_broadcast(128), writes=['gb'])

        statr = Ring('stat', 8)

        def norm_T(x_ap, x_res, hT_ap, hT_res, hn_ap, hn_res):
            k = statr.next()
            ss = stat[:, 2 * k:2 * k + 1]
            rs = stat[:, 2 * k + 1:2 * k + 2]
            sres = ('stat', k)
            S.add('act', lambda e: e.activation(out=junk[:], in_=x_ap, func=AF.Square, scale=1.0 / 32.0, accum_out=ss),
                  reads=[x_res], writes=[sres])
            S.add('act', lambda e: e.activation(out=rs, in_=ss, func=AF.Ln, bias=EPS, scale=1.0), reads=[sres], writes=[sres])
            S.add('act', lambda e: e.activation(out=rs, in_=rs, func=AF.Exp, scale=-0.5), reads=[sres], writes=[sres])
            S.add('dve', lambda e: e.scalar_tensor_tensor(out=hn_ap, in0=x_ap, scalar=rs, in1=gb[:], op0=ALU.mult, op1=ALU.mult),
                  reads=[x_res, sres, 'gb'], writes=[hn_res])
            pb, pbr = bankB()
            for c in range(8):
                S.add('pe', lambda e, c=c: e.transpose(out=pb[:, c * 128:(c + 1) * 128], in_=hn_ap[:, c * 128:(c + 1) * 128], identity=ident[:]),
                      reads=[hn_res, 'ident'], writes=[pbr])
            S.add('act', lambda e: e.copy(out=hT_ap, in_=pb[:].rearrange("p (c t) -> p c t", c=8)), reads=[pbr], writes=[hT_res])

        ropetmp = sb("ropetmp", [128, 1, 4, 256], F32)
        rtr = Ring('rt', 1)

        def rope(src, src_res, dst, dst_res, cs, cs_res, nh, P=128):
            k = rtr.next()
            tr = ('ropetmp', k)
            t = [ropetmp[0:P, k, i, 0:nh * 32].rearrange("p (h d) -> p h d", h=nh) for i in range(4)]
            cosb = cs[:, 0:32].unsqueeze(1).to_broadcast([P, nh, 32])
            sinb = cs[:, 32:64].unsqueeze(1).to_broadcast([P, nh, 32])
            x1 = src[:, :, 0:32]
            x2 = src[:, :, 32:64]
            S.add('dve', lambda e: e.tensor_tensor(out=t[0], in0=x1, in1=cosb, op=ALU.mult), reads=[src_res, cs_res], writes=[(tr, 0)])
            S.add('pool', lambda e: e.tensor_tensor(out=t[1], in0=x2, in1=sinb, op=ALU.mult), reads=[src_res, cs_res], writes=[(tr, 1)])
            S.add('dve', lambda e: e.tensor_tensor(out=t[2], in0=x2, in1=cosb, op=ALU.mult), reads=[src_res, cs_res], writes=[(tr, 2)])
            S.add('pool', lambda e: e.tensor_tensor(out=t[3], in0=x1, in1=sinb, op=ALU.mult), reads=[src_res, cs_res], writes=[(tr, 3)])
            S.add('dve', lambda e: e.tensor_tensor(out=dst[:, :, 0:32], in0=t[0], in1=t[1], op=ALU.subtract),
                  reads=[(tr, 0), (tr, 1), src_res], writes=[dst_res])
            S.add('pool', lambda e: e.tensor_tensor(out=dst[:, :, 32:64], in0=t[2], in1=t[3], op=ALU.add),
                  reads=[(tr, 2), (tr, 3), src_res], writes=[dst_res])

        with contextlib.ExitStack() as lst:
            lam = sb("lam", [128, 256], F32, lst)
            lamp = sb("lamp", [128, 128], F32, lst)
            S.dma('sp', lam[:], lam_d.partition_broadcast(128), writes=['lam'])
            S.dma('sp', sgb[:], subln_d.partition_broadcast(128), writes=['sgb'])
            lam4v = lam[:].rearrange("p (a b d) -> p a b d", a=2, b=2)
            S.add('dve', lambda e: e.tensor_tensor(out=lamp[:].rearrange("p (a d) -> p a d", a=2), in0=lam4v[:, :, 0, :], in1=lam4v[:, :, 1, :], op=ALU.mult), reads=['lam'], writes=['lamp'])
            S.add('dve', lambda e: e.reduce_sum(out=lamv[:, 0:2], in_=lamp[:].rearrange("p (a d) -> p a d", a=2), axis=AX.X), reads=['lamp'], writes=['lamv'])
            S.add('act', lambda e: e.activation(out=lamv[:, 0:2], in_=lamv[:, 0:2], func=AF.Exp), reads=['lamv'], writes=['lamv'])
            S.add('dve', lambda e: e.tensor_tensor(out=lamv[:, 2:3], in0=lamv[:, 1:2], in1=lamv[:, 0:1], op=ALU.subtract), reads=['lamv'], writes=['lamv'])
            S.add('dve', lambda e: e.tensor_scalar_add(out=lamv[:, 2:3], in0=lamv[:, 2:3], scalar1=-LAM_INIT), reads=['lamv'], writes=['lamv'])
            S.add('act', lambda e: e.mul(out=sgb[:], in_=sgb[:], mul=1.0 - LAM_INIT), reads=['sgb'], writes=['sgb'])
            S.barrier(bar[:])

        kvst = contextlib.ExitStack()
        kT4 = sb("kT4", [128, 4, T], BF16, kvst)
        dkT = sb("dkT", [128, 4, T], BF16, kvst)
        svx = sb("svx", [128, NT, 2, 65], BF16, kvst)
        wvx = sb("wvx", [128, NT, 2, 65], BF16, kvst)
        dvx = sb("dvx", [128, NT, 4, 129], BF16, kvst)
        kcT = sb("kcT", [128, 256], BF16, kvst)
        vcx = sb("vcx", [128, 2, 2, 129], BF16, kvst)
        wq = sb("wq_sb", [128, 8, QW], BF16, kvst)
        vld = sb("vld", [128, NT], F32, kvst)
        S.dma('sp', vld[:], validc, writes=['vld'])
        for k in range(8):
            S.dma('pool', wq[:, k, :], wq_d[k * 128:(k + 1) * 128, :], writes=['wq'])
        S.add('pool', lambda e: e.tensor_copy(out=svx[:, :, :, 64], in_=vld[:].unsqueeze(2).to_broadcast([128, NT, 2])), reads=['vld'], writes=['svx_v'])
        S.add('pool', lambda e: e.tensor_copy(out=wvx[:, :, :, 64], in_=vld[:].unsqueeze(2).to_broadcast([128, NT, 2])), reads=['vld'], writes=['wvx_v'])
        S.add('pool', lambda e: e.tensor_copy(out=dvx[:, :, :, 128], in_=vld[:].unsqueeze(2).to_broadcast([128, NT, 4])), reads=['vld'], writes=['dvx_v'])

        if 'kv' in phases:
            with contextlib.ExitStack() as pst:
                wkv = sb("wkv_sb", [128, 8, KVW], BF16, pst)
                for k in range(8):
                    S.dma('pool', wkv[:, k, :], wkv_d[k * 128:(k + 1) * 128, :], writes=['wkv'])
                if 'sample' in phases:
                  with contextlib.ExitStack() as s0st:
                      xs4 = sb("xs4", [4, D], F32, s0st)
                      hn4 = sb("hn4", [4, D], BF16, s0st)
                      hsT = sb("hsT", [128, 8, 4], BF16, s0st)
                      zc = sb("zc", [4, 2, 512], F32, s0st)
                      rps = sb("rps", [4, 64], F32, s0st)
                      S.dma('sp', xs4[:], xs_d, writes=['xs4'])
                      S.dma('sp', rps[:], ropes_d, writes=['rps'])
                      k_ = statr.next()
                      ss4 = stat[0:4, 2 * k_:2 * k_ + 1]
                      rs4 = stat[0:4, 2 * k_ + 1:2 * k_ + 2]
                      sres4 = ('stat', k_)
                      S.add('act', lambda e, ss4=ss4: e.activation(out=junk[0:4, :], in_=xs4[:], func=AF.Square, scale=1.0 / 32.0, accum_out=ss4), reads=['xs4'], writes=[sres4])
                      S.add('act', lambda e, ss4=ss4, rs4=rs4: e.activation(out=rs4, in_=ss4, func=AF.Ln, bias=EPS, scale=1.0), reads=[sres4], writes=[sres4])
                      S.add('act', lambda e, rs4=rs4: e.activation(out=rs4, in_=rs4, func=AF.Exp, scale=-0.5), reads=[sres4], writes=[sres4])
                      S.add('dve', lambda e, rs4=rs4: e.scalar_tensor_tensor(out=hn4[:], in0=xs4[:], scalar=rs4, in1=gb[0:4, :], op0=ALU.mult, op1=ALU.mult), reads=['xs4', sres4, 'gb'], writes=['hn4'])
                      S.add('pool', lambda e: e.tensor_copy(out=xsa[:], in_=xs4[:]), reads=['xs4'], writes=['xsa'])
                      pb, pbr = bankB()
                      for c in range(8):
                          S.add('pe', lambda e, c=c, pb=pb: e.transpose(out=pb[:, c * 4:(c + 1) * 4], in_=hn4[:, c * 128:(c + 1) * 128], identity=ident[0:4, 0:4]),
                                reads=['hn4', 'ident'], writes=[pbr])
                      S.add('act', lambda e, pb=pb: e.copy(out=hsT[:], in_=pb[:, 0:32].rearrange("p (c t) -> p c t", c=8)), reads=[pbr], writes=['hsT'])
                      zi = 0
                      for (W, wres, c0, cw, kind) in ([(wq, 'wq', 0, 512, 'q'), (wq, 'wq', 512, 512, 'dq'), (wq, 'wq', 1024, 24, 'gate')]
                                                      + [(wkv, 'wkv', c * 512, min(512, KVW - c * 512), 'kv%d' % c) for c in range(4)]):
                          pf, pfr = bankS()
                          for k in range(8):
                              S.add('pe', lambda e, k=k, pf=pf, W=W, c0=c0, cw=cw: e.matmul(pf[0:4, 0:cw], lhsT=hsT[:, k, :], rhs=W[:, k, c0:c0 + cw], start=(k == 0), stop=(k == 7)),
                                    reads=['hsT', wres], writes=[pfr])
                          zr_ = zi % 2
                          zi += 1
                          zres = ('zc', zr_)
                          zz = zc[:, zr_, :]
                          if kind == 'gate':
                              S.add('act', lambda e, pf=pf, zz=zz: e.activation(out=zz[:, 0:24], in_=pf[0:4, 0:24], func=AF.Exp, scale=-1.0), reads=[pfr], writes=[zres])
                              S.add('dve', lambda e, zz=zz: e.tensor_scalar_add(out=zz[:, 0:24], in0=zz[:, 0:24], scalar1=1.0), reads=[zres], writes=[zres])
                              S.add('dve', lambda e, zz=zz: e.reciprocal(out=zz[:, 0:24], in_=zz[:, 0:24]), reads=[zres], writes=[zres])
                              S.dma('sp', zs_scr[:, 1536:1560], zz[:, 0:24], reads=[zres], writes=['zs'])
                              continue
                          S.add('act', lambda e, pf=pf, zz=zz, cw=cw: e.copy(out=zz[:, 0:cw], in_=pf[0:4, 0:cw]), reads=[pfr], writes=[zres])

                          def rps_(a, b, nh, zz=zz, zres=zres):
                              v = zz[:, a:b].rearrange("p (h d) -> p h d", h=nh)
                              rope(v, zres, v, zres, rps[:], 'rps', nh, P=4)
                          if kind == 'q':
                              S.dma('sp', zs_scr[:, 0:512], zz[:, 0:512], reads=[zres], writes=['zs'])
                              rps_(0, 512, 8)
                              S.dma('sp', zs_scr[:, 512:1024], zz[:, 0:512], reads=[zres], writes=['zs'])
                          elif kind == 'dq':
                              rps_(0, 512, 8)
                              S.dma('sp', zs_scr[:, 1024:1536], zz[:, 0:512], reads=[zres], writes=['zs'])
                          else:
                              c = int(kind[2])
                              if c == 0:
                                  rps_(256, 384, 2)
                              elif c == 1:
                                  rps_(0, 128, 2)
                                  rps_(256, 512, 4)
                              elif c == 2:
                                  rps_(0, 256, 4)
                              S.dma('sp', zs_scr[:, 1600 + c0:1600 + c0 + cw], zz[:, 0:cw], reads=[zres], writes=['zs'])
                              S.dma('sp', o_skv[:, c0:c0 + cw], zz[:, 0:cw], reads=[zres])
                  S.barrier(bar[:])
                xk = sb("xk", [128, 2, D], F32, pst)
                hnk = sb("hnk", [128, 1, D], BF16, pst)
                hTk = sb("hTk", [128, 2, 8, 128], BF16, pst)
                zst = sb("zst", [128, 2, KVW], F32, pst)
                zb = sb("zb", [128, 2, 1024], BF16, pst)
                csk = sb("csk", [128, 2, 64], F32, pst)
                kt_pending = []
                for t in range(int(os.environ.get('KV_TILES', NT))):
                    r = t % 2
                    S.dma('sp', xk[:, r, :], xkv[t * 128:(t + 1) * 128, :], writes=[('xk', r)])
                    S.dma('sp', csk[:, r, :], ropekv[t * 128:(t + 1) * 128, :], writes=[('csk', r)])
                    norm_T(xk[:, r, :], ('xk', r), hTk[:, r], ('hTk', r), hnk[:, 0, :], ('hnk', 0))
                    zr = ('zst', r)
                    z = zst[:, r, :]
                    for c in range(4):
                        c0 = c * 512
                        cw = min(512, KVW - c0)
                        pf, pfr = bankS()
                        for k in range(8):
                            S.add('pe', lambda e, k=k, pf=pf, c0=c0, cw=cw, r=r: e.matmul(pf[:, 0:cw], lhsT=hTk[:, r, k, :], rhs=wkv[:, k, c0:c0 + cw], start=(k == 0), stop=(k == 7)),
                                  reads=[('hTk', r), 'wkv'], writes=[pfr])
                        zc = (zr, c)
                        S.add('act', lambda e, pf=pf, z=z, c0=c0, cw=cw: e.copy(out=z[:, c0:c0 + cw], in_=pf[:, 0:cw]), reads=[pfr], writes=[zc])

                        def rp(a, b, nh, z=z, zc=zc, r=r):
                            v = z[:, a:b].rearrange("p (h d) -> p h d", h=nh)
                            rope(v, zc, v, zc, csk[:, r, :], ('csk', r), nh)
                        if c == 0:
                            rp(256, 384, 2)
                        elif c == 1:
                            rp(512, 640, 2)
                            rp(768, 1024, 4)
                        elif c == 2:
                            rp(1024, 1280, 4)
                    zall = [(zr, c) for c in range(4)]
                    if kt_pending:
                        kt_pending.pop(0)()
                    S.dma('sp', o_kv[t * 128:(t + 1) * 128, :], z, reads=zall)
                    zbr = ('zb', r)
                    S.add('pool', lambda e, z=z, r=r: e.tensor_copy(out=zb[:, r, 0:384], in_=z[:, 0:384]), reads=zall, writes=[(zbr, 0)])
                    S.add('pool', lambda e, z=z, r=r: e.tensor_copy(out=zb[:, r, 384:512], in_=z[:, 512:640]), reads=zall, writes=[(zbr, 1)])
                    S.add('pool', lambda e, z=z, r=r: e.tensor_copy(out=zb[:, r, 512:1024], in_=z[:, 768:1280]), reads=zall, writes=[(zbr, 2)])
                    S.add('pool', lambda e, t=t, z=z: e.tensor_copy(out=svx[:, t, :, 0:64], in_=z[:, 384:512].rearrange("p (g d) -> p g d", g=2)), reads=zall, writes=[('svx', t)])
                    S.add('pool', lambda e, t=t, z=z: e.tensor_copy(out=wvx[:, t, :, 0:64], in_=z[:, 640:768].rearrange("p (g d) -> p g d", g=2)), reads=zall, writes=[('wvx', t)])
                    S.add('pool', lambda e, t=t, z=z: e.tensor_copy(out=dvx[:, t, :, 0:128], in_=z[:, 1280:1792].rearrange("p (h d) -> p h d", h=4)), reads=zall, writes=[('dvx', t)])
                    def kt_emit(t=t, zbr=zbr, r=r):
                        pb, pbr = bankB()
                        for c in range(8):
                            S.add('pe', lambda e, c=c, pb=pb, r=r: e.transpose(out=pb[:, c * 128:(c + 1) * 128], in_=zb[:, r, c * 128:(c + 1) * 128], identity=ident[:]),
                                  reads=[(zbr, 0), (zbr, 1), (zbr, 2), 'ident'], writes=[pbr])
                        S.add('act', lambda e, t=t, pb=pb: e.copy(out=kT4[:, :, t * 128:(t + 1) * 128], in_=pb[:, 0:512].rearrange("p (c t) -> p c t", c=4)),
                              reads=[pbr], writes=[('kT4', t)])
                        S.add('act', lambda e, t=t, pb=pb: e.copy(out=dkT[:, :, t * 128:(t + 1) * 128], in_=pb[:, 512:1024].rearrange("p (c t) -> p c t", c=4)),
                              reads=[pbr], writes=[('dkT', t)])
                    kt_pending.append(kt_emit)
                if kt_pending:
                    kt_pending.pop(0)()
            S.barrier(bar[:])

        if 'cmp' in phases:
            with contextlib.ExitStack() as pst:
                w1s = sb("w1s", [128, 2, 32, 128], BF16, pst)
                posT = sb("posT", [128, 2, 32], BF16, pst)
                w2p = sb("w2p", [128, 2, 128], BF16, pst)
                w2v = sb("w2v", [128, 64], BF16, pst)
                posb = sb("posb", [128, 2], F32, pst)
                xh = sb("xh", [128, 2, 256], F32, pst)
                gtmp = sb("gtmp", [128, 2, 256], F32, pst)
                gl = sb("gl", [128, 2, 256], BF16, pst)
                cvl = sb("cvl", [128, 2], F32, pst)
                ovs = sb("ovs", [128, 2, 64], F32, pst)
                for X, wd_ in enumerate((w1k_d, w1v_d)):
                    for half in range(2):
                        S.dma('pool', w1s[half * 64:(half + 1) * 64, X, :, :], wd_.rearrange("(s d) h -> d s h", d=64), writes=['w1s'])
                S.dma('pool', posT[0:64, 0, :], posk_d, writes=['posT'])
                S.dma('pool', posT[0:64, 1, :], posv_d, writes=['posT'])
                S.add('pool', lambda e: e.memset(w2p[:], 0.0), writes=['w2p0'])
                S.dma('pool', w2p[:, 0, 0:64], w2k_d, reads=['w2p0'], writes=['w2p'])
                S.dma('pool', w2p[:, 1, 64:128], w2k_d, reads=['w2p0'], writes=['w2p'])
                S.dma('pool', w2v[:], w2v_d, writes=['w2v'])
                S.dma('sp', cvl[:], cvalidd, writes=['cvl'])
                S.dma('sp', ovs[:], ovd, writes=['ovs'])
                S.add('pool', lambda e: e.memset(vcx[:], 0.0), writes=['vcx'])
                S.add('pool', lambda e: e.memset(kcT[:], 0.0), writes=['kcT'])
                S.add('pool', lambda e: e.tensor_copy(out=vcx[:, :, :, 64], in_=cvl[:].unsqueeze(2).to_broadcast([128, 2, 2])), reads=['cvl', 'vcx'], writes=['vcx'])
                for g in range(2):
                    S.add('pool', lambda e, g=g: e.tensor_copy(out=vcx[:, :, g, 65:129], in_=ovs[:]), reads=['ovs', 'vcx'], writes=['vcx'])
                for X in range(2):
                    pfb, pfbr = bankS()
                    for s_ in range(32):
                        S.add('pe', lambda e, s_=s_, X=X, pfb=pfb: e.matmul(pfb[:, 0:1], lhsT=w1s[0:64, X, s_, :], rhs=posT[0:64, X, s_:s_ + 1], start=(s_ == 0), stop=(s_ == 31)),
                              reads=['w1s', 'posT'], writes=[pfbr])
                    S.add('act', lambda e, X=X, pfb=pfb: e.copy(out=posb[:, X:X + 1], in_=pfb[:, 0:1]), reads=[pfbr], writes=[('posb', X)])
                    kvv = kT4[:, X, :].rearrange("p (n s) -> p n s", s=16)
                    for g in range(2):
                        pf, pfr = bankS()
                        for s_ in range(32):
                            S.add('pe', lambda e, s_=s_, X=X, g=g, pf=pf, kvv=kvv: e.matmul(pf[:, 0:NCMP], lhsT=w1s[g * 64:(g + 1) * 64, X, s_, :],
                                                                                   rhs=kvv[g * 64:(g + 1) * 64, (s_ // 16):(s_ // 16) + NCMP, s_ % 16],
                                                                                   start=(s_ == 0), stop=(s_ == 31)),
                                  reads=['w1s'] + [('kT4', t) for t in range(NT)], writes=[pfr])
                        xg = xh[:, g, 0:NCMP]
                        tg = gtmp[:, g, 0:NCMP]
                        S.add('act', lambda e, pf=pf, xg=xg, X=X: e.activation(out=xg, in_=pf[:, 0:NCMP], func=AF.Identity, bias=posb[:, X:X + 1], scale=1.0),
                              reads=[pfr, ('posb', X)], writes=[('xh', g)])
                        S.add('dve', lambda e, xg=xg, tg=tg: e.tensor_tensor(out=tg, in0=xg, in1=xg, op=ALU.mult), reads=[('xh', g)], writes=[('gtmp', g)])
                        S.add('dve', lambda e, tg=tg: e.tensor_scalar(out=tg, in0=tg, scalar1=0.044715, scalar2=1.0, op0=ALU.mult, op1=ALU.add), reads=[('gtmp', g)], writes=[('gtmp', g)])
                        S.add('dve', lambda e, xg=xg, tg=tg: e.tensor_tensor(out=tg, in0=tg, in1=xg, op=ALU.mult), reads=[('gtmp', g), ('xh', g)], writes=[('gtmp', g)])
                        S.add('act', lambda e, tg=tg: e.activation(out=tg, in_=tg, func=AF.Tanh, scale=0.7978845608028654), reads=[('gtmp', g)], writes=[('gtmp', g)])
                        S.add('dve', lambda e, xg=xg, tg=tg: e.scalar_tensor_tensor(out=tg, in0=tg, scalar=1.0, in1=xg, op0=ALU.add, op1=ALU.mult), reads=[('gtmp', g), ('xh', g)], writes=[('gtmp', g)])
                        S.add('dve', lambda e, tg=tg, g=g: e.tensor_scalar_mul(out=gl[:, g, 0:NCMP], in0=tg, scalar1=0.5), reads=[('gtmp', g)], writes=[('gl', g)])
                    if X == 0:
                        pk, pkr = bankS()
                        for g in range(2):
                            S.add('pe', lambda e, g=g, pk=pk: e.matmul(pk[:, 0:NCMP], lhsT=w2p[:, g, :], rhs=gl[:, g, 0:NCMP], start=(g == 0), stop=(g == 1)),
                                  reads=['w2p', ('gl', g)], writes=[pkr])
                        S.add('act', lambda e, pk=pk: e.copy(out=kcT[:, 0:NCMP], in_=pk[:, 0:NCMP]), reads=[pkr, 'kcT'], writes=['kcT'])
                    else:
                        for tt in range(2):
                            rows = 128 if tt == 0 else NCMP - 128
                            pv, pvr = bankS()
                            for g in range(2):
                                S.add('pe', lambda e, g=g, tt=tt, rows=rows, pv=pv: e.matmul(pv[0:rows, g * 64:(g + 1) * 64], lhsT=gl[:, g, tt * 128:tt * 128 + rows], rhs=w2v[:], start=True, stop=True),
                                      reads=['w2v', ('gl', g)], writes=[pvr])
                            S.add('act', lambda e, tt=tt, rows=rows, pv=pv: e.copy(out=vcx[0:rows, tt, :, 0:64], in_=pv[0:rows, 0:128].rearrange("p (g d) -> p g d", g=2)),
                                  reads=[pvr, 'vcx'], writes=['vcx'])
            S.barrier(bar[:])

        if 'attn' in phases:
            with contextlib.ExitStack() as pst:
                wo = kT4[:, 0:2, :].rearrange("p a (k n) -> p (a k) n", k=4)
                e2 = sb("e2", [128, T], BF16, pst)
                xq = sb("xq", [128, 2, D], F32, pst)
                csq = sb("csq", [128, 2, 64], F32, pst)
                cm = sb("cm", [128, 2, 2, 128], BF16, pst)
                slm = sb("slm", [128, 2, 128], F32, pst)
                hnq = sb("hnq", [128, D], BF16, pst)
                hTq = sb("hTq", [128, 8, 128], BF16, pst)
                qf = sb("qf", [128, 2, 512], F32, pst)
                qb = sb("qb", [128, 1536], BF16, pst)
                qT = sb("qT", [128, 12, 128], BF16, pst)
                gsg = sb("gsg", [128, 24], F32, pst)
                pP = sb("pP", [128, 4, 512], BF16, pst)
                pM = sb("pM", [128, 3, 512], BF16, pst)
                osb = sb("osb", [128, 2, 520], F32, pst)
                usb = sb("usb", [128, 512], F32, pst)
                rl = sb("rl", [128, 32], F32, pst)
                cf = sb("cf", [128, 8], F32, pst)
                sc = sb("sc", [128, 2, 64], F32, pst)
                screp = sb("screp", [128, 2, 64], F32, pst)
                t8 = sb("t8", [128, 4, 8], F32, pst)
                sel01 = sb("sel01", [128, 128], F32, pst)
                negb = sb("negb", [128, 128], BF16, pst)
                negT = sb("negT", [128, 128], BF16, pst)
                negTr = sb("negTr", [128, 4, 128], BF16, pst)
                onsa = sb("onsa", [128, 512], F32, pst)
                otmp = sb("otmp", [128, 256], F32, pst)
                od = sb("od", [128, 128], F32, pst)
                ob = sb("ob", [128, D], BF16, pst)
                oT = sb("oT", [128, 8, 128], BF16, pst)
                for k in range(8):
                    S.dma('pool', wo[:, k, :], wout_d[k * 128:(k + 1) * 128, :], writes=['wo'])
                for k in range(4):
                    S.dma('pool', e2[:, k * 1024:(k + 1) * 1024], e2d[:, k * 1024:(k + 1) * 1024], writes=['e2'])
                rP = Ring('pP', 4)
                rM = Ring('pM', 3)
                rO = Ring('osb', 2)

                def exp_tile(pf, pfr, mask_ap=None, mask_res=None, nbc=4, eng='dve'):
                    kp = rP.next()
                    p = pP[:, kp, :]
                    S.add('act', lambda e: e.activation(out=p, in_=pf[:, 0:512], func=AF.Exp, scale=SCALE), reads=[pfr], writes=[('pP', kp)])
                    if mask_ap is None:
                        return p, ('pP', kp)
                    km = rM.next()
                    pm = pM[:, km, :]
                    S.add(eng, lambda e: e.tensor_tensor(out=pm.rearrange("p (r q) -> p r q", r=nbc), in0=p.rearrange("p (r q) -> p r q", r=nbc),
                                                         in1=mask_ap.unsqueeze(1).to_broadcast([128, nbc, 128]), op=ALU.mult),
                          reads=[('pP', kp), mask_res], writes=[('pM', km)])
                    return pm, ('pM', km)

                def evac(pa, par, ncol):
                    ko = rO.next()
                    o = osb[:, ko, 0:ncol]
                    S.add('act', lambda e: e.copy(out=o, in_=pa[:, 0:ncol]), reads=[par], writes=[('osb', ko)])
                    return o, ('osb', ko)

                def pipeline(tiles, emit_qk, emit_pv, depth=2):
                    n = len(tiles)
                    st_ = {}
                    for i in range(min(depth, n)):
                        st_[i] = emit_qk(tiles[i])
                    for i in range(n):
                        if i + depth < n:
                            st_[i + depth] = emit_qk(tiles[i + depth])
                        emit_pv(tiles[i], st_.pop(i))

                rlr = Ring('rl', 8)

                def recip_l(lview, lres, n):
                    k = rlr.next()
                    o = rl[:, k * 4:k * 4 + n]
                    S.add('dve', lambda e: e.tensor_scalar_max(out=o, in0=lview, scalar1=1e-30), reads=[lres], writes=[('rl', k)])
                    S.add('dve', lambda e: e.reciprocal(out=o, in_=o), reads=[('rl', k)], writes=[('rl', k)])
                    return o, ('rl', k)

                nqb = int(os.environ.get('N_QB', NQB))
                for j in range(nqb):
                    fb = QB[j]
                    r = j % 2
                    xr_ = ('xq', r)
                    ACUT = int(os.environ.get('ACUT', 99))
                    S.dma('sp', xq[:, r, :], xkv[fb * 128:(fb + 1) * 128, :], writes=[xr_])
                    S.dma('sp', csq[:, r, :], ropekv[fb * 128:(fb + 1) * 128, :], writes=[('csq', r)])
                    S.dma('pool', cm[:, r], cmaskd[j], writes=[('cm', r)])
                    S.dma('sp', slm[:, r, :], selmd[j], writes=[('slm', r)])
                    norm_T(xq[:, r, :], xr_, hTq[:], 'hTq', hnq[:], 'hnq')
                    for c in range(3):
                        c0 = c * 512
                        cw = min(512, QW - c0)
                        pf, pfr = bankS()
                        for k in range(8):
                            S.add('pe', lambda e, k=k, pf=pf, c0=c0, cw=cw: e.matmul(pf[:, 0:cw], lhsT=hTq[:, k, :], rhs=wq[:, k, c0:c0 + cw], start=(k == 0), stop=(k == 7)),
                                  reads=['hTq', 'wq'], writes=[pfr])
                        if c < 2:
                            S.add('act', lambda e, pf=pf, c=c: e.copy(out=qf[:, c, :], in_=pf[:, 0:512]), reads=[pfr], writes=[('qf', c)])
                            if c == 0:
                                S.add('pool', lambda e: e.tensor_copy(out=qb[:, 0:512], in_=qf[:, 0, :]), reads=[('qf', 0)], writes=[('qb', 0)])
                            rope(qf[:, c, :].rearrange("p (h d) -> p h d", h=8), ('qf', c),
                                 qb[:, 512 * (c + 1):512 * (c + 2)].rearrange("p (h d) -> p h d", h=8), ('qb', c + 1), csq[:, r, :], ('csq', r), 8)
                        else:
                            S.add('act', lambda e, pf=pf: e.activation(out=gsg[:], in_=pf[:, 0:24], func=AF.Exp, scale=-1.0), reads=[pfr], writes=['gsg'])
                            S.add('dve', lambda e: e.tensor_scalar_add(out=gsg[:], in0=gsg[:], scalar1=1.0), reads=['gsg'], writes=['gsg'])
                            S.add('dve', lambda e: e.reciprocal(out=gsg[:], in_=gsg[:]), reads=['gsg'], writes=['gsg'])
                    if ACUT < 2:
                        continue
                    for half in range(2):
                        pb, pbr = bankB()
                        nb = 8 if half == 0 else 4
                        for c in range(nb):
                            cc = half * 8 + c
                            S.add('pe', lambda e, c=c, cc=cc, pb=pb: e.transpose(out=pb[:, c * 128:(c + 1) * 128], in_=qb[:, cc * 128:(cc + 1) * 128], identity=ident[:]),
                                  reads=[('qb', 0), ('qb', 1), ('qb', 2), 'ident'], writes=[pbr])
                        S.add('act', lambda e, half=half, nb=nb, pb=pb: e.copy(out=qT[:, half * 8:half * 8 + nb, :], in_=pb[:, 0:nb * 128].rearrange("p (c t) -> p c t", c=nb)),
                              reads=[pbr], writes=[('qT', half)])
                    qTr = [('qT', 0), ('qT', 1)]
                    gv = gsg[:].rearrange("p (g r b) -> p g r b", g=2, r=4)

                    def combine(o, ores, g, br, first):
                        ov_ = o.rearrange("p (r c) -> p r c", c=65)
                        rcp, rres = recip_l(ov_[:, :, 64], ores, 4)
                        cfa = cf[:, g * 4:(g + 1) * 4]
                        S.add('dve', lambda e: e.tensor_tensor(out=cfa, in0=rcp, in1=gv[:, g, :, br], op=ALU.mult), reads=[rres, 'gsg'], writes=[('cf', g)])
                        dst = onsa[:, g * 256:(g + 1) * 256].rearrange("p (r d) -> p r d", r=4)
                        cfb = cfa.unsqueeze(2).to_broadcast([128, 4, 64])
                        if first:
                            S.add('pool', lambda e: e.tensor_tensor(out=dst, in0=ov_[:, :, 0:64], in1=cfb, op=ALU.mult), reads=[ores, ('cf', g)], writes=[('onsa', g)])
                        else:
                            tv = otmp[:].rearrange("p (r d) -> p r d", r=4)
                            S.add('pool', lambda e: e.tensor_tensor(out=tv, in0=ov_[:, :, 0:64], in1=cfb, op=ALU.mult), reads=[ores, ('cf', g)], writes=['otmp'])
                            S.add('pool', lambda e: e.tensor_tensor(out=dst, in0=dst, in1=tv, op=ALU.add), reads=['otmp', ('onsa', g)], writes=[('onsa', g)])

                    if ACUT < 3:
                        continue
                    pu, pur = bankA()
                    for g in range(2):
                        pa, par = bankA()
                        for tt in range(2):
                            pf, pfr = bankS()
                            S.add('pe', lambda e, g=g, tt=tt, pf=pf: e.matmul(pf[:, 0:512], lhsT=kcT[g * 64:(g + 1) * 64, tt * 128:(tt + 1) * 128], rhs=qT[g * 64:(g + 1) * 64, 0:4, :], start=True, stop=True),
                                  reads=['kcT'] + qTr, writes=[pfr])
                            p, pres = exp_tile(pf, pfr, cm[:, r, tt, :], ('cm', r))
                            for rr in range(4):
                                S.add('pe', lambda e, g=g, tt=tt, rr=rr, pa=pa, p=p: e.matmul(pa[:, rr * 65:(rr + 1) * 65], lhsT=p[:, rr * 128:(rr + 1) * 128], rhs=vcx[:, tt, g, 0:65], start=(tt == 0 and rr == 0), stop=(tt == 1), skip_group_check=True),
                                      reads=[pres, 'vcx'], writes=[par])
                                S.add('pe', lambda e, g=g, tt=tt, rr=rr, pu=pu, p=p: e.matmul(pu[:, g * 256 + rr * 64:g * 256 + (rr + 1) * 64], lhsT=p[:, rr * 128:(rr + 1) * 128], rhs=vcx[:, tt, g, 65:129], start=(g == 0 and tt == 0 and rr == 0), stop=(tt == 1), skip_group_check=True),
                                      reads=[pres, 'vcx'], writes=[pur])
                        o, ores = evac(pa, par, 260)
                        combine(o, ores, g, 0, True)
                        ov_ = o.rearrange("p (r c) -> p r c", c=65)
                        rcp, rres = recip_l(ov_[:, :, 64], ores, 4)
                        if g == 0:
                            rc0, rr0 = rcp, rres
                        else:
                            rc1, rr1 = rcp, rres
                    if ACUT < 4:
                        continue
                    S.add('act', lambda e, pu=pu: e.copy(out=usb[:], in_=pu[:, 0:512]), reads=[pur], writes=['usb'])
                    for g, (rcp, rres) in enumerate(((rc0, rr0), (rc1, rr1))):
                        for rr in range(4):
                            u_ = usb[:, g * 256 + rr * 64:g * 256 + (rr + 1) * 64]
                            if rr == 0:
                                S.add('dve', lambda e, g=g, u_=u_, rcp=rcp: e.tensor_scalar_mul(out=sc[:, g, :], in0=u_, scalar1=rcp[:, 0:1]), reads=['usb', rres], writes=[('sc', g)])
                            else:
                                S.add('dve', lambda e, g=g, u_=u_, rcp=rcp, rr=rr: e.scalar_tensor_tensor(out=sc[:, g, :], in0=u_, scalar=rcp[:, rr:rr + 1], in1=sc[:, g, :], op0=ALU.mult, op1=ALU.add),
                                      reads=['usb', rres, ('sc', g)], writes=[('sc', g)])
                    okm = slm[:, r, 0:64]
                    adm = slm[:, r, 64:128]
                    S.add('dve', lambda e, okm=okm: e.tensor_tensor(out=sc[:], in0=sc[:], in1=okm.unsqueeze(1).to_broadcast([128, 2, 64]), op=ALU.mult), reads=[('sc', 0), ('sc', 1), ('slm', r)], writes=[('sc', 0), ('sc', 1)])
                    S.add('dve', lambda e, adm=adm: e.tensor_tensor(out=sc[:], in0=sc[:], in1=adm.unsqueeze(1).to_broadcast([128, 2, 64]), op=ALU.add), reads=[('sc', 0), ('sc', 1), ('slm', r)], writes=[('sc', 0), ('sc', 1)])
                    for g in range(2):
                        S.add('dve', lambda e, g=g: e.max(out=t8[:, g, :], in_=sc[:, g, :]), reads=[('sc', g)], writes=[('t8', g)])
                        S.add('dve', lambda e, g=g: e.match_replace(out=screp[:, g, :], in_to_replace=t8[:, g, :], in_values=sc[:, g, :], imm_value=-3.0e38), reads=[('sc', g), ('t8', g)], writes=[('screp', g)])
                        S.add('dve', lambda e, g=g: e.max(out=t8[:, 2 + g, :], in_=screp[:, g, :]), reads=[('screp', g)], writes=[('t8b', g)])
                        S.add('dve', lambda e, g=g, okm=okm: e.scalar_tensor_tensor(out=sel01[:, g * 64:(g + 1) * 64], in0=sc[:, g, :], scalar=t8[:, 2 + g, 7:8], in1=okm, op0=ALU.is_ge, op1=ALU.mult),
                              reads=[('sc', g), ('t8b', g), ('slm', r)], writes=[('sel01', g)])
                    S.add('dve', lambda e: e.tensor_scalar(out=negb[:], in0=sel01[:], scalar1=-1.0, scalar2=30000.0, op0=ALU.add, op1=ALU.mult), reads=[('sel01', 0), ('sel01', 1)], writes=['negb'])
                    if ACUT < 5:
                        continue
                    pb, pbr = bankB()
                    S.add('pe', lambda e, pb=pb: e.transpose(out=pb[:, 0:128], in_=negb[:], identity=ident[:]), reads=['negb', 'ident'], writes=[pbr])
                    S.add('act', lambda e, pb=pb: e.copy(out=negT[:], in_=pb[:, 0:128]), reads=[pbr], writes=['negT'])
                    S.add('pool', lambda e: e.tensor_copy(out=negTr[:], in_=negT[:].unsqueeze(1).to_broadcast([128, 4, 128])), reads=['negT'], writes=['negTr'])

                    if ACUT < 6:
                        continue
                    for g in range(2):
                        pa, par = bankA()
                        t_lo = max(0, fb - 4)

                        def qk_w(t, g=g):
                            pf, pfr = bankS()
                            S.add('pe', lambda e, g=g, t=t, pf=pf: e.matmul(pf[:, 0:512], lhsT=kT4[g * 64:(g + 1) * 64, 3, t * 128:(t + 1) * 128], rhs=qT[g * 64:(g + 1) * 64, 4:8, :], start=True, stop=True),
                                  reads=[('kT4', t)] + qTr, writes=[pfr])
                            if t == fb:
                                return exp_tile(pf, pfr, trim[:, 0:128], 'trim', eng='pool')
                            elif t == fb - 4:
                                return exp_tile(pf, pfr, trim[:, 128:256], 'trim', eng='pool')
                            return exp_tile(pf, pfr)

                        def pv_w(t, st_, g=g, pa=pa, par=par, t_lo=t_lo):
                            p, pres = st_
                            for rr in range(4):
                                S.add('pe', lambda e, g=g, t=t, rr=rr, pa=pa, p=p, t_lo=t_lo: e.matmul(pa[:, rr * 65:(rr + 1) * 65], lhsT=p[:, rr * 128:(rr + 1) * 128], rhs=wvx[:, t, g, :], start=(t == t_lo and rr == 0), stop=(t == fb), skip_group_check=True),
                                      reads=[pres, ('wvx', t), 'wvx_v'], writes=[par])
                        pipeline(list(range(t_lo, fb + 1)), qk_w, pv_w)
                        o, ores = evac(pa, par, 260)
                        combine(o, ores, g, 2, False)

                    if ACUT < 7:
                        continue
                    DCUT = int(os.environ.get('DCUT', 99))
                    for hp in range(2):
                        pas = [bankA(), bankA()]
                        def qk_d(t, hp=hp):
                            kp = rP.next()
                            p = pP[:, kp, :]
                            pres = ('pP', kp)
                            for m in range(2):
                                pf, pfr = bankS()
                                for hh in range(2):
                                    h_ = hp * 2 + hh
                                    S.add('pe', lambda e, h_=h_, m=m, hh=hh, t=t, pf=pf: e.matmul(pf[:, hh * 128:(hh + 1) * 128], lhsT=dkT[m * 64:(m + 1) * 64, h_, t * 128:(t + 1) * 128],
                                                                                            rhs=qT[m * 64:(m + 1) * 64, 8 + h_, :], start=True, stop=True),
                                          reads=[('dkT', t)] + qTr, writes=[pfr])
                                S.add('act', lambda e, m=m, pf=pf, p=p: e.activation(out=p[:, m * 256:(m + 1) * 256], in_=pf[:, 0:256], func=AF.Exp, scale=SCALE), reads=[pfr], writes=[pres])
                            if t == fb:
                                km = rM.next()
                                pm = pM[:, km, :]
                                S.add('dve', lambda e, p=p, pm=pm: e.tensor_tensor(out=pm.rearrange("p (r q) -> p r q", r=4), in0=p.rearrange("p (r q) -> p r q", r=4),
                                                                             in1=trim[:, 0:128].unsqueeze(1).to_broadcast([128, 4, 128]), op=ALU.mult),
                                      reads=[pres, 'trim'], writes=[('pM', km)])
                                return pm, ('pM', km)
                            return p, pres

                        def pv_d(t, st_, hp=hp, pas=pas):
                            p, pres = st_
                            for hh in range(2):
                                h_ = hp * 2 + hh
                                pa, par = pas[hh]
                                for m in range(2):
                                    cidx = m * 2 + hh
                                    S.add('pe', lambda e, h_=h_, m=m, cidx=cidx, t=t, pa=pa, p=p: e.matmul(pa[:, m * 129:(m + 1) * 129], lhsT=p[:, cidx * 128:(cidx + 1) * 128], rhs=dvx[:, t, h_, :], start=(t == 0 and m == 0), stop=(t == fb), skip_group_check=True),
                                          reads=[pres, ('dvx', t), 'dvx_v'], writes=[par])
                        pipeline(list(range(fb + 1)), qk_d, pv_d)
                        if DCUT < 3:
                            continue
                        for hh in range(2):
                            h_ = hp * 2 + hh
                            pa, par = pas[hh]
                            o, ores = evac(pa, par, 258)
                            if DCUT < 4:
                                continue
                            ov_ = o.rearrange("p (m c) -> p m c", c=129)
                            rcp, rres = recip_l(ov_[:, :, 128], ores, 2)
                            S.add('dve', lambda e, rcp=rcp: e.tensor_tensor(out=rcp[:, 1:2], in0=rcp[:, 1:2], in1=lamv[:, 2:3], op=ALU.mult), reads=[rres, 'lamv'], writes=[rres])
                            S.add('dve', lambda e, o=o, rcp=rcp: e.tensor_scalar_mul(out=od[:], in0=o[:, 0:128], scalar1=rcp[:, 0:1]), reads=[ores, rres], writes=['od'])
                            S.add('dve', lambda e, o=o, rcp=rcp: e.scalar_tensor_tensor(out=od[:], in0=o[:, 129:257], scalar=rcp[:, 1:2], in1=od[:], op0=ALU.mult, op1=ALU.add), reads=[ores, rres, 'od'], writes=['od'])
                            k = statr.next()
                            ss = stat[:, 2 * k:2 * k + 1]
                            rs = stat[:, 2 * k + 1:2 * k + 2]
                            sres = ('stat', k)
                            S.add('act', lambda e, ss=ss: e.activation(out=junk[:, 0:128], in_=od[:], func=AF.Square, scale=1.0 / math.sqrt(128.0), accum_out=ss), reads=['od'], writes=[sres])
                            S.add('act', lambda e, ss=ss, rs=rs: e.activation(out=rs, in_=ss, func=AF.Ln, bias=EPS, scale=1.0), reads=[sres], writes=[sres])
                            S.add('act', lambda e, rs=rs: e.activation(out=rs, in_=rs, func=AF.Exp, scale=-0.5), reads=[sres], writes=[sres])
                            S.add('dve', lambda e, rs=rs, h_=h_: e.scalar_tensor_tensor(out=ob[:, 512 + h_ * 128:512 + (h_ + 1) * 128], in0=od[:], scalar=rs, in1=sgb[:], op0=ALU.mult, op1=ALU.mult),
                                  reads=['od', sres, 'sgb'], writes=[('ob', 4 + h_)])

                    if ACUT < 8:
                        continue
                    for g in range(2):
                        pa, par = bankA()
                        def qk_s(t, g=g):
                            pf, pfr = bankS()
                            S.add('pe', lambda e, g=g, t=t, pf=pf: e.matmul(pf[:, 0:512], lhsT=kT4[g * 64:(g + 1) * 64, 2, t * 128:(t + 1) * 128], rhs=qT[g * 64:(g + 1) * 64, 4:8, :], start=True, stop=False),
                                  reads=[('kT4', t)] + qTr, writes=[pfr])
                            S.add('pe', lambda e, g=g, t=t, pf=pf: e.matmul(pf[:, 0:512], lhsT=e2[g * 64:(g + 1) * 64, t * 128:(t + 1) * 128], rhs=negTr[g * 64:(g + 1) * 64, :, :], start=False, stop=True),
                                  reads=['e2', 'negTr'], writes=[pfr])
                            if t == fb:
                                return exp_tile(pf, pfr, trim[:, 0:128], 'trim')
                            return exp_tile(pf, pfr)

                        def pv_s(t, st_, g=g, pa=pa, par=par):
                            p, pres = st_
                            for rr in range(4):
                                S.add('pe', lambda e, g=g, t=t, rr=rr, pa=pa, p=p: e.matmul(pa[:, rr * 65:(rr + 1) * 65], lhsT=p[:, rr * 128:(rr + 1) * 128], rhs=svx[:, t, g, :], start=(t == 0 and rr == 0), stop=(t == fb), skip_group_check=True),
                                      reads=[pres, ('svx', t), 'svx_v'], writes=[par])
                        pipeline(list(range(fb + 1)), qk_s, pv_s)
                        o, ores = evac(pa, par, 260)
                        combine(o, ores, g, 1, False)
                    S.add('pool', lambda e: e.tensor_copy(out=ob[:, 0:512], in_=onsa[:]), reads=[('onsa', 0), ('onsa', 1)], writes=[('ob', 0)])

                    if ACUT < 9:
                        continue
                    pb, pbr = bankB()
                    for c in range(8):
                        S.add('pe', lambda e, c=c, pb=pb: e.transpose(out=pb[:, c * 128:(c + 1) * 128], in_=ob[:, c * 128:(c + 1) * 128], identity=ident[:]),
                              reads=[('ob', 0)] + [('ob', 4 + h_) for h_ in range(4)] + ['ident'], writes=[pbr])
                    S.add('act', lambda e, pb=pb: e.copy(out=oT[:], in_=pb[:].rearrange("p (c t) -> p c t", c=8)), reads=[pbr], writes=['oT'])
                    for n in range(2):
                        pf, pfr = bankS()
                        for c in range(8):
                            S.add('pe', lambda e, c=c, n=n, pf=pf: e.matmul(pf[:, 0:512], lhsT=oT[:, c, :], rhs=wo[:, c, n * 512:(n + 1) * 512], start=(c == 0), stop=(c == 7)),
                                  reads=['oT', 'wo'], writes=[pfr])
                        S.add('dve', lambda e, n=n, pf=pf, r=r: e.tensor_tensor(out=xq[:, r, n * 512:(n + 1) * 512], in0=pf[:, 0:512], in1=xq[:, r, n * 512:(n + 1) * 512], op=ALU.add),
                              reads=[pfr, xr_], writes=[xr_])
                    S.dma('sp', xp_scr[j * 128:(j + 1) * 128, :], xq[:, r, :], reads=[xr_], writes=[('xps', j)])
            S.barrier(bar[:])
        kvst.close()

        if 'sample' in phases:
            IOA = bass.IndirectOffsetOnAxis
            with contextlib.ExitStack() as pst:
                wo2 = sb("wo2", [128, 8, D], BF16, pst)
                ptc = sb("ptc", [128, 4], I32, pst)
                onesb = sb("onesb", [128, 1], BF16, pst)
                onesf = sb("onesf", [128, 128], F32, pst)
                bm8 = sb("bm8", [8, 512], F32, pst)
                c01 = sb("c01", [8, 8], F32, pst)
                cmat = sb("cmat", [8, 4], F32, pst)
                T2 = sb("T2", [128, 8, 4], BF16, pst)
                oTd = sb("oTd", [128, 4, 4], BF16, pst)
                qdb = sb("qdb", [128, 512], F32, pst)
                qrb = sb("qrb", [128, 512], F32, pst)
                small = sb("small", [128, 512], F32, pst)
                smallb = sb("smallb", [128, 512], BF16, pst)
                for k in range(8):
                    S.dma('pool', wo2[:, k, :], wout_d[k * 128:(k + 1) * 128, :], writes=['wo2'])
                S.dma('sp', ptc[:], ptc_d, writes=['ptc'])
                io16 = sb("io16", [128, 16], F32, pst)
                ptf = sb("ptf", [128, 8], F32, pst)
                idxf = sb("idxf", [128, 4, 16], F32, pst)
                idxd = sb("idxd", [128, 4, 16], I32, pst)
                idxc = sb("idxc", [128, 4, 4], I32, pst)
                S.dma('sp', io16[:], iota16_d, writes=['io16'])
                S.add('dve', lambda e: e.tensor_copy(out=ptf[:, 0:4], in_=ptc[:]), reads=['ptc'], writes=['ptf'])
                S.add('dve', lambda e: e.tensor_scalar_mul(out=ptf[:, 4:8], in0=ptf[:, 0:4], scalar1=16.0), reads=['ptf'], writes=['ptf'])
                for si_ in range(4):
                    S.add('dve', lambda e, si_=si_: e.tensor_scalar_add(out=idxf[:, si_, :], in0=io16[:], scalar1=ptf[:, 4 + si_:5 + si_]), reads=['ptf', 'io16'], writes=['idxf'])
                S.add('dve', lambda e: e.tensor_copy(out=idxd[:], in_=idxf[:]), reads=['idxf'], writes=['idxd'])
                S.add('dve', lambda e: e.tensor_scalar_mul(out=ptf[:, 4:8], in0=ptf[:, 0:4], scalar1=4.0), reads=['ptf', 'idxf'], writes=['ptf'])
                for si_ in range(4):
                    S.add('dve', lambda e, si_=si_: e.tensor_scalar_add(out=idxf[:, si_, 0:4], in0=io16[:, 0:4], scalar1=ptf[:, 4 + si_:5 + si_]), reads=['ptf', 'io16', 'idxd'], writes=['idxf'])
                S.add('dve', lambda e: e.tensor_copy(out=idxc[:], in_=idxf[:, :, 0:4]), reads=['idxf'], writes=['idxc'])
                cdk_r = cdk_d.rearrange("n (t c) -> (n t) c", c=4096)
                cdv_r = cdv_d.rearrange("n (t c) -> (n t) c", c=4096)
                cck_r = cck_d.rearrange("n (t c) -> (n t) c", c=4096)
                ccv_r = ccv_d.rearrange("n (t c) -> (n t) c", c=4096)
                S.dma('sp', bm8[:], bm8_d, writes=['bm8'])
                S.dma('sp', c01[:], c01_d, writes=['c01'])
                S.add('pool', lambda e: e.memset(onesb[:], 1.0), writes=['onesb'])
                S.add('pool', lambda e: e.memset(T2[:], 0.0), writes=['T2'])
                S.add('pool', lambda e: e.memset(oTd[:], 0.0), writes=['oTd'])
                S.add('pool', lambda e: e.memset(onesf[:], 1.0), writes=['onesf'])
                S.add('dve', lambda e: e.scalar_tensor_tensor(out=cmat[:], in0=c01[:, 4:8], scalar=lamv[0:8, 2:3], in1=c01[:, 0:4], op0=ALU.mult, op1=ALU.add), reads=['c01', 'lamv'], writes=['cmat'])
                nseq = int(os.environ.get('N_SEQ', 4))

                def evac_s(pa, par, P, ncol, dst, dres):
                    S.add('act', lambda e: e.copy(out=dst, in_=pa[0:P, 0:ncol]), reads=[par], writes=[dres])

                for si in range(nseq):
                    with contextlib.ExitStack() as dst_:
                        kch = sb("kch", [128, 3, 4096], F32, dst_)
                        vch = sb("vch", [128, 3, 4096], F32, dst_)
                        vbf = sb("vbf", [128, 3, 4096], BF16, dst_)
                        prod = sb("prod", [128, 4096], F32, dst_)
                        sS = sb("sS", [128, 3, 64], F32, dst_)
                        pS = sb("pS", [128, 3, 64], BF16, dst_)
                        knv = sb("knv", [1, 1024], F32, dst_)
                        od8 = sb("od8", [8, 512], F32, dst_)
                        on8 = sb("on8", [8, 132], F32, dst_)
                        S.dma('sp', qdb[:], zs_scr[si:si + 1, 1024:1536].partition_broadcast(128), reads=['zs'], writes=['qdb'])
                        S.dma('sp', knv[:], zs_scr[si:si + 1, 1600 + 768:1600 + 1792], reads=['zs'], writes=['knv'])
                        pod, podr = bankA()
                        pld, pldr = bankA()
                        ntk = int(os.environ.get('N_TK', 16))
                        first = True
                        for tk in range(ntk):
                            r = tk % 3
                            S.add('pool', lambda e, r=r, tk=tk, si=si: e.indirect_dma_start(out=kch[:, r, :], out_offset=None, in_=cdk_r[:, :],
                                                                                           in_offset=IOA(ap=idxd[:, si, tk:tk + 1], axis=0)),
                                  reads=['idxd'], writes=[('kch', r)], dma=True)
                            S.add('pool', lambda e, r=r, tk=tk, si=si: e.indirect_dma_start(out=vch[:, r, :], out_offset=None, in_=cdv_r[:, :],
                                                                                           in_offset=IOA(ap=idxd[:, si, tk:tk + 1], axis=0)),
                                  reads=['idxd'], writes=[('vch', r)], dma=True)
                            S.add('dve', lambda e, r=r: e.tensor_tensor(out=prod[:].rearrange("p (t c) -> p t c", t=8), in0=kch[:, r, :].rearrange("p (t c) -> p t c", t=8),
                                                                        in1=qdb[:].unsqueeze(1).to_broadcast([128, 8, 512]), op=ALU.mult),
                                  reads=[('kch', r), 'qdb'], writes=['prod'])
                            S.add('dve', lambda e, r=r: e.reduce_sum(out=sS[:, r, :], in_=prod[:].rearrange("p (a d) -> p a d", d=64), axis=AX.X), reads=['prod'], writes=[('sS', r)])
                            S.add('act', lambda e, r=r: e.activation(out=pS[:, r, :], in_=sS[:, r, :], func=AF.Exp, scale=SCALE), reads=[('sS', r)], writes=[('pS', r)])
                            S.add('act', lambda e, r=r: e.copy(out=vbf[:, r, :], in_=vch[:, r, :]), reads=[('vch', r)], writes=[('vbf', r)])
                            for tok in range(8):
                                S.add('pe', lambda e, r=r, tok=tok, first=first, pod=pod: e.matmul(pod[0:8, 0:512], lhsT=pS[:, r, tok * 8:(tok + 1) * 8], rhs=vbf[:, r, tok * 512:(tok + 1) * 512], start=first, stop=False),
                                      reads=[('pS', r), ('vbf', r)], writes=[podr])
                                S.add('pe', lambda e, r=r, tok=tok, first=first, pld=pld: e.matmul(pld[0:8, 0:1], lhsT=pS[:, r, tok * 8:(tok + 1) * 8], rhs=onesb[:, 0:1], start=first, stop=False),
                                      reads=[('pS', r), 'onesb'], writes=[pldr])
                                first = False
                        S.add('dve', lambda e: e.tensor_tensor(out=prod[0:1, 0:512], in0=knv[:, 0:512], in1=qdb[0:1, :], op=ALU.mult), reads=['knv', 'qdb', 'prod'], writes=['prod'])
                        S.add('dve', lambda e: e.reduce_sum(out=sS[0:1, 0, 0:8], in_=prod[0:1, 0:512].rearrange("p (a d) -> p a d", d=64), axis=AX.X), reads=['prod', ('sS', 0)], writes=[('sS', 0)])
                        S.add('act', lambda e: e.activation(out=pS[0:1, 0, 0:8], in_=sS[0:1, 0, 0:8], func=AF.Exp, scale=SCALE), reads=[('sS', 0), ('pS', 0)], writes=[('pS', 0)])
                        S.add('act', lambda e: e.copy(out=vbf[0:1, 0, 0:512], in_=knv[:, 512:1024]), reads=['knv', ('vbf', 0)], writes=[('vbf', 0)])
                        S.add('pe', lambda e, pod=pod: e.matmul(pod[0:8, 0:512], lhsT=pS[0:1, 0, 0:8], rhs=vbf[0:1, 0, 0:512], start=False, stop=True), reads=[('pS', 0), ('vbf', 0)], writes=[podr])
                        S.add('pe', lambda e, pld=pld: e.matmul(pld[0:8, 0:1], lhsT=pS[0:1, 0, 0:8], rhs=onesb[0:1, 0:1], start=False, stop=True), reads=[('pS', 0), 'onesb'], writes=[pldr])
                        evac_s(pod, podr, 8, 512, od8[:], 'od8')
                        evac_s(pld, pldr, 8, 1, on8[:, 128:129], 'ld8')
                        S.add('dve', lambda e: e.tensor_tensor(out=od8[:], in0=od8[:], in1=bm8[:], op=ALU.mult), reads=['od8', 'bm8'], writes=['od8'])
                        S.add('dve', lambda e: e.reduce_sum(out=on8[:, 0:128], in_=od8[:].rearrange("p (h e) -> p e h", h=4), axis=AX.X), reads=['od8'], writes=['on8'])
                        S.add('dve', lambda e: e.tensor_scalar_max(out=on8[:, 128:129], in0=on8[:, 128:129], scalar1=1e-30), reads=['ld8'], writes=['ld8'])
                        S.add('dve', lambda e: e.reciprocal(out=on8[:, 128:129], in_=on8[:, 128:129]), reads=['ld8'], writes=['ld8'])
                        S.add('dve', lambda e: e.tensor_scalar_mul(out=on8[:, 0:128], in0=on8[:, 0:128], scalar1=on8[:, 128:129]), reads=['on8', 'ld8'], writes=['on8'])
                        pf, pfr = bankS()
                        S.add('pe', lambda e, pf=pf: e.matmul(pf[0:4, 0:128], lhsT=cmat[:], rhs=on8[:, 0:128], start=True, stop=True), reads=['cmat', 'on8'], writes=[pfr])
                        od4 = small[0:4, 0:128]
                        evac_s(pf, pfr, 4, 128, od4, 'od4')
                        k_ = statr.next()
                        ss4 = stat[0:4, 2 * k_:2 * k_ + 1]
                        rs4 = stat[0:4, 2 * k_ + 1:2 * k_ + 2]
                        sres4 = ('stat', k_)
                        S.add('act', lambda e, ss4=ss4: e.activation(out=junk[0:4, 0:128], in_=od4, func=AF.Square, scale=1.0 / math.sqrt(128.0), accum_out=ss4), reads=['od4'], writes=[sres4])
                        S.add('act', lambda e, ss4=ss4, rs4=rs4: e.activation(out=rs4, in_=ss4, func=AF.Ln, bias=EPS, scale=1.0), reads=[sres4], writes=[sres4])
                        S.add('act', lambda e, rs4=rs4: e.activation(out=rs4, in_=rs4, func=AF.Exp, scale=-0.5), reads=[sres4], writes=[sres4])
                        odn = smallb[0:4, 0:128]
                        S.add('dve', lambda e, rs4=rs4: e.scalar_tensor_tensor(out=odn, in0=od4, scalar=rs4, in1=sgb[0:4, :], op0=ALU.mult, op1=ALU.mult), reads=['od4', sres4, 'sgb'], writes=['odn'])
                        pb, pbr = bankB()
                        S.add('pe', lambda e, pb=pb: e.transpose(out=pb[:, 0:4], in_=odn, identity=ident[0:4, 0:4]), reads=['odn', 'ident'], writes=[pbr])
                        S.add('act', lambda e, pb=pb, si=si: e.copy(out=oTd[:, :, si], in_=pb[:, 0:4]), reads=[pbr, 'oTd'], writes=['oTd'])
                    S.barrier(bar[:])
                    with contextlib.ExitStack() as wst:
                        wkt = sb("wkt", [128, 4, 128], F32, wst)
                        wvt = sb("wvt", [128, 4, 128], F32, wst)
                        wvx_s = sb("wvx_s", [128, 4, 2, 65], BF16, wst)
                        prw = sb("prw", [128, 4, 512], F32, wst)
                        sW = sb("sW", [128, 32], F32, wst)
                        pW8 = sb("pW8", [128, 4, 2, 8], BF16, wst)
                        winm = sb("winm", [128, 32], F32, wst)
                        nkv = sb("nkv", [1, 1792], F32, wst)
                        nvx = sb("nvx", [1, 2, 2, 65], BF16, wst)
                        pn8 = sb("pn8", [1, 2, 2, 8], BF16, wst)
                        gate8 = sb("gate8", [8, 3], F32, wst)
                        o3 = sb("o3", [8, 3, 65], F32, wst)
                        onsa8 = sb("onsa8", [8, 128], F32, wst)
                        onsab = sb("onsab", [8, 128], BF16, wst)
                        S.dma('sp', wkt[:], wink_d[si].rearrange("(t p) c -> p t c", p=128), writes=['wkt'])
                        S.dma('sp', wvt[:], winv_d[si].rearrange("(t p) c -> p t c", p=128), writes=['wvt'])
                        S.dma('sp', winm[:], winm_d, writes=['winm'])
                        S.dma('sp', qrb[:], zs_scr[si:si + 1, 512:1024].partition_broadcast(128), reads=['zs'], writes=['qrb'])
                        S.dma('sp', nkv[:], zs_scr[si:si + 1, 1600:1600 + 1792], reads=['zs'], writes=['nkv'])
                        S.dma('sp', gate8[:], zs_scr[si, 1536:1560].rearrange("(h b) -> h b", b=3), reads=['zs'], writes=['gate8'])
                        S.dma('sp', o_swk[si, 0:511, :], wink_d[si, 1:512, :])
                        S.dma('sp', o_swv[si, 0:511, :], winv_d[si, 1:512, :])
                        S.dma('sp', o_swk[si, 511:512, :], zs_scr[si:si + 1, 1600 + 512:1600 + 640], reads=['zs'])
                        S.dma('sp', o_swv[si, 511:512, :], zs_scr[si:si + 1, 1600 + 640:1600 + 768], reads=['zs'])
                        qv = qrb[:].rearrange("p (r g d) -> p g r d", r=4, g=2)
                        S.add('pool', lambda e: e.memset(wvx_s[:], 1.0), writes=['wvx_s'])
                        S.add('pool', lambda e: e.tensor_copy(out=wvx_s[:, :, :, 0:64], in_=wvt[:].rearrange("p t (g d) -> p t g d", g=2)), reads=['wvt', 'wvx_s'], writes=['wvx_s'])
                        S.add('pool', lambda e: e.memset(pW8[:], 0.0), writes=['pW8'])
                        S.add('pool', lambda e: e.memset(pn8[:], 0.0), writes=['pn8'])
                        S.add('pool', lambda e: e.memset(nvx[:], 1.0), writes=['nvx'])
                        for t in range(4):
                            S.add('dve', lambda e, t=t: e.tensor_tensor(out=prw[:, t, :].rearrange("p (g r d) -> p g r d", g=2, r=4),
                                                                        in0=wkt[:, t, :].rearrange("p (g d) -> p g d", g=2).unsqueeze(2).to_broadcast([128, 2, 4, 64]),
                                                                        in1=qv, op=ALU.mult), reads=['wkt', 'qrb'], writes=['prw'])
                        S.add('dve', lambda e: e.reduce_sum(out=sW[:], in_=prw[:].rearrange("p t (a d) -> p (t a) d", d=64), axis=AX.X), reads=['prw'], writes=['sW'])
                        S.add('act', lambda e: e.activation(out=sW[:], in_=sW[:], func=AF.Exp, scale=SCALE), reads=['sW'], writes=['sW'])
                        S.add('dve', lambda e: e.tensor_tensor(out=sW[:], in0=sW[:], in1=winm[:], op=ALU.mult), reads=['sW', 'winm'], writes=['sW'])
                        sWv = sW[:].rearrange("p (t g r) -> p t g r", t=4, g=2)
                        for g in range(2):
                            S.add('pool', lambda e, g=g: e.tensor_copy(out=pW8[:, :, g, g * 4:(g + 1) * 4], in_=sWv[:, :, g, :]), reads=['sW', 'pW8'], writes=['pW8'])
                        for bi, (kc0, vc0) in enumerate(((512, 640), (256, 384))):
                            S.add('dve', lambda e, kc0=kc0: e.tensor_tensor(out=prw[0:1, 0, :].rearrange("p (g r d) -> p g r d", g=2, r=4),
                                                                          in0=nkv[:, kc0:kc0 + 128].rearrange("p (g d) -> p g d", g=2).unsqueeze(2).to_broadcast([1, 2, 4, 64]),
                                                                          in1=qv[0:1], op=ALU.mult), reads=['nkv', 'qrb', 'prw', 'sW'], writes=['prw'])
                            sn = small[0:1, 256 + bi * 8:256 + bi * 8 + 8]
                            S.add('dve', lambda e, sn=sn: e.reduce_sum(out=sn, in_=prw[0:1, 0, :].rearrange("p (a d) -> p a d", d=64), axis=AX.X), reads=['prw'], writes=[('sn', bi)])
                            S.add('act', lambda e, sn=sn: e.activation(out=sn, in_=sn, func=AF.Exp, scale=SCALE), reads=[('sn', bi)], writes=[('sn', bi)])
                            for g in range(2):
                                S.add('pool', lambda e, g=g, bi=bi, sn=sn: e.tensor_copy(out=pn8[:, bi, g, g * 4:(g + 1) * 4], in_=sn[:, g * 4:(g + 1) * 4]), reads=[('sn', bi), 'pn8'], writes=['pn8'])
                            S.add('pool', lambda e, bi=bi, vc0=vc0: e.tensor_copy(out=nvx[:, bi, :, 0:64], in_=nkv[:, vc0:vc0 + 128].rearrange("p (g d) -> p g d", g=2)), reads=['nkv', 'nvx'], writes=['nvx'])
                        pw_, pwr = bankA()
                        first = True
                        for t in range(4):
                            for g in range(2):
                                S.add('pe', lambda e, t=t, g=g, first=first, pw_=pw_: e.matmul(pw_[0:8, 0:65], lhsT=pW8[:, t, g, :], rhs=wvx_s[:, t, g, :], start=first, stop=False), reads=['pW8', 'wvx_s'], writes=[pwr])
                                first = False
                        for g in range(2):
                            S.add('pe', lambda e, g=g, pw_=pw_: e.matmul(pw_[0:8, 0:65], lhsT=pn8[:, 0, g, :], rhs=nvx[:, 0, g, :], start=False, stop=(g == 1)), reads=['pn8', 'nvx'], writes=[pwr])
                        evac_s(pw_, pwr, 8, 65, o3[:, 2, :], ('o3', 2))

                        with contextlib.ExitStack() as cst:
                            cTs = sb("cTs", [128, 16384], BF16, cst)
                            pg = sb("pg", [128, 2, 4096], F32, cst)
                            pgb = sb("pgb", [128, 2, 4096], BF16, cst)
                            w1s2 = sb("w1s2", [128, 2, 32, 128], BF16, cst)
                            posT2 = sb("posT2", [128, 2, 32], BF16, cst)
                            w2p2 = sb("w2p2", [128, 2, 128], BF16, cst)
                            w2v2 = sb("w2v2", [128, 64], BF16, cst)
                            posb2 = sb("posb2", [128, 2], F32, cst)
                            xh2 = sb("xh2", [128, 512], F32, cst)
                            gt2 = sb("gt2", [128, 512], F32, cst)
                            gl2 = sb("gl2", [128, 2, 1024], BF16, cst)
                            kcTs = sb("kcTs", [128, 1024], BF16, cst)
                            vcs = sb("vcs", [128, 8, 128], BF16, cst)
                            qsT = sb("qsT", [128, 4], F32, cst)
                            qsTb = sb("qsTb", [128, 4], BF16, cst)
                            cmsk = sb("cmsk", [128, 8], F32, cst)
                            pC = sb("pC", [128, 2, 8, 4], F32, cst)
                            lsum = sb("lsum", [128, 64], F32, cst)
                            l8 = sb("l8", [128, 8], F32, cst)
                            imp = sb("imp", [128, 2, 8], F32, cst)
                            pC8 = sb("pC8", [128, 2, 8, 8], BF16, cst)
                            ovsS = sb("ovs_s", [128, 8, 257], F32, cst)
                            for X, wd_ in enumerate((w1k_d, w1v_d)):
                                for half in range(2):
                                    S.dma('pool', w1s2[half * 64:(half + 1) * 64, X, :, :], wd_.rearrange("(s d) h -> d s h", d=64), writes=['w1s2'])
                            S.dma('pool', posT2[0:64, 0, :], posk_d, writes=['posT2'])
                            S.dma('pool', posT2[0:64, 1, :], posv_d, writes=['posT2'])
                            S.add('pool', lambda e: e.memset(w2p2[:], 0.0), writes=['w2p20'])
                            S.dma('pool', w2p2[:, 0, 0:64], w2k_d, reads=['w2p20'], writes=['w2p2'])
                            S.dma('pool', w2p2[:, 1, 64:128], w2k_d, reads=['w2p20'], writes=['w2p2'])
                            S.dma('pool', w2v2[:], w2v_d, writes=['w2v2'])
                            S.dma('sp', cmsk[:], cmsk_d, writes=['cmsk'])
                            S.dma('sp', ovsS[:], ovs_d, writes=['ovs_s'])
                            S.dma('sp', qsT[:], zs_scr[si, 0:512].rearrange("(r g d) -> (g d) r", r=4, g=2), reads=['zs'], writes=['qsT'], allow_slow_non_contiguous=True)
                            S.add('pool', lambda e: e.tensor_copy(out=qsTb[:], in_=qsT[:]), reads=['qsT'], writes=['qsTb'])
                            S.add('pool', lambda e: e.memset(pC8[:], 0.0), writes=['pC8'])
                            S.add('pool', lambda e: e.memset(gl2[:], 0.0), writes=[('gl2', 0), ('gl2', 1)])
                            cTv = cTs[:].rearrange("p (pg tk) -> p tk pg", tk=128)
                            for X, cache_d in enumerate((cck_r, ccv_r)):
                                for c in range(4):
                                    r = c % 2
                                    S.add('pool', lambda e, r=r, c=c, si=si, cache_d=cache_d: e.indirect_dma_start(out=pg[:, r, :], out_offset=None, in_=cache_d[:, :],
                                                                                                              in_offset=IOA(ap=idxc[:, si, c:c + 1], axis=0)),
                                          reads=['idxc'], writes=[('pg', r)], dma=True)
                                    S.add('act', lambda e, r=r: e.copy(out=pgb[:, r, :], in_=pg[:, r, :]), reads=[('pg', r)], writes=[('pgb', r)])
                                    for b8 in range(4):
                                        pb, pbr = bankB()
                                        for jj in range(8):
                                            tok = b8 * 8 + jj
                                            S.add('pe', lambda e, r=r, tok=tok, jj=jj, pb=pb: e.transpose(out=pb[:, jj * 128:(jj + 1) * 128], in_=pgb[:, r, tok * 128:(tok + 1) * 128], identity=ident[:]),
                                                  reads=[('pgb', r), 'ident'], writes=[pbr])
                                        t0_ = c * 32 + b8 * 8
                                        S.add('act', lambda e, pb=pb, t0_=t0_: e.copy(out=cTv[:, t0_:t0_ + 8, :], in_=pb[:].rearrange("p (j q) -> p j q", j=8)), reads=[pbr], writes=['cTs'])
                                pfb, pfbr = bankS()
                                for s_ in range(32):
                                    S.add('pe', lambda e, s_=s_, X=X, pfb=pfb: e.matmul(pfb[:, 0:1], lhsT=w1s2[0:64, X, s_, :], rhs=posT2[0:64, X, s_:s_ + 1], start=(s_ == 0), stop=(s_ == 31)),
                                          reads=['w1s2', 'posT2'], writes=[pfbr])
                                S.add('act', lambda e, X=X, pfb=pfb: e.copy(out=posb2[:, X:X + 1], in_=pfb[:, 0:1]), reads=[pfbr], writes=[('posb2', X)])
                                kvv = cTs[:].rearrange("p (n s) -> p n s", s=16)
                                for g in range(2):
                                    for (n0, ncl) in ((0, 512), (512, 511)):
                                        pf, pfr = bankS()
                                        for s_ in range(32):
                                            S.add('pe', lambda e, s_=s_, X=X, g=g, pf=pf, n0=n0, ncl=ncl: e.matmul(pf[:, 0:ncl], lhsT=w1s2[g * 64:(g + 1) * 64, X, s_, :],
                                                                                                      rhs=kvv[g * 64:(g + 1) * 64, n0 + (s_ // 16):n0 + (s_ // 16) + ncl, s_ % 16],
                                                                                                      start=(s_ == 0), stop=(s_ == 31)),
                                                  reads=['w1s2', 'cTs'], writes=[pfr])
                                        xg = xh2[:, 0:ncl]
                                        tg = gt2[:, 0:ncl]
                                        S.add('act', lambda e, pf=pf, xg=xg, X=X, ncl=ncl: e.activation(out=xg, in_=pf[:, 0:ncl], func=AF.Identity, bias=posb2[:, X:X + 1], scale=1.0), reads=[pfr, ('posb2', X)], writes=['xh2'])
                                        S.add('dve', lambda e, xg=xg, tg=tg: e.tensor_tensor(out=tg, in0=xg, in1=xg, op=ALU.mult), reads=['xh2'], writes=['gt2'])
                                        S.add('dve', lambda e, tg=tg: e.tensor_scalar(out=tg, in0=tg, scalar1=0.044715, scalar2=1.0, op0=ALU.mult, op1=ALU.add), reads=['gt2'], writes=['gt2'])
                                        S.add('dve', lambda e, xg=xg, tg=tg: e.tensor_tensor(out=tg, in0=tg, in1=xg, op=ALU.mult), reads=['gt2', 'xh2'], writes=['gt2'])
                                        S.add('act', lambda e, tg=tg: e.activation(out=tg, in_=tg, func=AF.Tanh, scale=0.7978845608028654), reads=['gt2'], writes=['gt2'])
                                        S.add('dve', lambda e, xg=xg, tg=tg: e.scalar_tensor_tensor(out=tg, in0=tg, scalar=1.0, in1=xg, op0=ALU.add, op1=ALU.mult), reads=['gt2', 'xh2'], writes=['gt2'])
                                        S.add('dve', lambda e, tg=tg, g=g, n0=n0, ncl=ncl: e.tensor_scalar_mul(out=gl2[:, g, n0:n0 + ncl], in0=tg, scalar1=0.5), reads=['gt2'], writes=[('gl2', g)])
                                if X == 0:
                                    for (n0, ncl) in ((0, 512), (512, 512)):
                                        pk, pkr = bankS()
                                        for g in range(2):
                                            S.add('pe', lambda e, g=g, pk=pk, n0=n0, ncl=ncl: e.matmul(pk[:, 0:ncl], lhsT=w2p2[:, g, :], rhs=gl2[:, g, n0:n0 + ncl], start=(g == 0), stop=(g == 1)),
                                                  reads=['w2p2', ('gl2', g)], writes=[pkr])
                                        S.add('act', lambda e, pk=pk, n0=n0, ncl=ncl: e.copy(out=kcTs[:, n0:n0 + ncl], in_=pk[:, 0:ncl]), reads=[pkr], writes=['kcTs'])
                                else:
                                    for tt in range(8):
                                        pv, pvr = bankS()
                                        for g in range(2):
                                            S.add('pe', lambda e, g=g, tt=tt, pv=pv: e.matmul(pv[:, g * 64:(g + 1) * 64], lhsT=gl2[:, g, tt * 128:(tt + 1) * 128], rhs=w2v2[:], start=True, stop=True),
                                                  reads=['w2v2', ('gl2', g)], writes=[pvr])
                                        S.add('act', lambda e, tt=tt, pv=pv: e.copy(out=vcs[:, tt, :], in_=pv[:, 0:128]), reads=[pvr], writes=['vcs'])
                            psg = []
                            for g in range(2):
                                pf, pfr = bankS()
                                psg.append((pf, pfr))
                                for tt in range(8):
                                    S.add('pe', lambda e, g=g, tt=tt, pf=pf: e.matmul(pf[:, tt * 4:(tt + 1) * 4], lhsT=kcTs[g * 64:(g + 1) * 64, tt * 128:(tt + 1) * 128], rhs=qsTb[g * 64:(g + 1) * 64, :], start=True, stop=True),
                                          reads=['kcTs', 'qsTb'], writes=[pfr])
                            for g in range(2):
                                pf, pfr = psg[g]
                                S.add('act', lambda e, g=g, pf=pf: e.activation(out=pC[:, g].rearrange("p t r -> p (t r)"), in_=pf[:, 0:32], func=AF.Exp, scale=SCALE), reads=[pfr], writes=[('pC', g)])
                            S.add('dve', lambda e: e.tensor_tensor(out=pC[:], in0=pC[:], in1=cmsk[:].unsqueeze(1).unsqueeze(3).to_broadcast([128, 2, 8, 4]), op=ALU.mult), reads=[('pC', 0), ('pC', 1), 'cmsk'], writes=[('pC', 0), ('pC', 1)])
                            pl, plr = bankS()
                            S.add('pe', lambda e, pl=pl: e.matmul(pl[:, 0:64], lhsT=onesf[:], rhs=pC[:].rearrange("p g t r -> p (g t r)"), start=True, stop=True), reads=['onesf', ('pC', 0), ('pC', 1)], writes=[plr])
                            evac_s(pl, plr, 128, 64, lsum[:], 'lsum')
                            S.add('dve', lambda e: e.reduce_sum(out=l8[:].rearrange("p (g r) -> p g r", g=2), in_=lsum[:].rearrange("p (g t r) -> p g r t", g=2, t=8), axis=AX.X), reads=['lsum'], writes=['l8'])
                            S.add('dve', lambda e: e.tensor_scalar_max(out=l8[:], in0=l8[:], scalar1=1e-30), reads=['l8'], writes=['l8'])
                            S.add('dve', lambda e: e.reciprocal(out=l8[:], in_=l8[:]), reads=['l8'], writes=['l8'])
                            S.add('dve', lambda e: e.tensor_tensor(out=pC[:], in0=pC[:], in1=l8[:].rearrange("p (g r) -> p g r", g=2).unsqueeze(2).to_broadcast([128, 2, 8, 4]), op=ALU.mult), reads=[('pC', 0), ('pC', 1), 'l8'], writes=[('pC', 0), ('pC', 1)])
                            S.add('dve', lambda e: e.reduce_sum(out=imp[:], in_=pC[:], axis=AX.X), reads=[('pC', 0), ('pC', 1)], writes=['imp'])
                            for g in range(2):
                                S.add('pool', lambda e, g=g: e.tensor_copy(out=pC8[:, g, :, g * 4:(g + 1) * 4], in_=pC[:, g]), reads=[('pC', g), 'pC8'], writes=['pC8'])
                            psc, pscr = bankS()
                            for tt in range(8):
                                S.add('pe', lambda e, tt=tt, psc=psc: e.matmul(psc[0:2, 0:257], lhsT=imp[:, :, tt], rhs=ovsS[:, tt, :], start=(tt == 0), stop=(tt == 7)), reads=['imp', 'ovs_s'], writes=[pscr])
                            scs = small[0:2, 0:257]
                            evac_s(psc, pscr, 2, 257, scs, 'scs')
                            poc, pocr = bankA()
                            first = True
                            for g in range(2):
                                for tt in range(8):
                                    S.add('pe', lambda e, g=g, tt=tt, first=first, poc=poc: e.matmul(poc[0:8, 0:64], lhsT=pC8[:, g, tt, :], rhs=vcs[:, tt, g * 64:(g + 1) * 64], start=first, stop=(g == 1 and tt == 7)), reads=['pC8', 'vcs'], writes=[pocr])
                                    first = False
                            evac_s(poc, pocr, 8, 64, o3[:, 0, 0:64], ('o3', 0))
                        S.barrier(bar[:])

                        with contextlib.ExitStack() as sst:
                            ptr2 = sb("ptr2", [2, 128], I32, sst)
                            hp1 = sb("hp1", [2, 256], F32, sst)
                            sq = sb("sq", [2, 256], F32, sst)
                            t8s = sb("t8s", [2, 32], F32, sst)
                            ids = sb("ids", [2, 16], F32, sst)
                            offs = sb("offs", [16, 2], I32, sst)
                            offf = sb("offf", [16, 2], F32, sst)
                            ksel = sb("ksel", [16, 8192], F32, sst)
                            vsel = sb("vsel", [16, 8192], F32, sst)
                            vselb = sb("vselb", [16, 64, 65], BF16, sst)
                            qrs = sb("qrs", [16, 4, 64], F32, sst)
                            prs = sb("prs", [16, 8, 4, 64], F32, sst)
                            sSel = sb("sSel", [16, 256], F32, sst)
                            pSel8 = sb("pSel8", [16, 64, 8], BF16, sst)
                            S.dma('sp', ptr2[:], ptr_d[si:si + 1, :].partition_broadcast(2), writes=['ptr2'])
                            hv = hp1[:].rearrange("p (k b) -> p k b", b=2)
                            S.add('dve', lambda e: e.tensor_copy(out=hv[:, :, 0], in_=ptr2[:]), reads=['ptr2'], writes=['hp1'])
                            S.add('dve', lambda e: e.tensor_scalar(out=hv[:, :, 0], in0=hv[:, :, 0], scalar1=2.0, scalar2=1.0, op0=ALU.mult, op1=ALU.add), reads=['hp1'], writes=['hp1'])
                            S.add('dve', lambda e: e.tensor_scalar_add(out=hv[:, :, 1], in0=hv[:, :, 0], scalar1=1.0), reads=['hp1'], writes=['hp1'])
                            sp_ = scs[:, 1:255]
                            S.add('dve', lambda e: e.max(out=t8s[:, 0:8], in_=sp_), reads=['scs'], writes=['t8s'])
                            S.add('dve', lambda e: e.match_replace(out=sq[:, 0:254], in_to_replace=t8s[:, 0:8], in_values=sp_, imm_value=-3.0e38), reads=['scs', 't8s'], writes=['sq'])
                            S.add('dve', lambda e: e.max(out=t8s[:, 8:16], in_=sq[:, 0:254]), reads=['sq'], writes=['t8s'])
                            S.add('dve', lambda e: e.scalar_tensor_tensor(out=sq[:, 0:254], in0=sp_, scalar=t8s[:, 12:13], in1=hp1[:, 1:255], op0=ALU.is_ge, op1=ALU.mult), reads=['scs', 't8s', 'hp1', 'sq'], writes=['sq'])
                            S.add('dve', lambda e: e.memset(ids[:], 0.0), writes=['ids'])
                            S.add('dve', lambda e: e.max(out=ids[:, 0:8], in_=sq[:, 0:254]), reads=['sq', 'ids'], writes=['ids'])
                            S.add('dve', lambda e: e.match_replace(out=sq[:, 0:254], in_to_replace=ids[:, 0:8], in_values=sq[:, 0:254], imm_value=0.0), reads=['sq', 'ids'], writes=['sq'])
                            S.add('dve', lambda e: e.max(out=t8s[:, 16:24], in_=sq[:, 0:254]), reads=['sq'], writes=['t8s'])
                            S.add('dve', lambda e: e.tensor_copy(out=ids[:, 8:13], in_=t8s[:, 16:21]), reads=['t8s', 'ids'], writes=['ids'])
                            S.add('dve', lambda e: e.tensor_copy(out=ids[:, 13:14], in_=hp1[:, 0:1]), reads=['hp1', 'ids'], writes=['ids'])
                            S.add('dve', lambda e: e.tensor_copy(out=ids[:, 14:15], in_=hp1[:, 255:256]), reads=['hp1', 'ids'], writes=['ids'])
                            pt_, ptr_ = bankS()
                            S.add('pe', lambda e, pt_=pt_: e.transpose(out=pt_[0:16, 0:2], in_=ids[:], identity=identf[0:2, 0:2]), reads=['ids', 'identf'], writes=[ptr_])
                            evac_s(pt_, ptr_, 16, 2, offf[:], 'offf')
                            S.add('dve', lambda e: e.tensor_scalar_add(out=offf[:], in0=offf[:], scalar1=-1.0), reads=['offf'], writes=['offf'])
                            S.add('dve', lambda e: e.tensor_copy(out=offs[:], in_=offf[:]), reads=['offf'], writes=['offs'])
                            S.add('pool', lambda e: e.memset(pSel8[:], 0.0), writes=['pSel8'])
                            S.add('pool', lambda e: e.memset(vselb[:], 1.0), writes=['vselb'])
                            pos_, posr_ = bankA()
                            first = True
                            for g in range(2):
                                S.add('pool', lambda e, g=g: e.indirect_dma_start(out=ksel[0:15, :], out_offset=None, in_=csk_d[:, :], in_offset=IOA(ap=offs[0:15, g:g + 1], axis=0)),
                                      reads=['offs'], writes=['ksel'], dma=True)
                                S.add('pool', lambda e, g=g: e.indirect_dma_start(out=vsel[0:15, :], out_offset=None, in_=csv_d[:, :], in_offset=IOA(ap=offs[0:15, g:g + 1], axis=0)),
                                      reads=['offs'], writes=['vsel'], dma=True)
                                S.dma('sp', qrs[0:15], zs_scr[si, 512:1024].rearrange("(r g d) -> g r d", r=4, g=2)[g].partition_broadcast(15), reads=['zs'], writes=['qrs'])
                                kv3 = ksel[0:15, :].rearrange("p (t c) -> p t c", c=128)
                                for c8 in range(8):
                                    S.add('dve', lambda e, g=g, c8=c8, kv3=kv3: e.tensor_tensor(out=prs[0:15], in0=kv3[:, c8 * 8:(c8 + 1) * 8, g * 64:(g + 1) * 64].unsqueeze(2).to_broadcast([15, 8, 4, 64]),
                                                                                         in1=qrs[0:15].unsqueeze(1).to_broadcast([15, 8, 4, 64]), op=ALU.mult), reads=['ksel', 'qrs'], writes=['prs'])
                                    S.add('dve', lambda e, c8=c8: e.reduce_sum(out=sSel[0:15, c8 * 32:(c8 + 1) * 32], in_=prs[0:15].rearrange("p t r d -> p (t r) d"), axis=AX.X), reads=['prs'], writes=['sSel'])
                                S.add('act', lambda e: e.activation(out=sSel[0:15, :], in_=sSel[0:15, :], func=AF.Exp, scale=SCALE), reads=['sSel'], writes=['sSel'])
                                S.add('pool', lambda e, g=g: e.tensor_copy(out=pSel8[0:15, :, g * 4:(g + 1) * 4], in_=sSel[0:15, :].rearrange("p (t r) -> p t r", r=4)), reads=['sSel', 'pSel8'], writes=['pSel8'])
                                if g == 1:
                                    S.add('pool', lambda e: e.memset(pSel8[0:15, :, 0:4], 0.0), reads=['pSel8'], writes=['pSel8'])
                                S.add('act', lambda e, g=g: e.copy(out=vselb[0:15, :, 0:64], in_=vsel[0:15, :].rearrange("p (t c) -> p t c", c=128)[:, :, g * 64:(g + 1) * 64]), reads=['vsel', 'vselb'], writes=['vselb'])
                                for tok in range(64):
                                    S.add('pe', lambda e, tok=tok, first=first, pos_=pos_: e.matmul(pos_[0:8, 0:65], lhsT=pSel8[0:15, tok, :], rhs=vselb[0:15, tok, :], start=first, stop=False), reads=['pSel8', 'vselb'], writes=[posr_])
                                    first = False
                                S.add('pe', lambda e, g=g, pos_=pos_: e.matmul(pos_[0:8, 0:65], lhsT=pn8[:, 1, g, :], rhs=nvx[:, 1, g, :], start=False, stop=(g == 1)), reads=['pn8', 'nvx'], writes=[posr_])
                            evac_s(pos_, posr_, 8, 65, o3[:, 1, :], ('o3', 1))
                        S.add('dve', lambda e: e.memset(o3[:, 0, 64:65], 1.0), reads=[('o3', 0)], writes=[('o3', 0)])
                        rl3 = small[0:8, 300:303]
                        S.add('dve', lambda e: e.tensor_scalar_max(out=rl3, in0=o3[:, :, 64], scalar1=1e-30), reads=[('o3', 0), ('o3', 1), ('o3', 2)], writes=['rl3'])
                        S.add('dve', lambda e: e.reciprocal(out=rl3, in_=rl3), reads=['rl3'], writes=['rl3'])
                        S.add('dve', lambda e: e.tensor_tensor(out=rl3, in0=rl3, in1=gate8[:], op=ALU.mult), reads=['rl3', 'gate8'], writes=['rl3'])
                        S.add('dve', lambda e: e.tensor_scalar_mul(out=onsa8[:, 0:64], in0=o3[:, 0, 0:64], scalar1=rl3[:, 0:1]), reads=['rl3', ('o3', 0)], writes=['onsa8'])
                        for br in (1, 2):
                            S.add('dve', lambda e, br=br: e.scalar_tensor_tensor(out=onsa8[:, 0:64], in0=o3[:, br, 0:64], scalar=rl3[:, br:br + 1], in1=onsa8[:, 0:64], op0=ALU.mult, op1=ALU.add), reads=['rl3', ('o3', br), 'onsa8'], writes=['onsa8'])
                        S.add('dve', lambda e: e.tensor_copy(out=onsa8[:, 64:128], in_=onsa8[:, 0:64]), reads=['onsa8'], writes=['onsa8'])
                        S.add('dve', lambda e: e.tensor_copy(out=onsab[:], in_=onsa8[:]), reads=['onsa8'], writes=['onsab'])
                        pb, pbr = bankB()
                        S.add('pe', lambda e, pb=pb: e.transpose(out=pb[:, 0:8], in_=onsab[:], identity=ident[0:8, 0:8]), reads=['onsab', 'ident'], writes=[pbr])
                        S.add('act', lambda e, pb=pb, si=si: e.copy(out=T2[:, :, si], in_=pb[:, 0:8]), reads=[pbr, 'T2'], writes=['T2'])
                    S.barrier(bar[:])
                for n in range(2):
                    pf, pfr = bankS()
                    seq_ = [('e', hd) for hd in (0, 2, 4, 6)] + [('d', h_) for h_ in range(4)] + [('o', hd) for hd in (1, 3, 5, 7)]
                    for ii, (kd, v_) in enumerate(seq_):
                        if kd == 'd':
                            S.add('pe', lambda e, v_=v_, n=n, pf=pf, ii=ii: e.matmul(pf[0:4, 0:512], lhsT=oTd[:, v_, :], rhs=wo2[:, 4 + v_, n * 512:(n + 1) * 512], start=(ii == 0), stop=(ii == 11)), reads=['oTd', 'wo2'], writes=[pfr])
                        else:
                            b0 = (v_ % 2) * 64
                            S.add('pe', lambda e, v_=v_, n=n, pf=pf, ii=ii, b0=b0: e.matmul(pf[0:4, 0:512], lhsT=T2[b0:b0 + 64, v_, :], rhs=wo2[b0:b0 + 64, v_ // 2, n * 512:(n + 1) * 512], start=(ii == 0), stop=(ii == 11)), reads=['T2', 'wo2'], writes=[pfr])
                    S.add('dve', lambda e, n=n, pf=pf: e.tensor_tensor(out=xsa[:, n * 512:(n + 1) * 512], in0=pf[0:4, 0:512], in1=xsa[:, n * 512:(n + 1) * 512], op=ALU.add), reads=[pfr, 'xsa'], writes=['xsa'])
            S.barrier(bar[:])

        if 'ffn' in phases:
            with contextlib.ExitStack() as pst:
                hw = sb("hw", [128, 22 * 1024 + 512], BF16, pst)
                h2T = hw[:, 0:8 * 2560].rearrange("p (k t) -> p k t", k=8)
                wdn = hw[:, 0:22 * 1024].rearrange("p (i n) -> p i n", i=22)
                gT = sb("gT", [128, 22, 2048], BF16, pst)
                wu = sb("wu", [128, 3, 2, 8, 128], BF16, pst)
                cvp = sb("cvp", [128, 44, 4], F32, pst)
                xr = sb("xr", [128, 2, D], F32, pst)
                hn2 = sb("hn2", [128, D], BF16, pst)
                cab = sb("cab", [128, 2, 2, 260], F32, pst)
                sab = sb("sab", [128, 2, 260], F32, pst)
                ucv = sb("ucv", [128, 44, 2], F32, pst)
                yt = sb("yt", [128, 2, D], F32, pst)
                do_s = 'sample' in phases
                if do_s:
                    hn4f = sb("hn4f", [4, D], BF16, pst)
                    h2sT = sb("h2sT", [128, 8, 4], BF16, pst)
                    stT = sb("stT", [128, 44, 8], F32, pst)
                    usT = sb("usT", [128, 44, 4], F32, pst)
                    csb = sb("csb", [128, 2, 4], F32, pst)
                    gsT = sb("gsT", [128, 22, 4], BF16, pst)
                    ys4 = sb("ys4", [4, D], F32, pst)
                    S.dma('sp', stT[:], stT_d.rearrange("(t p) c -> p t c", p=128), writes=['stT'])
                    S.dma('sp', o_sprev[:, :], stp_d[:, :])
                rS.n = 6
                S.dma('sp', gb[:], ffn_norm_d.partition_broadcast(128), writes=['gb'])
                S.dma('sp', cvp[:], convpT_d.rearrange("(t p) c -> p t c", p=128), writes=['cvp'])
                nqb = int(os.environ.get('N_QB', NQB))
                for j in range(nqb):
                    r = j % 2
                    S.dma('sp', xr[:, r, :], xp_scr[j * 128:(j + 1) * 128, :], reads=[('xps', j)], writes=[('xr', r)])
                    norm_T(xr[:, r, :], ('xr', r), h2T[:, :, j * 128:(j + 1) * 128], ('h2T', j), hn2[:], 'hn2')
                if do_s:
                    k_ = statr.next()
                    ss4 = stat[0:4, 2 * k_:2 * k_ + 1]
                    rs4 = stat[0:4, 2 * k_ + 1:2 * k_ + 2]
                    sres4 = ('stat', k_)
                    S.add('act', lambda e, ss4=ss4: e.activation(out=junk[0:4, :], in_=xsa[:], func=AF.Square, scale=1.0 / 32.0, accum_out=ss4), reads=['xsa'], writes=[sres4])
                    S.add('act', lambda e, ss4=ss4, rs4=rs4: e.activation(out=rs4, in_=ss4, func=AF.Ln, bias=EPS, scale=1.0), reads=[sres4], writes=[sres4])
                    S.add('act', lambda e, rs4=rs4: e.activation(out=rs4, in_=rs4, func=AF.Exp, scale=-0.5), reads=[sres4], writes=[sres4])
                    S.add('dve', lambda e, rs4=rs4: e.scalar_tensor_tensor(out=hn4f[:], in0=xsa[:], scalar=rs4, in1=gb[0:4, :], op0=ALU.mult, op1=ALU.mult), reads=['xsa', sres4, 'gb'], writes=['hn4f'])
                    pb, pbr = bankB()
                    for c in range(8):
                        S.add('pe', lambda e, c=c, pb=pb: e.transpose(out=pb[:, c * 4:(c + 1) * 4], in_=hn4f[:, c * 128:(c + 1) * 128], identity=ident[0:4, 0:4]), reads=['hn4f', 'ident'], writes=[pbr])
                    S.add('act', lambda e, pb=pb: e.copy(out=h2sT[:], in_=pb[:, 0:32].rearrange("p (c t) -> p c t", c=8)), reads=[pbr], writes=['h2sT'])
                rW = Ring('wu', 3)
                rC = Ring('cab', 2)
                nseg = nqb // 5
                n_ch = int(os.environ.get('N_CH', 22))
                for i in range(n_ch):
                    kw = rW.next()
                    for ab in range(2):
                        col0 = ab * DFF + i * 128
                        S.dma('pool', wu[:, kw, ab, :, :], wup_d[:, col0:col0 + 128].rearrange("(k p) n -> p k n", p=128), writes=[('wu', kw)])
                    if do_s:
                        pfs, pfsr = bankS()
                        for ab in range(2):
                            for k in range(8):
                                S.add('pe', lambda e, k=k, ab=ab, pfs=pfs, kw=kw: e.matmul(pfs[:, ab * 4:(ab + 1) * 4], lhsT=wu[:, kw, ab, k, :], rhs=h2sT[:, k, :], start=(ab == 0 and k == 0), stop=(k == 7), skip_group_check=True),
                                      reads=[('wu', kw), 'h2sT'], writes=[pfsr])
                        for ab in range(2):
                            ct = i + ab * 22
                            S.add('act', lambda e, pfs=pfs, ab=ab, ct=ct: e.copy(out=usT[:, ct, :], in_=pfs[:, ab * 4:(ab + 1) * 4]), reads=[pfsr], writes=[('usT', ct)])
                            stv = stT[:, ct, :].rearrange("p (s j) -> p s j", j=2)
                            S.add('act', lambda e, pfs=pfs, ab=ab, ct=ct: e.activation(out=csb[:, ab, :], in_=pfs[:, ab * 4:(ab + 1) * 4], func=AF.Identity, bias=cvp[:, ct, 3:4], scale=cvp[:, ct, 2:3]), reads=[pfsr, 'cvp'], writes=[('csb', ab)])
                            S.add('dve', lambda e, ab=ab, ct=ct, stv=stv: e.scalar_tensor_tensor(out=csb[:, ab, :], in0=stv[:, :, 1], scalar=cvp[:, ct, 1:2], in1=csb[:, ab, :], op0=ALU.mult, op1=ALU.add), reads=['stT', 'cvp', ('csb', ab)], writes=[('csb', ab)])
                            S.add('dve', lambda e, ab=ab, ct=ct, stv=stv: e.scalar_tensor_tensor(out=csb[:, ab, :], in0=stv[:, :, 0], scalar=cvp[:, ct, 0:1], in1=csb[:, ab, :], op0=ALU.mult, op1=ALU.add), reads=['stT', 'cvp', ('csb', ab)], writes=[('csb', ab)])
                        S.add('act', lambda e: e.activation(out=csb[:, 0, :], in_=csb[:, 0, :], func=AF.Silu), reads=[('csb', 0)], writes=[('csb', 0)])
                        S.add('dve', lambda e, i=i: e.tensor_tensor(out=gsT[:, i, :], in0=csb[:, 0, :], in1=csb[:, 1, :], op=ALU.mult), reads=[('csb', 0), ('csb', 1)], writes=[('gsT', i)])
                    for seg in range(nseg):
                        for (c0, N) in ((126, 257), (381, 259)):
                            cols = seg * 640 + c0
                            kc = rC.next()
                            pu2 = []
                            for ab in range(2):
                                pf, pfr = bankS()
                                pu2.append((pf, pfr))
                                for k in range(8):
                                    S.add('pe', lambda e, k=k, ab=ab, pf=pf, kw=kw, cols=cols, N=N: e.matmul(pf[:, 0:N], lhsT=wu[:, kw, ab, k, :], rhs=h2T[:, k, cols:cols + N], start=(k == 0), stop=(k == 7)),
                                          reads=[('wu', kw)] + [('h2T', jj) for jj in range(seg * 5, seg * 5 + 5)], writes=[pfr])
                                ct = i + ab * 22
                                cc = cab[:, kc, ab, 0:N - 2]
                                cres = ('cab', kc, ab)
                                S.add('act', lambda e, pf=pf, cc=cc, ct=ct, N=N: e.activation(out=cc, in_=pf[:, 2:N], func=AF.Identity, bias=cvp[:, ct, 3:4], scale=cvp[:, ct, 2:3]), reads=[pfr, 'cvp'], writes=[cres])
                                S.add('dve', lambda e, pf=pf, cc=cc, ct=ct, N=N: e.scalar_tensor_tensor(out=cc, in0=pf[:, 1:N - 1], scalar=cvp[:, ct, 1:2], in1=cc, op0=ALU.mult, op1=ALU.add), reads=[pfr, 'cvp', cres], writes=[cres])
                                S.add('dve', lambda e, pf=pf, cc=cc, ct=ct, N=N: e.scalar_tensor_tensor(out=cc, in0=pf[:, 0:N - 2], scalar=cvp[:, ct, 0:1], in1=cc, op0=ALU.mult, op1=ALU.add), reads=[pfr, 'cvp', cres], writes=[cres])
                                if seg == nseg - 1 and c0 == 381:
                                    S.add('act', lambda e, pf=pf, ct=ct, N=N: e.copy(out=ucv[:, ct, :], in_=pf[:, N - 2:N]), reads=[pfr], writes=[('ucv', ct)])
                            sa = sab[:, kc, 0:N - 2]
                            S.add('act', lambda e, sa=sa, kc=kc, N=N: e.activation(out=sa, in_=cab[:, kc, 0, 0:N - 2], func=AF.Silu), reads=[('cab', kc, 0)], writes=[('sab', kc)])
                            o0 = seg * 512 + (c0 + 2 - 128)
                            S.add('pool', lambda e, sa=sa, kc=kc, N=N, i=i, o0=o0: e.tensor_tensor(out=gT[:, i, o0:o0 + N - 2], in0=sa, in1=cab[:, kc, 1, 0:N - 2], op=ALU.mult),
                                  reads=[('sab', kc), ('cab', kc, 1)], writes=[('gT', i)])
                for jj in range(2):
                    ocv = o_conv[jj:jj + 1, :].rearrange("o (t p) -> p (o t)", p=128)
                    for q4 in range(4):
                        S.dma('sp', ocv[:, q4 * 11:(q4 + 1) * 11], ucv[:, q4 * 11:(q4 + 1) * 11, jj],
                              reads=[('ucv', ct) for ct in range(44)], allow_slow_non_contiguous=True)
                if do_s:
                    S.dma('sp', o_suT.rearrange("(t p) c -> p t c", p=128), usT[:], reads=[('usT', ct) for ct in range(44)])
                S.barrier(bar[:])
                for i in range(22):
                    S.dma('pool', wdn[:, i, :], wdown_d[i * 128:(i + 1) * 128, :], writes=['wdn'])
                S.dma('sp', gb[:], final_norm_d.partition_broadcast(128), writes=['gb'])
                if do_s:
                    for n in range(2):
                        pf, pfr = bankS()
                        for i in range(22):
                            S.add('pe', lambda e, i=i, n=n, pf=pf: e.matmul(pf[0:4, 0:512], lhsT=gsT[:, i, :], rhs=wdn[:, i, n * 512:(n + 1) * 512], start=(i == 0), stop=(i == 21)),
                                  reads=['wdn'] + [('gsT', ii) for ii in range(22)], writes=[pfr])
                        S.add('dve', lambda e, n=n, pf=pf: e.tensor_tensor(out=xsa[:, n * 512:(n + 1) * 512], in0=pf[0:4, 0:512], in1=xsa[:, n * 512:(n + 1) * 512], op=ALU.add), reads=[pfr, 'xsa'], writes=['xsa'])
                    k_ = statr.next()
                    ss4 = stat[0:4, 2 * k_:2 * k_ + 1]
                    rs4 = stat[0:4, 2 * k_ + 1:2 * k_ + 2]
                    sres4 = ('stat', k_)
                    S.add('act', lambda e, ss4=ss4: e.activation(out=junk[0:4, :], in_=xsa[:], func=AF.Square, scale=1.0 / 32.0, accum_out=ss4), reads=['xsa'], writes=[sres4])
                    S.add('act', lambda e, ss4=ss4, rs4=rs4: e.activation(out=rs4, in_=ss4, func=AF.Ln, bias=EPS, scale=1.0), reads=[sres4], writes=[sres4])
                    S.add('act', lambda e, rs4=rs4: e.activation(out=rs4, in_=rs4, func=AF.Exp, scale=-0.5), reads=[sres4], writes=[sres4])
                    S.add('dve', lambda e, rs4=rs4: e.scalar_tensor_tensor(out=ys4[:], in0=xsa[:], scalar=rs4, in1=gb[0:4, :], op0=ALU.mult, op1=ALU.add if False else ALU.mult), reads=['xsa', sres4, 'gb'], writes=['ys4'])
                    S.dma('sp', o_ys[:, :], ys4[:], reads=['ys4'])
                ob_i = 0
                for j in range(nqb):
                    if j % 5 == 0:
                        continue
                    r = ob_i % 2
                    S.dma('sp', xr[:, r, :], xp_scr[j * 128:(j + 1) * 128, :], writes=[('xr', r)])
                    for n in range(2):
                        pf, pfr = bankS()
                        for i in range(22):
                            S.add('pe', lambda e, i=i, n=n, pf=pf, ob_i=ob_i: e.matmul(pf[:, 0:512], lhsT=gT[:, i, ob_i * 128:(ob_i + 1) * 128], rhs=wdn[:, i, n * 512:(n + 1) * 512], start=(i == 0), stop=(i == 21)),
                                  reads=['wdn'] + [('gT', ii) for ii in range(22)], writes=[pfr])
                        S.add('dve', lambda e, n=n, pf=pf, r=r: e.tensor_tensor(out=xr[:, r, n * 512:(n + 1) * 512], in0=pf[:, 0:512], in1=xr[:, r, n * 512:(n + 1) * 512], op=ALU.add),
                              reads=[pfr, ('xr', r)], writes=[('xr', r)])
                    k = statr.next()
                    ss = stat[:, 2 * k:2 * k + 1]
                    rs = stat[:, 2 * k + 1:2 * k + 2]
                    sres = ('stat', k)
                    S.add('act', lambda e, ss=ss, r=r: e.activation(out=junk[:], in_=xr[:, r, :], func=AF.Square, scale=1.0 / 32.0, accum_out=ss), reads=[('xr', r)], writes=[sres])
                    S.add('act', lambda e, ss=ss, rs=rs: e.activation(out=rs, in_=ss, func=AF.Ln, bias=EPS, scale=1.0), reads=[sres], writes=[sres])
                    S.add('act', lambda e, rs=rs: e.activation(out=rs, in_=rs, func=AF.Exp, scale=-0.5), reads=[sres], writes=[sres])
                    S.add('dve', lambda e, rs=rs, r=r: e.scalar_tensor_tensor(out=yt[:, r, :], in0=xr[:, r, :], scalar=rs, in1=gb[:], op0=ALU.mult, op1=ALU.mult), reads=[('xr', r), sres, 'gb'], writes=[('yt', r)])
                    S.dma('sp', o_y[ob_i * 128:(ob_i + 1) * 128, :], yt[:, r, :], reads=[('yt', r)])
                    ob_i += 1

        S.emit()
    return nc, S


_CACHE = {}


def _rope_tab(pos):
    half = 32
    inv = (1.0 / (10000.0 ** (np.arange(half, dtype=np.float32) / half))).astype(np.float32)
    ang = pos.astype(np.float32)[:, None] * inv[None, :]
    return np.concatenate([np.cos(ang), np.sin(ang)], axis=1).astype(np.float32)


def _host_consts(h):
    off = 512 * (1 - h)
    pos = np.arange(T) - off
    c = {}
    c["ropekv"] = _rope_tab(np.maximum(pos, 0))
    valid = (pos >= 0).astype(np.float32)
    c["validc"] = np.ascontiguousarray(valid.reshape(NT, 128).T)
    i = np.arange(256)
    cval = (16 * i - off >= 0) & (i < NCMP)
    cm = np.zeros((NQB, 128, 2, 128), np.float32)
    sm = np.zeros((NQB, 128, 128), np.float32)
    blk = np.arange(64)
    bvalid = blk >= (off // 64)
    for j, fb in enumerate(QB):
        q = 128 * fb + np.arange(128)
        for tt in range(2):
            ii = tt * 128 + np.arange(128)
            m = cval[ii][:, None] & ((16 * ii + 31)[:, None] <= q[None, :])
            cm[j, :, tt, :] = m
        ok = (64 * blk[None, :] <= q[:, None]) & bvalid[None, :]
        cur = q // 64
        forced = (blk[None, :] == off // 64) | (blk[None, :] == cur[:, None]) | (blk[None, :] == cur[:, None] - 1)
        okf = ok.astype(np.float32)
        sm[j, :, 0:64] = okf
        sm[j, :, 64:128] = okf * forced * 1.0e4 + (okf - 1.0) * 1.0e30
    c["cmaskd"] = cm
    c["cvalidd"] = np.ascontiguousarray(cval.astype(np.float32).reshape(2, 128).T)
    c["selmd"] = sm
    ii = np.arange(256)
    cs = ii * 16
    ss = blk * 64
    ov = ((cs[:, None] < ss[None, :] + 64) & (cs[:, None] + 32 > ss[None, :])).astype(np.float32)
    ov[NCMP:] = 0
    c["ovd"] = np.ascontiguousarray(ov.reshape(2, 128, 64).transpose(1, 0, 2))
    p = np.arange(128)
    tri = (p[:, None] <= p[None, :]).astype(np.float32)
    triu = (p[:, None] > p[None, :]).astype(np.float32)
    c["trid"] = np.concatenate([tri, triu], axis=1)
    cc = np.arange(T)
    e2 = ((np.arange(128) % 64)[:, None] == (cc // 64)[None, :]).astype(np.float32)
    c["e2d"] = e2
    c["identd"] = np.eye(128, dtype=np.float32)
    return c


def _sample_consts():
    c = {}
    c["ropes"] = np.repeat(_rope_tab(np.array([PAST])), 4, axis=0)
    bm = np.zeros((8, 4, 128), np.float32)
    c0 = np.zeros((8, 4), np.float32)
    c1 = np.zeros((8, 4), np.float32)
    for h in range(4):
        for m in range(2):
            bm[h * 2 + m, h, :] = 1.0
        c0[2 * h, h] = 1.0
        c1[2 * h + 1, h] = 1.0
    c["bm8"] = bm.reshape(8, 512)
    c["c01"] = np.concatenate([c0, c1], axis=1)
    cm = np.ones((128, 8), np.float32)
    cm[127, 7] = 0.0
    c["cmsk"] = cm
    n = np.arange(1024)
    j = np.arange(257)
    ov = ((16 * n[:, None] < 64 * j[None, :] + 64) & (16 * n[:, None] + 32 > 64 * j[None, :])).astype(np.float32)
    ov[1023:] = 0
    c["ovs"] = np.ascontiguousarray(ov.reshape(8, 128, 257).transpose(1, 0, 2))
    wm = np.ones((128, 32), np.float32)
    wm[0, 0:8] = 0.0
    c["winm"] = wm
    c["identf"] = np.eye(128, dtype=np.float32)
    c["iota16"] = np.tile(np.arange(16, dtype=np.float32)[None, :], (128, 1))
    return c


def kernel(**inp):
    f = lambda a: np.ascontiguousarray(np.asarray(a, dtype=np.float32))
    x = f(inp["x_prompt"])
    w_in = f(inp["w_in"])[0]
    permq = np.arange(512).reshape(2, 4, 64).transpose(1, 0, 2).reshape(-1)
    wq = np.ascontiguousarray(np.concatenate([w_in[:, permq], w_in[:, 1304:1816], w_in[:, 1280:1304]], axis=1))
    wkv = np.ascontiguousarray(np.concatenate([w_in[:, 512:1280], w_in[:, 1816:2840]], axis=1))
    lam4 = np.concatenate([f(inp["lambda_q1"])[0], f(inp["lambda_k1"])[0], f(inp["lambda_q2"])[0], f(inp["lambda_k2"])[0]])[None, :]
    convp = np.ascontiguousarray(np.concatenate([f(inp["conv_w"])[0], f(inp["conv_b"])], axis=0))
    shared = {
        "wq": wq, "wkv": wkv,
        "attn_norm": f(inp["attn_norm"]), "ffn_norm": f(inp["ffn_norm"]), "final_norm": f(inp["final_norm"])[None, :],
        "cmp_w1_k": f(inp["cmp_w1_k"])[0], "cmp_w1_v": f(inp["cmp_w1_v"])[0],
        "cmp_w2_k": f(inp["cmp_w2_k"])[0], "cmp_w2_v": f(inp["cmp_w2_v"])[0],
        "cmp_pos_kT": np.ascontiguousarray(f(inp["cmp_pos_k"])[0].T), "cmp_pos_vT": np.ascontiguousarray(f(inp["cmp_pos_v"])[0].T),
        "lam4": np.ascontiguousarray(lam4), "subln_g": f(inp["subln_g"]),
        "w_out": f(inp["w_out"])[0], "w_up": f(inp["w_up"])[0], "w_down": f(inp["w_down"])[0], "convpT": np.ascontiguousarray(convp.T),
    }
    shared.update(_sample_consts())
    shared["cache_cmp_k"] = f(inp["cache_cmp_k"]).reshape(5120, 16384)
    shared["cache_cmp_v"] = f(inp["cache_cmp_v"]).reshape(5120, 16384)
    shared["cache_sel_k"] = f(inp["cache_sel_k"]).reshape(10240, 8192)
    shared["cache_sel_v"] = f(inp["cache_sel_v"]).reshape(10240, 8192)
    shared["cache_diff_k"] = f(inp["cache_diff_k"]).reshape(5120, 65536)
    shared["cache_diff_v"] = f(inp["cache_diff_v"]).reshape(5120, 65536)
    xs = f(inp["x_sample"])[:, 0, :]
    pt = np.ascontiguousarray(np.asarray(inp["page_table"], dtype=np.int32))
    wink = f(inp["cache_win_k"])[0].reshape(32, 512, 128)
    winv = f(inp["cache_win_v"])[0].reshape(32, 512, 128)
    st = f(inp["state_ffn_conv"])[0]
    hc = [_host_consts(0), _host_consts(1)]
    in_maps = []
    for c in range(8):
        b, h = c % 4, c // 4
        m = dict(shared)
        m.update(hc[h])
        if h == 1:
            m["xkv"] = x[b]
        else:
            m["xkv"] = np.ascontiguousarray(np.concatenate([np.zeros((512, D), np.float32), x[b, :3584]], axis=0))
        sl = slice(4 * c, 4 * c + 4)
        m["xs"] = np.ascontiguousarray(xs[sl])
        m["ptc"] = np.ascontiguousarray(pt[sl].T)
        m["ptr"] = np.ascontiguousarray(pt[sl])
        m["win_k"] = np.ascontiguousarray(wink[sl])
        m["win_v"] = np.ascontiguousarray(winv[sl])
        m["stT"] = np.ascontiguousarray(st[sl].transpose(2, 0, 1).reshape(2 * DFF, 8))
        m["stp"] = np.ascontiguousarray(st[sl, 1, :])
        in_maps.append(m)
    if "nc" not in _CACHE:
        _CACHE["nc"] = build_nc()
    nc, _ = _CACHE["nc"]
    res = run_bass_kernel_spmd(nc, in_maps, core_ids=list(range(8)))
    R = res.results
    y_prompt = np.zeros((4, T, D), np.float32)
    for c in range(8):
        b, h = c % 4, c // 4
        oy = np.asarray(R[c]["o_y"])
        for s_ in range(4):
            a0 = 512 * (2 * s_ + h)
            y_prompt[b, a0:a0 + 512] = oy[512 * s_:512 * (s_ + 1)]
    y_sample = np.concatenate([np.asarray(R[c]["o_ys"]) for c in range(8)], axis=0).reshape(32, 1, D)
    okv = np.stack([np.asarray(R[4 + b]["o_kv"]) for b in range(4)], axis=0)
    p_cmp_k = okv[:, :, 0:128].reshape(1, 4, T, 2, 64)
    p_cmp_v = okv[:, :, 128:256].reshape(1, 4, T, 2, 64)
    p_sel_k = okv[:, :, 256:384].reshape(1, 4, T, 2, 64)
    p_sel_v = okv[:, :, 384:512].reshape(1, 4, T, 2, 64)
    p_win_k = okv[:, T - 512:, 512:640].reshape(1, 4, 512, 2, 64)
    p_win_v = okv[:, T - 512:, 640:768].reshape(1, 4, 512, 2, 64)
    p_diff_k = okv[:, :, 768:1280].reshape(1, 4, T, 4, 2, 64)
    p_diff_v = okv[:, :, 1280:1792].reshape(1, 4, T, 4, 128)
    p_conv = np.stack([np.asarray(R[4 + b]["o_conv"]) for b in range(4)], axis=0).reshape(1, 4, 2, 2 * DFF)
    skv = np.concatenate([np.asarray(R[c]["o_skv"]) for c in range(8)], axis=0)
    s_cmp_k = skv[:, 0:128].reshape(1, 32, 1, 2, 64)
    s_cmp_v = skv[:, 128:256].reshape(1, 32, 1, 2, 64)
    s_sel_k = skv[:, 256:384].reshape(1, 32, 1, 2, 64)
    s_sel_v = skv[:, 384:512].reshape(1, 32, 1, 2, 64)
    s_diff_k = skv[:, 768:1280].reshape(1, 32, 1, 4, 2, 64)
    s_diff_v = skv[:, 1280:1792].reshape(1, 32, 1, 4, 128)
    s_win_k = np.concatenate([np.asarray(R[c]["o_swk"]) for c in range(8)], axis=0).reshape(1, 32, 512, 2, 64)
    s_win_v = np.concatenate([np.asarray(R[c]["o_swv"]) for c in range(8)], axis=0).reshape(1, 32, 512, 2, 64)
    sprev = np.concatenate([np.asarray(R[c]["o_sprev"]) for c in range(8)], axis=0)
    su = np.concatenate([np.asarray(R[c]["o_suT"]).T for c in range(8)], axis=0)
    s_conv = np.stack([sprev, su], axis=1).reshape(1, 32, 2, 2 * DFF)
    outs = (y_prompt, y_sample, p_cmp_k, p_cmp_v, p_sel_k, p_sel_v, p_diff_k, p_diff_v, p_win_k, p_win_v, p_conv,
            s_cmp_k, s_cmp_v, s_sel_k, s_sel_v, s_diff_k, s_diff_v, s_win_k, s_win_v, s_conv)
    return tuple(np.ascontiguousarray(o, dtype=np.float32) for o in outs)
```
